# Optimizing a Trainium2 kernel written in Bass

```python
import math
import jax, jax.numpy as jnp
from jax import lax
import numpy as np

D_MODEL = 1024
BATCH = 4
SEQ = 8192
DEPTH = 4

CTX_LEN = 256
GRID_W = 64
N_MIXERS = 4
MIX_GDN, MIX_RET, MIX_GLA, MIX_HYENA = 0, 1, 2, 3
CHUNK = 64
EPS = 1e-6
D_FF = -(-8 * D_MODEL // (3 * 256)) * 256

H_A = D_MODEL // 128
DK_A = 128
DV_A = 128
QK_A = H_A * DK_A
V_A = H_A * DV_A
GDN_QKV = 2 * QK_A + V_A
GDN_CONV = 5
GDN_IN = GDN_QKV + V_A + 4 * H_A

H_R = D_MODEL // 256
DK_R = 256
DV_R = 512
QK_R = H_R * DK_R
V_R = H_R * DV_R
RET_IN = 2 * QK_R + 2 * V_R
ROPE_BASE = 10000.0

H_C = 4
DK_C = D_MODEL // 2 // H_C
DV_C = D_MODEL // H_C
QK_C = H_C * DK_C
V_C = H_C * DV_C
GLA_RANK = 16
GLA_GATE_NORM = 16.0
GLA_IN = 2 * QK_C + 2 * V_C + 2 * GLA_RANK

HY_ORDER = 2
HY_CONV = 3
HY_EMB = 33
HY_FF = 64
HY_TARGET = 1e-2
HY_FAST = 0.3
HY_SLOW = 1.5

kernel_name = 'hybrid_interleaved_diffusion_trunk'


def rms_norm(x, g):
    xf = x.astype(jnp.float32)
    y = xf * lax.rsqrt(jnp.mean(xf * xf, axis=-1, keepdims=True) + EPS)
    return (y * g.astype(jnp.float32)).astype(x.dtype)


def modulate(x, g, shift, scale):
    return rms_norm(x, g) * (1.0 + scale) + shift


def swiglu(h, w1, w3, w2):
    return (jax.nn.silu(h @ w1) * (h @ w3)) @ w2


def short_conv(u, w):
    k = w.shape[0]
    return lax.conv_general_dilated(u, w[:, None, :].astype(u.dtype), window_strides=(1,),
                                    padding=[(k // 2, k // 2)],
                                    dimension_numbers=('NWC', 'WIO', 'NWC'),
                                    feature_group_count=u.shape[-1])


def to_heads(t, n_heads):
    b, l, _ = t.shape
    return t.reshape(b, l, n_heads, -1).transpose(0, 2, 1, 3)


def from_heads(t):
    b, h, l, d = t.shape
    return t.transpose(0, 2, 1, 3).reshape(b, l, h * d)


def l2norm(t):
    tf = t.astype(jnp.float32)
    return tf * lax.rsqrt(jnp.sum(tf * tf, axis=-1, keepdims=True) + EPS)


def head_norm_gate(o, g, gate, center):
    if center:
        o = o - jnp.mean(o, axis=-1, keepdims=True)
    o = o * lax.rsqrt(jnp.mean(o * o, axis=-1, keepdims=True) + EPS)
    g = g.astype(jnp.float32)
    o = o * (g[:, None, :] if g.ndim == 2 else g)
    return from_heads(o).astype(gate.dtype) * jax.nn.silu(gate)


def axial_rotary(n, dim):
    rows = n // GRID_W
    row = jnp.repeat(jnp.arange(rows, dtype=jnp.float32), GRID_W)
    col = jnp.tile(jnp.arange(GRID_W, dtype=jnp.float32), rows)
    nf = dim // 4
    inv = ROPE_BASE ** (-jnp.arange(nf, dtype=jnp.float32) / nf)
    ang = jnp.concatenate([row[:, None] * inv, col[:, None] * inv], axis=-1)
    return jnp.cos(ang), jnp.sin(ang)


def apply_rotary(x, cos, sin):
    half = x.shape[-1] // 2
    x1, x2 = x[..., :half], x[..., half:]
    return jnp.concatenate([x1 * cos - x2 * sin, x1 * sin + x2 * cos], axis=-1)


def gated_delta_chunks(q, k, v, beta, g, s0):
    f32 = jnp.float32
    b, h, l, dk = q.shape
    dv = v.shape[-1]
    n = l // CHUNK
    q = q.astype(f32).reshape(b, h, n, CHUNK, dk)
    k = k.astype(f32).reshape(b, h, n, CHUNK, dk)
    v = v.astype(f32).reshape(b, h, n, CHUNK, dv)
    beta = beta.astype(f32).reshape(b, h, n, CHUNK)
    gc = jnp.cumsum(g.astype(f32).reshape(b, h, n, CHUNK), axis=-1)
    incl = jnp.tril(jnp.ones((CHUNK, CHUNK), bool))
    diff = gc[..., :, None] - gc[..., None, :]
    decay = jnp.where(incl, jnp.exp(jnp.where(incl, diff, 0.0)), 0.0)
    kb = k * beta[..., None]
    m = jnp.tril(jnp.einsum('bhnid,bhnjd->bhnij', kb, k) * decay, -1)
    a = jnp.eye(CHUNK, dtype=f32) + m
    rhs = jnp.concatenate([v * beta[..., None], kb * jnp.exp(gc)[..., None]], axis=-1)
    sol = lax.linalg.triangular_solve(a, rhs, left_side=True, lower=True, unit_diagonal=True)
    u, w = sol[..., :dv], sol[..., dv:]
    attn = jnp.einsum('bhnid,bhnjd->bhnij', q, k) * decay
    g_last = gc[..., -1:]
    xs = (u, w, attn, q * jnp.exp(gc)[..., None], k * jnp.exp(g_last - gc)[..., None],
          jnp.exp(g_last)[..., None])
    xs = tuple(jnp.moveaxis(t, 2, 0) for t in xs)

    def step(s, inp):
        u_n, w_n, attn_n, q_n, k_n, d_n = inp
        v_new = u_n - jnp.einsum('bhcd,bhde->bhce', w_n, s)
        o_n = jnp.einsum('bhcd,bhde->bhce', q_n, s) + jnp.einsum('bhij,bhje->bhie', attn_n, v_new)
        s = s * d_n + jnp.einsum('bhcd,bhce->bhde', k_n, v_new)
        return s, o_n

    s, o = lax.scan(step, s0.astype(f32), xs)
    return jnp.moveaxis(o, 0, 2).reshape(b, h, l, dv), s


def gla_chunks(q, k, v, logdecay, s0):
    f32 = jnp.float32
    b, h, l, dk = q.shape
    dv = v.shape[-1]
    n = l // CHUNK
    q = q.astype(f32).reshape(b, h, n, CHUNK, dk)
    k = k.astype(f32).reshape(b, h, n, CHUNK, dk)
    v = v.astype(f32).reshape(b, h, n, CHUNK, dv)
    ld = jnp.broadcast_to(logdecay.astype(f32), (b, h, l, dk)).reshape(b, h, n, CHUNK, dk)
    cum = jnp.cumsum(ld, axis=3)
    ref = cum[:, :, :, CHUNK // 2:CHUNK // 2 + 1]
    scores = jnp.einsum('bhnid,bhnjd->bhnij', q * jnp.exp(cum - ref), k * jnp.exp(ref - cum))
    incl = jnp.tril(jnp.ones((CHUNK, CHUNK), bool))
    o_intra = jnp.einsum('bhnij,bhnje->bhnie', jnp.where(incl, scores, 0.0), v)
    c_last = cum[:, :, :, -1:]
    xs = (q * jnp.exp(cum), k * jnp.exp(c_last - cum), v, jnp.swapaxes(jnp.exp(c_last), -1, -2))
    xs = tuple(jnp.moveaxis(t, 2, 0) for t in xs)

    def step(s, inp):
        q_n, k_n, v_n, d_n = inp
        o_n = jnp.einsum('bhcd,bhde->bhce', q_n, s)
        s = s * d_n + jnp.einsum('bhcd,bhce->bhde', k_n, v_n)
        return s, o_n

    s, o_inter = lax.scan(step, s0.astype(f32), xs)
    o = o_intra + jnp.moveaxis(o_inter, 0, 2)
    return o.reshape(b, h, l, dv), s


def bidirectional(scan_fn, ctx_dirs, lat_dirs, s0):
    o_lat, o_ctx = 0.0, 0.0
    for d in range(2):
        ca, la = ctx_dirs[d], lat_dirs[d]
        if d == 1:
            ca = [jnp.flip(t, axis=2) for t in ca]
            la = [jnp.flip(t, axis=2) for t in la]
        oc, sc = scan_fn(*ca, s0)
        ol, _ = scan_fn(*la, sc)
        if d == 1:
            oc, ol = jnp.flip(oc, axis=2), jnp.flip(ol, axis=2)
        o_lat = o_lat + ol
        o_ctx = o_ctx + oc
    return o_lat, o_ctx


def gdn_mixer(h, hc, w_in, conv_w, a_log, dt_bias, norm_g, w_out, ctx_out):
    def prep(t):
        bsz, l, _ = t.shape
        p = t @ w_in
        qkv = jax.nn.silu(short_conv(p[..., :GDN_QKV], conv_w))
        q = l2norm(to_heads(qkv[..., :QK_A], H_A)) * DK_A ** -0.5
        k = l2norm(to_heads(qkv[..., QK_A:2 * QK_A], H_A))
        v = to_heads(qkv[..., 2 * QK_A:], H_A).astype(jnp.float32)
        gate = p[..., GDN_QKV:GDN_QKV + V_A]
        sc = p[..., GDN_QKV + V_A:].astype(jnp.float32).reshape(bsz, l, 2, 2, H_A)
        sc = jnp.transpose(sc, (2, 3, 0, 4, 1))
        beta = jax.nn.sigmoid(sc[0])
        g = (-jnp.exp(a_log.astype(jnp.float32))[:, None, :, None]
             * jax.nn.softplus(sc[1] + dt_bias.astype(jnp.float32)[:, None, :, None]))
        return [(q, k, v, beta[d], g[d]) for d in range(2)], gate

    lat_dirs, gate = prep(h)
    ctx_dirs, gate_c = prep(hc)
    s0 = jnp.zeros((h.shape[0], H_A, DK_A, DV_A), jnp.float32)
    o_l, o_c = bidirectional(gated_delta_chunks, ctx_dirs, lat_dirs, s0)
    y = head_norm_gate(o_l, norm_g, gate, False) @ w_out
    yc = head_norm_gate(o_c, norm_g, gate_c, False) @ w_out if ctx_out else None
    return y, yc


def retention_mixer(h, hc, w_in, norm_g, w_out, ctx_out):
    log_gamma = jnp.log1p(-jnp.exp2(-5.0 - jnp.arange(H_R, dtype=jnp.float32)))
    decay_dirs = (log_gamma.reshape(1, H_R, 1, 1), log_gamma[::-1].reshape(1, H_R, 1, 1))

    def prep(t, rotary):
        p = t @ w_in
        q = to_heads(p[..., :QK_R], H_R).astype(jnp.float32) * DK_R ** -0.5
        k = to_heads(p[..., QK_R:2 * QK_R], H_R).astype(jnp.float32)
        if rotary:
            cos, sin = axial_rotary(t.shape[1], DK_R)
            q, k = apply_rotary(q, cos, sin), apply_rotary(k, cos, sin)
        v = to_heads(p[..., 2 * QK_R:2 * QK_R + V_R], H_R).astype(jnp.float32)
        return [(q, k, v, ld) for ld in decay_dirs], p[..., 2 * QK_R + V_R:]

    lat_dirs, gate = prep(h, True)
    ctx_dirs, gate_c = prep(hc, False)
    s0 = jnp.zeros((h.shape[0], H_R, DK_R, DV_R), jnp.float32)
    o_l, o_c = bidirectional(gla_chunks, ctx_dirs, lat_dirs, s0)
    y = head_norm_gate(o_l, norm_g, gate, True) @ w_out
    yc = head_norm_gate(o_c, norm_g, gate_c, True) @ w_out if ctx_out else None
    return y, yc


def gla_mixer(h, hc, w_in, gate_w2, gate_b, norm_g, w_out, ctx_out):
    def prep(t):
        bsz, l, _ = t.shape
        p = t @ w_in
        q = to_heads(p[..., :QK_C], H_C).astype(jnp.float32) * DK_C ** -0.5
        k = to_heads(p[..., QK_C:2 * QK_C], H_C).astype(jnp.float32)
        v = to_heads(p[..., 2 * QK_C:2 * QK_C + V_C], H_C).astype(jnp.float32)
        gate = p[..., 2 * QK_C + V_C:2 * QK_C + 2 * V_C]
        low = p[..., 2 * QK_C + 2 * V_C:].reshape(bsz, l, 2, GLA_RANK)
        logit = jnp.einsum('blyr,yrk->yblk', low, gate_w2) + gate_b[:, None, None, :]
        ld = jax.nn.log_sigmoid(logit.astype(jnp.float32)) / GLA_GATE_NORM
        return [(q, k, v, to_heads(ld[d], H_C)) for d in range(2)], gate

    lat_dirs, gate = prep(h)
    ctx_dirs, gate_c = prep(hc)
    s0 = jnp.zeros((h.shape[0], H_C, DK_C, DV_C), jnp.float32)
    o_l, o_c = bidirectional(gla_chunks, ctx_dirs, lat_dirs, s0)
    y = head_norm_gate(o_l, norm_g, gate, False) @ w_out
    yc = head_norm_gate(o_c, norm_g, gate_c, False) @ w_out if ctx_out else None
    return y, yc


def hyena_filters(l, ff_w1, ff_b1, ff_w2, ff_b2, ff_w3, sin_freq):
    f32 = jnp.float32
    pos = jnp.arange(l, dtype=f32)
    t = pos / (l - 1)
    bands = (HY_EMB - 1) // 2
    freqs = jnp.linspace(1e-4, bands - 1, bands, dtype=f32)
    ang = (2.0 * math.pi / l) * pos[:, None] * freqs[None, :]
    z = jnp.concatenate([t[:, None], jnp.cos(ang), -jnp.sin(ang)], axis=-1)
    fr = sin_freq.astype(f32)
    hid = jnp.sin(fr * (z @ ff_w1.astype(f32) + ff_b1.astype(f32)))
    hid = jnp.sin(fr * (hid @ ff_w2.astype(f32) + ff_b2.astype(f32)))
    filt = (hid @ ff_w3.astype(f32)).reshape(l, HY_ORDER, D_MODEL)
    dist = jnp.abs(pos - l // 2) / (l // 2)
    deltas = jnp.abs(jnp.linspace(math.log(HY_TARGET) / HY_SLOW, math.log(HY_TARGET) / HY_FAST,
                                  D_MODEL, dtype=f32))
    filt = filt * jnp.exp(-dist[:, None, None] * deltas)
    return filt / jnp.sum(jnp.abs(filt), axis=0, keepdims=True)


def fft_conv_centred(u, filt, bias):
    l = u.shape[1]
    n = 2 * l
    uf = u.astype(jnp.float32)
    y = jnp.fft.irfft(jnp.fft.rfft(uf, n=n, axis=1) * jnp.fft.rfft(filt, n=n, axis=0)[None], n=n, axis=1)
    y = y[:, l // 2:l // 2 + l]
    return (y + uf * bias.astype(jnp.float32)).astype(u.dtype)


def hyena_mixer(h, hc, w_in, conv_w, ff_w1, ff_b1, ff_w2, ff_b2, ff_w3, sin_freq, skip, w_out):
    def operator(t):
        l = t.shape[1]
        u = short_conv(t @ w_in, conv_w)
        v, x1, x2 = jnp.split(u, 3, axis=-1)
        filt = hyena_filters(l, ff_w1, ff_b1, ff_w2, ff_b2, ff_w3, sin_freq)
        z = x1 * fft_conv_centred(v, filt[:, 0], skip[0])
        z = x2 * fft_conv_centred(z, filt[:, 1], skip[1])
        return z @ w_out

    y = operator(h)
    yc = operator(hc) if hc is not None else None
    return y, yc


def setup_inputs(seed: int = 0) -> dict:
    key = jax.random.key(seed)
    keys = iter(jax.random.split(key, 64))
    f32 = jnp.float32

    def normal(shape, scale):
        return jax.random.normal(next(keys), shape, f32) * scale

    def gain(shape):
        return 1.0 + normal(shape, 0.02)

    n_a, n_b, n_c, n_d = [len(range(m, DEPTH, N_MIXERS)) for m in range(N_MIXERS)]
    d = D_MODEL
    inp = {}
    inp['x'] = normal((BATCH, SEQ, d), 1.0)
    inp['c'] = normal((BATCH, d), 1.0)
    inp['ctx'] = normal((BATCH, CTX_LEN, d), 1.0)
    inp['c_ctx'] = normal((d,), 1.0)
    inp['ada_w'] = normal((DEPTH, d, 6 * d), 0.5 * d ** -0.5)
    inp['ada_b'] = normal((DEPTH, 6 * d), 0.02)
    inp['norm1_g'] = gain((DEPTH, d))
    inp['norm2_g'] = gain((DEPTH, d))
    inp['ffn_w1'] = normal((DEPTH, d, D_FF), d ** -0.5)
    inp['ffn_w3'] = normal((DEPTH, d, D_FF), d ** -0.5)
    inp['ffn_w2'] = normal((DEPTH, D_FF, d), D_FF ** -0.5)
    inp['gdn_w_in'] = normal((n_a, d, GDN_IN), d ** -0.5)
    inp['gdn_conv_w'] = normal((n_a, GDN_CONV, GDN_QKV), GDN_CONV ** -0.5)
    inp['gdn_a_log'] = jnp.log(jax.random.uniform(next(keys), (n_a, 2, H_A), f32, 1.0, 16.0))
    dt = jnp.exp(jax.random.uniform(next(keys), (n_a, 2, H_A), f32, math.log(1e-3), math.log(1e-1)))
    inp['gdn_dt_bias'] = dt + jnp.log(-jnp.expm1(-dt))
    inp['gdn_norm_g'] = gain((n_a, DV_A))
    inp['gdn_w_out'] = normal((n_a, V_A, d), V_A ** -0.5)
    inp['ret_w_in'] = normal((n_b, d, RET_IN), d ** -0.5)
    inp['ret_norm_g'] = gain((n_b, H_R, DV_R))
    inp['ret_w_out'] = normal((n_b, V_R, d), V_R ** -0.5)
    inp['gla_w_in'] = normal((n_c, d, GLA_IN), d ** -0.5)
    inp['gla_gate_w2'] = normal((n_c, 2, GLA_RANK, QK_C), GLA_RANK ** -0.5)
    inp['gla_gate_b'] = normal((n_c, 2, QK_C), 0.1)
    inp['gla_norm_g'] = gain((n_c, DV_C))
    inp['gla_w_out'] = normal((n_c, V_C, d), V_C ** -0.5)
    inp['hy_w_in'] = normal((n_d, d, 3 * d), d ** -0.5)
    inp['hy_conv_w'] = normal((n_d, HY_CONV, 3 * d), HY_CONV ** -0.5)
    inp['hy_ff_w1'] = normal((n_d, HY_EMB, HY_FF), HY_EMB ** -0.5)
    inp['hy_ff_b1'] = normal((n_d, HY_FF), 0.1)
    inp['hy_ff_w2'] = normal((n_d, HY_FF, HY_FF), HY_FF ** -0.5)
    inp['hy_ff_b2'] = normal((n_d, HY_FF), 0.1)
    inp['hy_ff_w3'] = normal((n_d, HY_FF, HY_ORDER * d), HY_FF ** -0.5)
    inp['hy_sin_freq'] = gain((n_d, HY_FF))
    inp['hy_skip'] = normal((n_d, HY_ORDER, d), 1.0)
    inp['hy_w_out'] = normal((n_d, d, d), d ** -0.5)
    inp['final_norm_g'] = gain((d,))
    return inp


def reference(x, c, ctx, c_ctx, ada_w, ada_b, norm1_g, norm2_g, ffn_w1, ffn_w3, ffn_w2,
              gdn_w_in, gdn_conv_w, gdn_a_log, gdn_dt_bias, gdn_norm_g, gdn_w_out,
              ret_w_in, ret_norm_g, ret_w_out,
              gla_w_in, gla_gate_w2, gla_gate_b, gla_norm_g, gla_w_out,
              hy_w_in, hy_conv_w, hy_ff_w1, hy_ff_b1, hy_ff_w2, hy_ff_b2, hy_ff_w3, hy_sin_freq,
              hy_skip, hy_w_out, final_norm_g):
    silu_c = jax.nn.silu(c)
    silu_cc = jax.nn.silu(c_ctx)
    for i in range(DEPTH):
        kind = i % N_MIXERS
        j = i // N_MIXERS
        carry_ctx = any((l % N_MIXERS) != MIX_HYENA for l in range(i + 1, DEPTH))
        use_ctx = carry_ctx or kind != MIX_HYENA
        mod = silu_c @ ada_w[i] + ada_b[i]
        sh1, sc1, gt1, sh2, sc2, gt2 = jnp.split(mod[:, None, :], 6, axis=-1)
        h = modulate(x, norm1_g[i], sh1, sc1)
        hc = None
        if use_ctx:
            csh1, csc1, cgt1, csh2, csc2, cgt2 = jnp.split(silu_cc @ ada_w[i] + ada_b[i], 6)
            hc = modulate(ctx, norm1_g[i], csh1, csc1)
        if kind == MIX_GDN:
            y, yc = gdn_mixer(h, hc, gdn_w_in[j], gdn_conv_w[j], gdn_a_log[j], gdn_dt_bias[j],
                              gdn_norm_g[j], gdn_w_out[j], carry_ctx)
        elif kind == MIX_RET:
            y, yc = retention_mixer(h, hc, ret_w_in[j], ret_norm_g[j], ret_w_out[j], carry_ctx)
        elif kind == MIX_GLA:
            y, yc = gla_mixer(h, hc, gla_w_in[j], gla_gate_w2[j], gla_gate_b[j], gla_norm_g[j],
                              gla_w_out[j], carry_ctx)
        else:
            y, yc = hyena_mixer(h, hc, hy_w_in[j], hy_conv_w[j], hy_ff_w1[j], hy_ff_b1[j], hy_ff_w2[j],
                                hy_ff_b2[j], hy_ff_w3[j], hy_sin_freq[j], hy_skip[j], hy_w_out[j])
        x = x + gt1 * y
        x = x + gt2 * swiglu(modulate(x, norm2_g[i], sh2, sc2), ffn_w1[i], ffn_w3[i], ffn_w2[i])
        if carry_ctx:
            ctx = ctx + cgt1 * yc
            ctx = ctx + cgt2 * swiglu(modulate(ctx, norm2_g[i], csh2, csc2), ffn_w1[i], ffn_w3[i], ffn_w2[i])
    return rms_norm(x, final_norm_g)
```

```python
import math
import numpy as np
from contextlib import ExitStack
import concourse.bass as bass
import concourse.mybir as mybir
from concourse.bass_utils import run_bass_kernel_spmd

F32 = mybir.dt.float32
ALU = mybir.AluOpType
AF = mybir.ActivationFunctionType
AX = mybir.AxisListType
P = 128
D = 1024
DFF = 2816
EPS = 1e-6
NCTX = 256
NDS = 24
SAME_WAIT = {'pe': False, 'dve': True, 'act': True, 'pool': True, 'sp': False}


class Res:
    __slots__ = ('name', 'w', 'r', 'excl')

    def __init__(self, name, excl=False):
        self.name = name
        self.w = {}
        self.r = {}
        self.excl = excl


class Prog:
    ENG = ['pe', 'dve', 'act', 'pool', 'sp']

    def __init__(self, nc, st):
        self.nc = nc
        self.st = st
        self.ops = {e: [] for e in self.ENG}
        self.sem = {e: st.enter_context(nc.semaphore('s_' + e)) for e in self.ENG}
        self.cnt = {e: 0 for e in self.ENG}
        self.known = {e: {} for e in self.ENG}
        self.dsem = [st.enter_context(nc.semaphore('d%d' % i)) for i in range(NDS)]
        self.dcnt = [0] * NDS
        self.dnext = 0
        self.nalloc = 0

    def sb(self, name, shape):
        t = self.st.enter_context(self.nc.sbuf_tensor(name, list(shape), F32))
        n = 1
        for s in shape[1:]:
            n *= s
        self.nalloc += n * 4
        return t, Res(name)

    def dram(self, name, shape):
        return self.nc.dram_tensor(name, list(shape), F32).ap(), Res(name)

    def _need(self, eng, reads, writes):
        waits = {}
        for r in reads:
            for dct in ((r.w, r.r) if r.excl else (r.w,)):
                for k, v in dct.items():
                    if waits.get(k, 0) < v:
                        waits[k] = v
        for r in writes:
            for dct in (r.w, r.r):
                for k, v in dct.items():
                    if waits.get(k, 0) < v:
                        waits[k] = v
        need = []
        kn = self.known[eng]
        for k, v in waits.items():
            if k == eng and not SAME_WAIT[eng]:
                continue
            if kn.get(k, 0) >= v:
                continue
            kn[k] = v
            need.append((k, v))
        return need

    def op(self, eng, fn, reads=(), writes=()):
        need = self._need(eng, reads, writes)
        self.cnt[eng] += 1
        v = self.cnt[eng]
        for r in reads:
            r.r[eng] = v
        for r in writes:
            r.w[eng] = v
        self.ops[eng].append((need, fn, None))

    def dma(self, eng, out, in_, reads=(), writes=(), **kw):
        i = self.dnext
        self.dnext = (i + 1) % NDS
        prev = self.dcnt[i]
        self.dcnt[i] += 16
        val = self.dcnt[i]
        key = ('d', i)
        need = self._need(eng, reads, writes)
        if prev > 0 and self.known[eng].get(key, 0) < prev:
            self.known[eng][key] = prev
            need.append((key, prev))
        for r in reads:
            r.r[key] = val
        for r in writes:
            r.w[key] = val
        self.ops[eng].append((need, lambda e: e.dma_start(out=out, in_=in_, **kw), i))

    def barrier(self):
        cur = {e: self.cnt[e] for e in self.ENG if self.cnt[e] > 0}
        for i in range(NDS):
            if self.dcnt[i] > 0:
                cur[('d', i)] = self.dcnt[i]
        for e in self.ENG:
            need = []
            for k, v in cur.items():
                if k == e or self.known[e].get(k, 0) >= v:
                    continue
                self.known[e][k] = v
                need.append((k, v))
            self.ops[e].append((need, None, None))

    def _semobj(self, k):
        return self.dsem[k[1]] if isinstance(k, tuple) else self.sem[k]

    def emit(self):
        nc = self.nc
        fin = [(('d', i), self.dcnt[i]) for i in range(NDS) if self.dcnt[i] > 0]
        fin += [(e, self.cnt[e]) for e in self.ENG if self.cnt[e] > 0 and e != 'sp']
        with nc.Block() as block:
            def run(eng, e, final=False):
                for need, fn, di in self.ops[eng]:
                    for k, v in need:
                        e.wait_ge(self._semobj(k), v)
                    if fn is None:
                        continue
                    ins = fn(e)
                    if di is None:
                        ins.then_inc(self.sem[eng], 1)
                    else:
                        ins.then_inc(self.dsem[di], 16)
                if final:
                    for k, v in fin:
                        e.wait_ge(self._semobj(k), v)

            @block.tensor
            def _(e):
                run('pe', e)

            @block.vector
            def _(e):
                run('dve', e)

            @block.scalar
            def _(e):
                run('act', e)

            @block.gpsimd
            def _(e):
                run('pool', e)

            @block.sync
            def _(e):
                run('sp', e, final=True)


def run_rr(gens):
    gens = list(gens)
    while gens:
        nxt = []
        for g in gens:
            try:
                next(g)
                nxt.append(g)
            except StopIteration:
                pass
        gens = nxt


class RingL:
    def __init__(self, items):
        self.items = items
        self.i = 0

    def next(self):
        it = self.items[self.i]
        self.i = (self.i + 1) % len(self.items)
        return it


class Ring:
    def __init__(self, pg, name, shape, n):
        self.items = [pg.sb('%s%d' % (name, i), shape) for i in range(n)]
        self.i = 0

    def next(self):
        it = self.items[self.i]
        self.i = (self.i + 1) % len(self.items)
        return it


def host_consts():
    i = np.arange(P)
    c = np.zeros((P, 8, P), np.float32)
    c[:, 0] = np.eye(P)
    c[:, 1] = 1.0
    c[:, 2] = (i[:, None] <= i[None, :])
    c[:, 3] = (i[:, None] >= i[None, :])
    c[:, 4] = (i[:, None] > i[None, :])
    c[:, 5] = (i[:, None] < i[None, :])
    return c.reshape(P, 8 * P)


class MK:
    def __init__(self, L, layers, dbg=()):
        self.L = L
        self.T = L + NCTX
        self.layers = layers
        self.dbg = dbg
        self.nc = bass.Bass("TRN2", target_bir_lowering=False)
        self.inputs = {}

    def inp(self, name, shape):
        t = self.nc.dram_tensor(name, list(shape), F32, kind="ExternalInput").ap()
        self.inputs[name] = t
        return t, Res(name)

    def build(self):
        nc = self.nc
        with ExitStack() as st:
            self.pg = pg = Prog(nc, st)
            T, L = self.T, self.L
            nl = len(self.layers)
            self.x_in, self.r_xin = self.inp('x', [L, D])
            self.ctx_in, self.r_ctxin = self.inp('ctx', [NCTX, D])
            self.c_in, _ = self.inp('c', [1, D])
            self.cc_in, _ = self.inp('c_ctx', [1, D])
            self.consts_in, _ = self.inp('consts', [P, 8 * P])
            self.ada_w, _ = self.inp('ada_w', [nl, D, 6 * D])
            self.ada_b, _ = self.inp('ada_b', [nl, 6 * D])
            self.n1g, _ = self.inp('norm1_g', [nl, D])
            self.n2g, _ = self.inp('norm2_g', [nl, D])
            self.w1, _ = self.inp('ffn_w1', [nl, D, DFF])
            self.w3, _ = self.inp('ffn_w3', [nl, D, DFF])
            self.w2, _ = self.inp('ffn_w2', [nl, DFF, D])
            self.fng, _ = self.inp('final_norm_g', [1, D])
            self.rW = Res('weights')
            self.out = nc.dram_tensor('out', [L, D], F32, kind="ExternalOutput").ap()
            self.r_out = Res('out')
            self.X, self.rX = pg.dram('X', [T, D])
            self.mixer_inputs()
            import os as _os
            xmb = int(_os.environ.get('EXTRA_DRAM_MB', '0'))
            if xmb:
                self.XTRA, self.rXTRA = pg.dram('XTRA', [xmb * 256, 1024])
                pg.dma('sp', self.XTRA[xmb * 256 - 256:xmb * 256, :], self.ctx_in[:, :], writes=[self.rXTRA])

            self.cst, self.r_cst = pg.sb('cst', [P, 8, P])
            pg.dma('sp', self.cst[:, :, :], self.consts_in.rearrange("p (k n) -> p k n", k=8),
                   writes=[self.r_cst])
            self.ident = self.cst[:, 0, :]
            self.ones = self.cst[:, 1, :]
            self.ps = []
            for i in range(8):
                t = st.enter_context(nc.psum_tensor('ps%d' % i, [P, 512], F32))
                self.ps.append((t, Res('ps%d' % i, excl=True)))
            self.NA = 34304
            self.arena = st.enter_context(nc.sbuf_tensor('arena', [P, self.NA], F32))
            pg.nalloc += self.NA * 4
            self.aoff = 0
            self.xin, self.r_xin_sb = self.carve('xin', [4, D])
            self.hT, self.r_hT = self.carve('hT', [8, 512])
            self.junk, self.r_junk = self.carve('junk', [D])
            self.wring = RingL([self.carve('wr%d' % i, [8, 512]) for i in range(3)])
            self.gT, self.r_gT = self.carve('gT', [22, 512])
            self.tmp512 = RingL([self.carve('tmp%d' % i, [512]) for i in range(3)])
            self.stat, self.r_stat = pg.sb('stat', [P, 40])
            self.hq = Ring(pg, 'hq', [P, 512], 4)
            self.gbc, self.r_gbc = pg.sb('gbc', [P, 2048])
            self.modT, self.r_modT = pg.sb('modT', [P, 48, 2])
            self.sc, self.r_sc = pg.sb('sc', [P, 8, 2])
            self.AB, self.r_AB = pg.sb('AB', [P, 4, 8, 2])
            self.gvec, self.r_gvec = pg.sb('gvec', [P, 3, 8])
            self.gtb, self.r_gtb = pg.sb('gtb', [P, 4, D])
            self.diag, self.r_diag = pg.sb('diag', [P, P])
            self.blocks = [(0, NCTX, 1)] + [(NCTX + 512 * k, 512, 0) for k in range(L // 512)]

            pg.dma('sp', self.X[0:NCTX, :], self.ctx_in[:, :], writes=[self.rX])
            pg.dma('sp', self.X[NCTX:T, :], self.x_in[:, :], writes=[self.rX])
            self.silu_c()
            for li, (kind, j) in enumerate(self.layers):
                self.layer_mod(li)
                self.mixer(li, kind, j)
                self.ffn(li)
            self.final_norm()
            pg.emit()
        return nc

    def carve(self, name, shape):
        n = 1
        for v in shape:
            n *= v
        assert self.aoff + n <= self.NA, (name, self.aoff, n)
        ap = self.arena[:, self.aoff:self.aoff + n]
        self.aoff += n
        if len(shape) == 2:
            ap = ap.rearrange("p (a b) -> p a b", a=shape[0])
        elif len(shape) == 3:
            ap = ap.rearrange("p (a b c) -> p a b c", a=shape[0], b=shape[1])
        return ap, Res(name)

    def psb(self, i):
        return self.ps[i]

    def rstd(self, out, tmp, ss, scale, res):
        pg = self.pg
        pg.op('dve', lambda e: e.tensor_scalar(tmp, ss, scale, EPS, ALU.mult, ALU.add), reads=[res], writes=[res])
        pg.op('act', lambda e: e.activation(tmp, tmp, AF.Sqrt), reads=[res], writes=[res])
        pg.op('dve', lambda e: e.reciprocal(out, tmp), reads=[res], writes=[res])

    def silu_c(self):
        pg = self.pg
        pg.dma('sp', self.sc[:, :, 0], self.c_in.rearrange("o (k p) -> p (o k)", p=P),
               writes=[self.r_sc], allow_slow_non_contiguous=True)
        pg.dma('sp', self.sc[:, :, 1], self.cc_in.rearrange("o (k p) -> p (o k)", p=P),
               writes=[self.r_sc], allow_slow_non_contiguous=True)
        pg.op('act', lambda e: e.activation(self.sc[:, :, :], self.sc[:, :, :], AF.Silu),
              reads=[self.r_sc], writes=[self.r_sc])

    def bcast_row(self, dst, r_dst, srcT, r_src, nchunk):
        pg = self.pg
        for c0 in range(0, nchunk, 4):
            pt, rp = self.psb(7)
            for c in range(c0, min(nchunk, c0 + 4)):
                pg.op('dve', lambda e, c=c: e.tensor_scalar(self.diag[:, :], self.ident, srcT[:, c:c + 1], None,
                                                            ALU.mult),
                      reads=[r_src, self.r_cst], writes=[self.r_diag])
                pg.op('pe', lambda e, c=c, pt=pt, c0=c0: e.matmul(pt[:, (c - c0) * P:(c - c0 + 1) * P], self.ones,
                                                                   self.diag[:, :], start=True, stop=True),
                      reads=[self.r_diag, self.r_cst], writes=[rp])
            n = min(nchunk, c0 + 4) - c0
            pg.op('act', lambda e, pt=pt, c0=c0, n=n: e.copy(dst[:, c0 * P:(c0 + n) * P], pt[:, 0:n * P]),
                  reads=[rp], writes=[r_dst])

    def layer_mod(self, li):
        pg = self.pg
        modT, sc = self.modT, self.sc
        badd, r_badd = self.tmp512.next()
        pg.dma('sp', badd[:, 0:48], self.ada_b[li:li + 1, :].rearrange("o (k p) -> p (o k)", p=P),
               writes=[r_badd], allow_slow_non_contiguous=True)
        pg.dma('sp', self.gvec[:, 0, :], self.n1g[li:li + 1, :].rearrange("o (k p) -> p (o k)", p=P),
               writes=[self.r_gvec], allow_slow_non_contiguous=True)
        pg.dma('sp', self.gvec[:, 1, :], self.n2g[li:li + 1, :].rearrange("o (k p) -> p (o k)", p=P),
               writes=[self.r_gvec], allow_slow_non_contiguous=True)
        for og in range(12):
            wt, rw = self.wring.next()
            pg.dma('sp', wt[:, :, :], self.ada_w[li, :, og * 512:(og + 1) * 512].rearrange("(k p) n -> p k n", p=P),
                   reads=[self.rW], writes=[rw])
            pt, rp = self.psb(6)
            for oc in range(4):
                for kc in range(8):
                    pg.op('pe', lambda e, wt=wt, pt=pt, oc=oc, kc=kc: e.matmul(
                        pt[:, oc * 2:oc * 2 + 2], wt[:, kc, oc * P:(oc + 1) * P], sc[:, kc, :],
                        start=(kc == 0), stop=(kc == 7)), reads=[rw, self.r_sc], writes=[rp])
            for oc in range(4):
                o = og * 4 + oc
                pg.op('dve', lambda e, pt=pt, oc=oc, o=o: e.tensor_scalar(
                    modT[:, o, :], pt[:, oc * 2:oc * 2 + 2], badd[:, o:o + 1], None, ALU.add),
                    reads=[rp, r_badd], writes=[self.r_modT])
        AB = self.AB
        for n, (gsel, so, sho) in enumerate([(0, 8, 0), (1, 32, 24)]):
            pg.op('dve', lambda e, n=n, so=so, gsel=gsel: e.scalar_tensor_tensor(
                AB[:, 2 * n, :, :], modT[:, so:so + 8, :], 1.0,
                self.gvec[:, gsel, :].unsqueeze(2).to_broadcast([P, 8, 2]), ALU.add, ALU.mult),
                reads=[self.r_modT, self.r_gvec], writes=[self.r_AB])
            pg.op('dve', lambda e, n=n, sho=sho: e.tensor_copy(AB[:, 2 * n + 1, :, :], modT[:, sho:sho + 8, :]),
                  reads=[self.r_modT], writes=[self.r_AB])
        gsrc, r_gsrc = self.tmp512.next()
        for n, go in enumerate([16, 40]):
            for j in range(2):
                idx = n * 2 + j
                pg.op('dve', lambda e, idx=idx, go=go, j=j: e.tensor_copy(gsrc[:, idx * 8:(idx + 1) * 8],
                                                                         modT[:, go:go + 8, j]),
                      reads=[self.r_modT], writes=[r_gsrc])
        for idx in range(4):
            self.bcast_row(self.gtb[:, idx, :], self.r_gtb, gsrc[:, idx * 8:(idx + 1) * 8], r_gsrc, 8)

    def load_norm(self, blk, which, g_final=None):
        pg = self.pg
        t0, nt, isctx = blk
        ns = nt // P
        xin, hT, stat = self.xin, self.hT, self.stat
        pg.dma('sp', xin[:, 0:ns, :], self.X[t0:t0 + nt, :].rearrange("(s p) d -> p s d", p=P),
               reads=[self.rX], writes=[self.r_xin_sb])
        for s in range(ns):
            pg.op('act', lambda e, s=s: e.activation(self.junk[:, :], xin[:, s, :], AF.Square,
                                                     accum_out=stat[:, s:s + 1]),
                  reads=[self.r_xin_sb], writes=[self.r_junk, self.r_stat])
        self.rstd(stat[:, 8:8 + ns], stat[:, 4:4 + ns], stat[:, 0:ns], 1.0 / D, self.r_stat)
        xn, r_xn = self.gT, self.r_gT
        for s in range(ns):
            pg.op('act', lambda e, s=s: e.activation(xn[:, s * 2:(s + 1) * 2, :].rearrange("p a b -> p (a b)"),
                                                     xin[:, s, :], AF.Copy, scale=stat[:, 8 + s:9 + s]),
                  reads=[self.r_xin_sb, self.r_stat], writes=[r_xn])
        for kc in range(8):
            pt, rp = self.psb(kc % 2)
            for s in range(ns):
                pg.op('pe', lambda e, s=s, kc=kc, pt=pt: e.transpose(
                    pt[:, s * P:(s + 1) * P],
                    xn[:, s * 2 + kc // 4, (kc % 4) * P:(kc % 4 + 1) * P], self.ident),
                    reads=[r_xn, self.r_cst], writes=[rp])
            if g_final is None:
                A = self.AB[:, 2 * which, kc, isctx:isctx + 1]
                B = self.AB[:, 2 * which + 1, kc, isctx:isctx + 1]
                pg.op('dve', lambda e, kc=kc, pt=pt, A=A, B=B: e.tensor_scalar(
                    hT[:, kc, 0:nt], pt[:, 0:nt], A, B, ALU.mult, ALU.add),
                    reads=[rp, self.r_AB], writes=[self.r_hT])
            else:
                pg.op('dve', lambda e, kc=kc, pt=pt: e.tensor_scalar(
                    hT[:, kc, 0:nt], pt[:, 0:nt], g_final[:, kc:kc + 1], None, ALU.mult),
                    reads=[rp, self.r_gvec], writes=[self.r_hT])

    def update_x_store(self, blk, s, dg, pt, rp, gsel, last):
        pg = self.pg
        t0, nt, isctx = blk
        tt, rt = self.tmp512.next()
        grow = self.gtb[:, gsel * 2 + isctx, dg * 512:(dg + 1) * 512]
        pg.op('dve', lambda e: e.tensor_tensor(tt[:, :], pt[:, :], grow, ALU.mult),
              reads=[rp, self.r_gtb], writes=[rt])
        pg.op('pool', lambda e: e.tensor_tensor(self.xin[:, s, dg * 512:(dg + 1) * 512],
                                                self.xin[:, s, dg * 512:(dg + 1) * 512], tt[:, :], ALU.add),
              reads=[rt, self.r_xin_sb], writes=[self.r_xin_sb])
        if last:
            ns = nt // P
            pg.dma('sp', self.X[t0:t0 + nt, :].rearrange("(s p) d -> p s d", p=P), self.xin[:, 0:ns, :],
                   reads=[self.r_xin_sb], writes=[self.rX])

    def ffn(self, li):
        for blk in self.blocks:
            self.ffn_blk(li, blk)

    def ffn_blk(self, li, blk):
        pg = self.pg
        t0, nt, isctx = blk
        ns = nt // P
        self.load_norm(blk, 1)
        hT, gT = self.hT, self.gT
        for fc in range(22):
            w1t, rw1 = self.wring.next()
            pg.dma('sp', w1t[:, :, 0:P], self.w1[li, :, fc * P:(fc + 1) * P].rearrange("(k p) n -> p k n", p=P),
                   reads=[self.rW], writes=[rw1])
            pg.dma('sp', w1t[:, :, P:2 * P], self.w3[li, :, fc * P:(fc + 1) * P].rearrange("(k p) n -> p k n", p=P),
                   reads=[self.rW], writes=[rw1])
            p1, rp1 = self.psb(2)
            p3, rp3 = self.psb(3)
            for kc in range(8):
                pg.op('pe', lambda e, kc=kc, w1t=w1t, p1=p1: e.matmul(
                    p1[:, 0:nt], w1t[:, kc, 0:P], hT[:, kc, 0:nt], start=(kc == 0), stop=(kc == 7)),
                    reads=[rw1, self.r_hT], writes=[rp1])
            for kc in range(8):
                pg.op('pe', lambda e, kc=kc, w1t=w1t, p3=p3: e.matmul(
                    p3[:, 0:nt], w1t[:, kc, P:2 * P], hT[:, kc, 0:nt], start=(kc == 0), stop=(kc == 7)),
                    reads=[rw1, self.r_hT], writes=[rp3])
            tt, rt = self.tmp512.next()
            pg.op('act', lambda e, tt=tt, p1=p1: e.activation(tt[:, 0:nt], p1[:, 0:nt], AF.Silu),
                  reads=[rp1], writes=[rt])
            pg.op('dve', lambda e, tt=tt, p3=p3, fc=fc: e.tensor_tensor(gT[:, fc, 0:nt], tt[:, 0:nt],
                                                                       p3[:, 0:nt], ALU.mult),
                  reads=[rt, rp3], writes=[self.r_gT])
        for dg in range(2):
            accs = [self.psb(4 + s) for s in range(ns)]
            for f0 in range(0, 22, 8):
                nf = min(8, 22 - f0)
                w2t, rw2 = self.wring.next()
                pg.dma('sp', w2t[:, 0:nf, :],
                       self.w2[li, f0 * P:(f0 + nf) * P, dg * 512:(dg + 1) * 512].rearrange("(k p) n -> p k n", p=P),
                       reads=[self.rW], writes=[rw2])
                for ff in range(nf):
                    fc = f0 + ff
                    for s in range(ns):
                        pt, rp = accs[s]
                        pg.op('pe', lambda e, pt=pt, fc=fc, ff=ff, s=s, w2t=w2t: e.matmul(
                            pt[:, :], gT[:, fc, s * P:(s + 1) * P], w2t[:, ff, :],
                            start=(fc == 0), stop=(fc == 21)), reads=[rw2, self.r_gT], writes=[rp])
            for s in range(ns):
                pt, rp = accs[s]
                self.update_x_store(blk, s, dg, pt, rp, 1, last=(dg == 1 and s == ns - 1))

    def final_norm(self):
        pg = self.pg
        pg.dma('sp', self.gvec[:, 2, :], self.fng.rearrange("o (k p) -> p (o k)", p=P),
               writes=[self.r_gvec], allow_slow_non_contiguous=True)
        grow, r_grow = self.gtb[:, 0, :], self.r_gtb
        self.bcast_row(grow, r_grow, self.gvec[:, 2, :], self.r_gvec, 8)
        for blk in self.blocks[1:]:
            t0, nt, isctx = blk
            ns = nt // P
            xin, stat = self.xin, self.stat
            pg.dma('sp', xin[:, 0:ns, :], self.X[t0:t0 + nt, :].rearrange("(s p) d -> p s d", p=P),
                   reads=[self.rX], writes=[self.r_xin_sb])
            for s in range(ns):
                pg.op('act', lambda e, s=s: e.activation(self.junk[:, :], xin[:, s, :], AF.Square,
                                                         accum_out=stat[:, s:s + 1]),
                      reads=[self.r_xin_sb], writes=[self.r_junk, self.r_stat])
            self.rstd(stat[:, 8:8 + ns], stat[:, 4:4 + ns], stat[:, 0:ns], 1.0 / D, self.r_stat)
            for s in range(ns):
                pg.op('dve', lambda e, s=s: e.scalar_tensor_tensor(
                    xin[:, s, :], xin[:, s, :], stat[:, 8 + s:9 + s], grow, ALU.mult, ALU.mult),
                    reads=[self.r_xin_sb, self.r_stat, r_grow], writes=[self.r_xin_sb])
            pg.dma('sp', self.out[t0 - NCTX:t0 - NCTX + nt, :].rearrange("(s p) d -> p s d", p=P),
                   xin[:, 0:ns, :], reads=[self.r_xin_sb], writes=[self.r_out])

    def mixer(self, li, kind, j):
        if kind < 0:
            return
        if kind in (1, 2):
            self.linattn(li, kind, j)
            return
        if kind == 0:
            self.gdn(li, j)
            return
        if kind == 3:
            self.hyena(li, j)
            return
        raise NotImplementedError

    def mixer_inputs(self):
        nc, pg, T = self.nc, self.pg, self.T
        kinds = [k for k, _ in self.layers]
        if 1 in kinds or 2 in kinds or 0 in kinds:
            self.QT, self.rQT = pg.dram('QT', [1024, T])
            self.KT, self.rKT = pg.dram('KT', [1024, T])
            self.Vd, self.rVd = pg.dram('Vd', [T, 2048])
            self.Gd, self.rGd = pg.dram('Gd', [T, 2048])
            self.Od, self.rOd = pg.dram('Od', [T, 2048])
            self.LOWT, self.rLOWT = pg.dram('LOWT', [32, T])
        if 0 in kinds:
            self.CQ, self.rCQ = pg.dram('CQ', [3072, T])
            self.BG, self.rBG = pg.dram('BG', [T, 32])
            self.gdn_w_in, _ = self.inp('gdn_w_in', [1, D, 4128])
            self.gdn_conv_w, _ = self.inp('gdn_conv_w', [1, 5, 3072])
            self.gdn_a_log, _ = self.inp('gdn_a_log', [1, 16])
            self.gdn_dt_bias, _ = self.inp('gdn_dt_bias', [1, 16])
            self.gdn_norm_g, _ = self.inp('gdn_norm_g', [1, 128])
            self.gdn_w_out, _ = self.inp('gdn_w_out', [1, 1024, D])
        if 3 in kinds:
            if 0 not in kinds:
                self.CQ, self.rCQ = pg.dram('CQ', [3072, T])
            self.FT, self.rFT = pg.dram('FT', [2048, self.L])
            self.Z1, self.rZ1 = pg.dram('Z1', [1024, self.L])
            self.Z2, self.rZ2 = pg.dram('Z2', [1024, self.L])
            self.hy_w_in, _ = self.inp('hy_w_in', [1, D, 3072])
            self.hy_conv_w, _ = self.inp('hy_conv_w', [1, 3, 3072])
            self.hy_ff_w1, _ = self.inp('hy_ff_w1', [1, 33, 64])
            self.hy_ff_b1, _ = self.inp('hy_ff_b1', [1, 64])
            self.hy_ff_w2, _ = self.inp('hy_ff_w2', [1, 64, 64])
            self.hy_ff_b2, _ = self.inp('hy_ff_b2', [1, 64])
            self.hy_ff_w3, _ = self.inp('hy_ff_w3', [1, 64, 2048])
            self.hy_sin_freq, _ = self.inp('hy_sin_freq', [1, 64])
            self.hy_skip, _ = self.inp('hy_skip', [1, 2048])
            self.hy_w_out, _ = self.inp('hy_w_out', [1, D, D])
            self.hyt, _ = self.inp('hyt', [P, 10 * P])
            self.hyz, _ = self.inp('hyz', [33, self.L])
            self.hydist, _ = self.inp('hydist', [1, self.L])
            self.hydelta, _ = self.inp('hydelta', [P, 8])
        if 1 in kinds:
            self.ret_w_in, _ = self.inp('ret_w_in', [1, D, 6144])
            self.ret_norm_g, _ = self.inp('ret_norm_g', [1, 2048])
            self.ret_w_out, _ = self.inp('ret_w_out', [1, 2048, D])
            self.rot, _ = self.inp('rot', [2, P, T])
        if 2 in kinds:
            self.gla_w_in, _ = self.inp('gla_w_in', [1, D, 3104])
            self.gla_gate_w2, _ = self.inp('gla_gate_w2', [1, 2, 16, 512])
            self.gla_gate_b, _ = self.inp('gla_gate_b', [1, 2, 512])
            self.gla_norm_g, _ = self.inp('gla_norm_g', [1, 256])
            self.gla_w_out, _ = self.inp('gla_w_out', [1, 1024, D])

    def proj_feat(self, W, col0, M, nt, bank):
        pg = self.pg
        wt, rw = self.wring.next()
        pg.dma('sp', wt[:, :, 0:M], W[:, col0:col0 + M].rearrange("(k p) n -> p k n", p=P),
               reads=[self.rW], writes=[rw])
        pt, rp = self.psb(bank)
        for kc in range(8):
            pg.op('pe', lambda e, kc=kc: e.matmul(pt[0:M, 0:nt], wt[:, kc, 0:M], self.hT[:, kc, 0:nt],
                                                  start=(kc == 0), stop=(kc == 7)),
                  reads=[rw, self.r_hT], writes=[rp])
        return pt, rp

    def proj_tok_store(self, W, col0, N, blk, dst, rdst, dcol0):
        pg = self.pg
        t0, nt, isctx = blk
        wt, rw = self.wring.next()
        pg.dma('sp', wt[:, :, 0:N], W[:, col0:col0 + N].rearrange("(k p) n -> p k n", p=P),
               reads=[self.rW], writes=[rw])
        for s in range(nt // P):
            pt, rp = self.psb(2 + s % 2)
            for kc in range(8):
                pg.op('pe', lambda e, kc=kc, s=s, pt=pt: e.matmul(
                    pt[:, 0:N], self.hT[:, kc, s * P:(s + 1) * P], wt[:, kc, 0:N],
                    start=(kc == 0), stop=(kc == 7)), reads=[rw, self.r_hT], writes=[rp])
            tt, rt = self.tmp512.next()
            pg.op('act', lambda e, pt=pt, tt=tt: e.copy(tt[:, 0:N], pt[:, 0:N]), reads=[rp], writes=[rt])
            pg.dma('sp', dst[t0 + s * P:t0 + (s + 1) * P, dcol0:dcol0 + N], tt[:, 0:N], reads=[rt], writes=[rdst])

    def linattn(self, li, kind, j):
        pg = self.pg
        if kind == 2:
            H, ndc, DV = 4, 1, 256
            W = self.gla_w_in[j]
            qoff, koff, voff, goff = 0, 512, 1024, 2048
            Wout, ng = self.gla_w_out[j], self.gla_norm_g
        else:
            H, ndc, DV = 4, 2, 512
            W = self.ret_w_in[j]
            qoff, koff, voff, goff = 0, 1024, 2048, 4096
            Wout, ng = self.ret_w_out[j], self.ret_norm_g
        DK = ndc * P
        V = H * DV
        QK = H * DK
        qscale = float(DK) ** -0.5
        def m1(blk):
            t0, nt, isctx = blk
            self.load_norm(blk, 0)
            rotate = (kind == 1 and not isctx)
            if rotate:
                cs, r_cs = self.junk, self.r_junk
                pg.dma('sp', cs[:, 0:nt], self.rot[0, :, t0:t0 + nt], writes=[r_cs])
                pg.dma('sp', cs[:, 512:512 + nt], self.rot[1, :, t0:t0 + nt], writes=[r_cs])
            for (off, dst, rdst, scale) in ((qoff, self.QT, self.rQT, qscale), (koff, self.KT, self.rKT, 1.0)):
                if not rotate:
                    for c in range(QK // P):
                        pt, rp = self.proj_feat(W, off + c * P, P, nt, 2 + c % 2)
                        tt, rt = self.tmp512.next()
                        pg.op('act', lambda e, pt=pt, tt=tt, scale=scale: e.activation(
                            tt[:, 0:nt], pt[:, 0:nt], AF.Copy, scale=scale), reads=[rp], writes=[rt])
                        pg.dma('sp', dst[c * P:(c + 1) * P, t0:t0 + nt], tt[:, 0:nt], reads=[rt], writes=[rdst])
                else:
                    for h in range(H):
                        pa, rpa = self.proj_feat(W, off + (2 * h) * P, P, nt, 2)
                        pb, rpb = self.proj_feat(W, off + (2 * h + 1) * P, P, nt, 3)
                        a, ra = self.tmp512.next()
                        b, rb = self.tmp512.next()
                        o1, ro1 = self.tmp512.next()
                        pg.op('act', lambda e, pa=pa, a=a, scale=scale: e.activation(
                            a[:, 0:nt], pa[:, 0:nt], AF.Copy, scale=scale), reads=[rpa], writes=[ra])
                        pg.op('act', lambda e, pb=pb, b=b, scale=scale: e.activation(
                            b[:, 0:nt], pb[:, 0:nt], AF.Copy, scale=scale), reads=[rpb], writes=[rb])
                        cos, sin = cs[:, 0:nt], cs[:, 512:512 + nt]
                        t1, rt1 = self.hq.next()
                        t2, rt2 = self.hq.next()
                        pg.op('dve', lambda e, t1=t1, a=a: e.tensor_tensor(t1[:, 0:nt], a[:, 0:nt], cos, ALU.mult),
                              reads=[ra, r_cs], writes=[rt1])
                        pg.op('pool', lambda e, t2=t2, b=b: e.tensor_tensor(t2[:, 0:nt], b[:, 0:nt], sin, ALU.mult),
                              reads=[rb, r_cs], writes=[rt2])
                        pg.op('dve', lambda e, o1=o1, t1=t1, t2=t2: e.tensor_tensor(
                            o1[:, 0:nt], t1[:, 0:nt], t2[:, 0:nt], ALU.subtract), reads=[rt1, rt2], writes=[ro1])
                        pg.dma('sp', dst[(2 * h) * P:(2 * h + 1) * P, t0:t0 + nt], o1[:, 0:nt], reads=[ro1],
                               writes=[rdst])
                        t3, rt3 = self.hq.next()
                        t4, rt4 = self.hq.next()
                        pg.op('dve', lambda e, t3=t3, a=a: e.tensor_tensor(t3[:, 0:nt], a[:, 0:nt], sin, ALU.mult),
                              reads=[ra, r_cs], writes=[rt3])
                        pg.op('pool', lambda e, t4=t4, b=b: e.tensor_tensor(t4[:, 0:nt], b[:, 0:nt], cos, ALU.mult),
                              reads=[rb, r_cs], writes=[rt4])
                        pg.op('dve', lambda e, t3=t3, t4=t4: e.tensor_tensor(
                            t3[:, 0:nt], t3[:, 0:nt], t4[:, 0:nt], ALU.add), reads=[rt3, rt4], writes=[rt3])
                        pg.dma('sp', dst[(2 * h + 1) * P:(2 * h + 2) * P, t0:t0 + nt], t3[:, 0:nt], reads=[rt3],
                               writes=[rdst])
            for cg in range(V // 512):
                self.proj_tok_store(W, voff + cg * 512, 512, blk, self.Vd, self.rVd, cg * 512)
                self.proj_tok_store(W, goff + cg * 512, 512, blk, self.Gd, self.rGd, cg * 512)
            if kind == 2:
                pt, rp = self.proj_feat(W, 3072, 32, nt, 2)
                tt, rt = self.tmp512.next()
                pg.op('act', lambda e, pt=pt, tt=tt: e.copy(tt[0:32, 0:nt], pt[0:32, 0:nt]), reads=[rp], writes=[rt])
                pg.dma('sp', self.LOWT[0:32, t0:t0 + nt], tt[0:32, 0:nt], reads=[rt], writes=[self.rLOWT])
        for blk in self.blocks:
            m1(blk)
        self.scan_linattn(kind, j, H, ndc, DV)
        self.out_proj(li, H, DV, ng, kind == 1, Wout, per_head_g=(kind == 1))

    def gdn(self, li, j):
        pg = self.pg
        W = self.gdn_w_in[j]
        H = 8
        cst = self.cst
        cb, r_cb = self.hq.next()
        pg.dma('sp', cb[:, 0:16], self.gdn_dt_bias[j:j + 1, :].partition_broadcast(P), writes=[r_cb])
        pg.dma('sp', cb[:, 16:32], self.gdn_a_log[j:j + 1, :].partition_broadcast(P), writes=[r_cb])
        pg.op('act', lambda e: e.activation(cb[:, 16:32], cb[:, 16:32], AF.Exp), reads=[r_cb], writes=[r_cb])
        pg.op('dve', lambda e: e.tensor_scalar(cb[:, 16:32], cb[:, 16:32], -1.0, None, ALU.mult), reads=[r_cb],
              writes=[r_cb])

        def m1(blk):
            t0, nt, isctx = blk
            self.load_norm(blk, 0)
            for c in range(24):
                pt, rp = self.proj_feat(W, c * P, P, nt, 2 + c % 2)
                tt, rt = self.tmp512.next()
                pg.op('act', lambda e, tt=tt, pt=pt: e.copy(tt[:, 0:nt], pt[:, 0:nt]), reads=[rp], writes=[rt])
                pg.dma('sp', self.CQ[c * P:(c + 1) * P, t0:t0 + nt], tt[:, 0:nt], reads=[rt], writes=[self.rCQ])
            for cg in range(2):
                self.proj_tok_store(W, 3072 + cg * 512, 512, blk, self.Gd, self.rGd, cg * 512)
            wt, rw = self.wring.next()
            pg.dma('sp', wt[:, :, 0:32], W[:, 4096:4128].rearrange("(k p) n -> p k n", p=P), reads=[self.rW],
                   writes=[rw])
            for s in range(nt // P):
                pt, rp = self.psb(2 + s % 2)
                for kc in range(8):
                    pg.op('pe', lambda e, kc=kc, s=s, pt=pt: e.matmul(
                        pt[:, 0:32], self.hT[:, kc, s * P:(s + 1) * P], wt[:, kc, 0:32],
                        start=(kc == 0), stop=(kc == 7)), reads=[rw, self.r_hT], writes=[rp])
                tt, rt = self.tmp512.next()
                pg.op('act', lambda e, pt=pt, tt=tt: e.activation(tt[:, 0:16], pt[:, 0:16], AF.Sigmoid),
                      reads=[rp], writes=[rt])
                pg.op('dve', lambda e, pt=pt, tt=tt: e.tensor_tensor(tt[:, 16:32], pt[:, 16:32], cb[:, 0:16], ALU.add),
                      reads=[rp, r_cb], writes=[rt])
                pg.op('act', lambda e, tt=tt: e.activation(tt[:, 16:32], tt[:, 16:32], AF.Exp), reads=[rt], writes=[rt])
                pg.op('act', lambda e, tt=tt: e.activation(tt[:, 16:32], tt[:, 16:32], AF.Ln, bias=1.0), reads=[rt],
                      writes=[rt])
                pg.op('dve', lambda e, tt=tt: e.tensor_tensor(tt[:, 16:32], tt[:, 16:32], cb[:, 16:32], ALU.mult),
                      reads=[rt, r_cb], writes=[rt])
                pg.dma('sp', self.BG[t0 + s * P:t0 + (s + 1) * P, :], tt[:, 0:32], reads=[rt], writes=[self.rBG])
        for blk in self.blocks:
            m1(blk)

        pg.barrier()
        save = self.aoff
        self.aoff = 0
        L = self.L
        sq, r_sq = self.carve('csq', [512])
        cin, r_cin = self.carve('cin', [L + 4])
        acc, r_acc = self.carve('cacc', [L])
        cw, r_cw = self.carve('cw', [24, 5])
        for k in range(5):
            pg.dma('sp', cw[:, :, k], self.gdn_conv_w[j, k:k + 1, :].rearrange("o (c p) -> p (o c)", p=P),
                   writes=[r_cw], allow_slow_non_contiguous=True)

        import os as _os
        skip = _os.environ.get('GDN_SKIP', '')

        def conv_seg(rc, a0, n):
            pg.op('pool', lambda e: e.memset(cin[:, 0:2], 0.0), writes=[r_cin])
            pg.op('pool', lambda e: e.memset(cin[:, n + 2:n + 4], 0.0), writes=[r_cin])
            pg.dma('sp', cin[:, 2:2 + n], self.CQ[rc * P:(rc + 1) * P, a0:a0 + n], reads=[self.rCQ], writes=[r_cin])
            def piece(p0, pn):
                pg.op('dve', lambda e: e.tensor_scalar(acc[:, p0:p0 + pn], cin[:, p0:p0 + pn], cw[:, rc, 0:1], None,
                                                       ALU.mult), reads=[r_cin, r_cw], writes=[r_acc])
                for k in range(1, 5):
                    pg.op('dve', lambda e, k=k: e.scalar_tensor_tensor(
                        acc[:, p0:p0 + pn], cin[:, p0 + k:p0 + k + pn], cw[:, rc, k:k + 1], acc[:, p0:p0 + pn],
                        ALU.mult, ALU.add), reads=[r_cin, r_cw, r_acc], writes=[r_acc])
                pg.op('act', lambda e: e.activation(acc[:, p0:p0 + pn], acc[:, p0:p0 + pn], AF.Silu), reads=[r_acc],
                      writes=[r_acc])
            for p0 in range(0, n, 2048):
                piece(p0, min(2048, n - p0))

            def l2piece(p0, pn, scale):
                pg.op('pool', lambda e: e.tensor_tensor(sq[:, 0:pn], acc[:, p0:p0 + pn], acc[:, p0:p0 + pn],
                                                        ALU.mult), reads=[r_acc], writes=[r_sq])
                pt, rp = self.psb((p0 // 512) % 2)
                pg.op('pe', lambda e: e.matmul(pt[:, 0:pn], self.ones, sq[:, 0:pn], start=True, stop=True),
                      reads=[r_sq, self.r_cst], writes=[rp])
                pg.op('dve', lambda e: e.tensor_scalar(sq[:, 0:pn], pt[:, 0:pn], 1.0, EPS, ALU.mult, ALU.add),
                      reads=[rp], writes=[r_sq])
                pg.op('act', lambda e: e.activation(sq[:, 0:pn], sq[:, 0:pn], AF.Sqrt), reads=[r_sq], writes=[r_sq])
                pg.op('dve', lambda e: e.reciprocal(sq[:, 0:pn], sq[:, 0:pn]), reads=[r_sq], writes=[r_sq])
                pg.op('dve', lambda e: e.scalar_tensor_tensor(acc[:, p0:p0 + pn], acc[:, p0:p0 + pn], scale,
                                                              sq[:, 0:pn], ALU.mult, ALU.mult),
                      reads=[r_acc, r_sq], writes=[r_acc])
            if rc < 16 and 'l2' not in skip:
                for p0 in range(0, n, 512):
                    l2piece(p0, min(512, n - p0), (128.0 ** -0.5) if rc < 8 else 1.0)
            pg.dma('sp', self.CQ[rc * P:(rc + 1) * P, a0:a0 + n], acc[:, 0:n], reads=[r_acc], writes=[self.rCQ])
        for rc in range(24):
            if 'conv' in skip:
                break
            conv_seg(rc, 0, NCTX)
            conv_seg(rc, NCTX, L)

        pg.barrier()
        self.aoff = 0
        cv = self.carve
        NQ = 8
        TB = 256
        S, rS = cv('gS', [H, P])
        fsets = RingL([dict(bgt=cv('bgt%d' % i, [2, 32]), fac=cv('fac%d' % i, [5, 2, 8]), gcs=cv('gcs%d' % i, [8]))
                       for i in range(2)])
        slots = []
        for q in range(NQ):
            slots.append(dict(
                q=cv('gq%d' % q, [TB]), k=cv('gk%d' % q, [TB]), v=cv('gv%d' % q, [TB]),
                of=cv('gof%d' % q, [2, P]), ob=cv('gob%d' % q, [2, P]),
                gle=cv('gle%d' % q, [P]), ggt=cv('ggt%d' % q, [P]), decs=cv('decs%d' % q, [P]),
                decT=cv('decT%d' % q, [P]),
                N=[cv('N%d_%d' % (i, q), [P]) for i in range(2)], NT=[cv('NT%d_%d' % (i, q), [P]) for i in range(2)],
                X=[cv('X%d_%d' % (i, q), [256]) for i in range(2)],
                khat=cv('khat%d' % q, [P]), wT=cv('wT%d' % q, [P]), attnT=cv('attnT%d' % q, [P]),
                vnew=cv('vnew%d' % q, [P]), tq=cv('tq%d' % q, [P]), bank=self.psb(q)))

        def blk_pre(d, blk, fs):
            t0, nt, isctx = blk
            ns = nt // P
            (bgt, r_bgt), (fac, r_fac), (gcs, r_gcs) = fs['bgt'], fs['fac'], fs['gcs']
            pg.dma('sp', bgt[:, 0:ns, :], self.BG[t0:t0 + nt, :].rearrange("(s p) c -> p s c", p=P),
                   reads=[self.rBG], writes=[r_bgt])
            for s in range(ns):
                gcol = bgt[:, s, 16 + 8 * d:24 + 8 * d]
                pgc, rpgc = self.psb(0)
                pg.op('pe', lambda e, gcol=gcol: e.matmul(pgc[:, 0:8], cst[:, 2 + d, :], gcol, start=True, stop=True),
                      reads=[r_bgt, self.r_cst], writes=[rpgc])
                pg.op('pe', lambda e, gcol=gcol: e.matmul(pgc[:, 8:16], self.ones, gcol, start=True, stop=True),
                      reads=[r_bgt, self.r_cst], writes=[rpgc])
                pg.op('act', lambda e, s=s: e.activation(fac[:, 0, s, :], pgc[:, 0:8], AF.Exp), reads=[rpgc],
                      writes=[r_fac])
                pg.op('act', lambda e, s=s: e.activation(fac[:, 2, s, :], pgc[:, 8:16], AF.Exp), reads=[rpgc],
                      writes=[r_fac])
                pg.op('act', lambda e: e.copy(gcs[:, :], pgc[:, 0:8]), reads=[rpgc], writes=[r_gcs])
                pg.op('dve', lambda e, s=s: e.tensor_tensor(fac[:, 1, s, :], pgc[:, 8:16], gcs[:, :], ALU.subtract),
                      reads=[rpgc, r_gcs], writes=[r_fac])
                pg.op('act', lambda e, s=s: e.activation(fac[:, 1, s, :], fac[:, 1, s, :], AF.Exp), reads=[r_fac],
                      writes=[r_fac])
                pg.op('dve', lambda e, s=s: e.tensor_tensor(fac[:, 3, s, :], fac[:, 0, s, :],
                                                            bgt[:, s, 8 * d:8 * d + 8], ALU.mult),
                      reads=[r_fac, r_bgt], writes=[r_fac])
                pg.op('dve', lambda e, s=s: e.tensor_scalar(fac[:, 4, s, :], bgt[:, s, 8 * d:8 * d + 8], -1.0, None,
                                                            ALU.mult), reads=[r_bgt], writes=[r_fac])

        def chunk(sl, fs, d, h, s):
            (bgt, r_bgt), (fac, r_fac) = fs['bgt'], fs['fac']
            (qT, rq), (kT, rk), (vT, rv) = sl['q'], sl['k'], sl['v']
            (ofb, r_ofb), (ob, rob) = sl['of'], sl['ob']
            (gle, r_gle), (ggt, r_ggt), (decs, r_decs), (decT, r_decT) = sl['gle'], sl['ggt'], sl['decs'], sl['decT']
            Nb, NTb, Xb = sl['N'], sl['NT'], sl['X']
            (khat, r_khat), (wT, r_wT), (attnT, r_attnT) = sl['khat'], sl['wT'], sl['attnT']
            (vnew, r_vnew), (tq, r_tq) = sl['vnew'], sl['tq']
            bk, rbk = sl['bank']
            c0, c1, c2, c3 = bk[:, 0:P], bk[:, P:2 * P], bk[:, 2 * P:3 * P], bk[:, 3 * P:4 * P]
            tk = slice(s * P, (s + 1) * P)
            gcol = bgt[:, s, 16 + 8 * d + h:17 + 8 * d + h]
            bcol = bgt[:, s, 8 * d + h:8 * d + h + 1]
            egc = fac[:, 0, s, h:h + 1]
            ekd = fac[:, 1, s, h:h + 1]
            egl = fac[:, 2, s, h:h + 1]
            bek = fac[:, 3, s, h:h + 1]
            nbeta = fac[:, 4, s, h:h + 1]
            pg.op('dve', lambda e: e.tensor_scalar(gle[:, :], cst[:, 2 + d, :], gcol, None, ALU.mult),
                  reads=[r_bgt, self.r_cst], writes=[r_gle])
            pg.op('pool', lambda e: e.tensor_scalar(ggt[:, :], cst[:, 4 + d, :], gcol, None, ALU.mult),
                  reads=[r_bgt, self.r_cst], writes=[r_ggt])
            yield
            pg.op('pe', lambda e: e.matmul(c0, gle[:, :], cst[:, 4 + d, :], start=True, stop=True),
                  reads=[r_gle, self.r_cst], writes=[rbk])
            pg.op('pe', lambda e: e.matmul(c1, ggt[:, :], cst[:, 2 + d, :], start=True, stop=True),
                  reads=[r_ggt, self.r_cst], writes=[rbk])
            pg.op('pe', lambda e: e.matmul(c2, kT[:, tk], kT[:, tk], start=True, stop=True),
                  reads=[rk], writes=[rbk])
            yield
            pg.op('act', lambda e: e.activation(decs[:, :], c0, AF.Exp), reads=[rbk], writes=[r_decs])
            pg.op('act', lambda e: e.activation(decT[:, :], c1, AF.Exp), reads=[rbk], writes=[r_decT])
            yield
            pg.op('pool', lambda e: e.tensor_tensor(decs[:, :], decs[:, :], cst[:, 4 + d, :], ALU.mult),
                  reads=[r_decs, self.r_cst], writes=[r_decs])
            pg.op('pool', lambda e: e.tensor_tensor(decT[:, :], decT[:, :], cst[:, 2 + d, :], ALU.mult),
                  reads=[r_decT, self.r_cst], writes=[r_decT])
            yield
            (N0, rN0), (NT0, rNT0) = Nb[0], NTb[0]
            pg.op('dve', lambda e: e.scalar_tensor_tensor(N0[:, :], c2, nbeta, decs[:, :], ALU.mult, ALU.mult),
                  reads=[rbk, r_fac, r_decs], writes=[rN0])
            yield
            pg.op('pe', lambda e: e.transpose(c3, N0[:, :], self.ident), reads=[rN0, self.r_cst], writes=[rbk])
            pg.op('pe', lambda e: e.transpose(c0, kT[:, tk], self.ident), reads=[rk, self.r_cst], writes=[rbk])
            pg.op('pe', lambda e: e.transpose(c1, vT[:, tk], self.ident), reads=[rv, self.r_cst], writes=[rbk])
            yield
            (X, rX) = Xb[0]
            pg.op('act', lambda e: e.copy(NT0[:, :], c3), reads=[rbk], writes=[rNT0])
            pg.op('act', lambda e: e.activation(X[:, 0:P], c1, AF.Copy, scale=bcol), reads=[rbk, r_bgt], writes=[rX])
            pg.op('act', lambda e: e.activation(khat[:, :], c0, AF.Copy, scale=ekd), reads=[rbk, r_fac],
                  writes=[r_khat])
            pg.op('act', lambda e: e.activation(X[:, P:2 * P], c0, AF.Copy, scale=bek), reads=[rbk, r_fac],
                  writes=[rX])
            yield
            cur = 0
            for lvl in range(7):
                (Nc, rNc), (NTc, rNTc) = Nb[cur], NTb[cur]
                (Xc, rXc), (Xn, rXn) = Xb[lvl % 2], Xb[(lvl + 1) % 2]
                pg.op('pe', lambda e, NTc=NTc, Xc=Xc: e.matmul(bk[:, 0:256], NTc[:, :], Xc[:, :], start=True, stop=True),
                      reads=[rNTc, rXc], writes=[rbk])
                if lvl < 6:
                    pg.op('pe', lambda e, NTc=NTc, Nc=Nc: e.matmul(c2, NTc[:, :], Nc[:, :], start=True, stop=True),
                          reads=[rNTc, rNc], writes=[rbk])
                    pg.op('pe', lambda e, NTc=NTc, Nc=Nc: e.matmul(c3, Nc[:, :], NTc[:, :], start=True, stop=True),
                          reads=[rNTc, rNc], writes=[rbk])
                yield
                pg.op('dve', lambda e, Xc=Xc, Xn=Xn: e.tensor_tensor(Xn[:, :], Xc[:, :], bk[:, 0:256], ALU.add),
                      reads=[rXc, rbk], writes=[rXn])
                if lvl < 6:
                    (Nn, rNn), (NTn, rNTn) = Nb[1 - cur], NTb[1 - cur]
                    pg.op('act', lambda e, Nn=Nn: e.copy(Nn[:, :], c2), reads=[rbk], writes=[rNn])
                    pg.op('act', lambda e, NTn=NTn: e.copy(NTn[:, :], c3), reads=[rbk], writes=[rNTn])
                    cur = 1 - cur
                yield
            (Xf, rXf) = Xb[7 % 2]
            pg.op('pe', lambda e: e.transpose(c0, Xf[:, P:2 * P], self.ident), reads=[rXf, self.r_cst], writes=[rbk])
            pg.op('pe', lambda e: e.matmul(c1, kT[:, tk], qT[:, tk], start=True, stop=True), reads=[rk, rq],
                  writes=[rbk])
            yield
            pg.op('act', lambda e: e.copy(wT[:, :], c0), reads=[rbk], writes=[r_wT])
            pg.op('dve', lambda e: e.tensor_tensor(attnT[:, :], c1, decT[:, :], ALU.mult), reads=[rbk, r_decT],
                  writes=[r_attnT])
            yield
            pg.op('pe', lambda e: e.matmul(c2, wT[:, :], S[:, h, :], start=True, stop=True), reads=[r_wT, rS],
                  writes=[rbk])
            pg.op('pe', lambda e: e.matmul(c3, qT[:, tk], S[:, h, :], start=True, stop=True), reads=[rq, rS],
                  writes=[rbk])
            yield
            pg.op('dve', lambda e: e.tensor_tensor(vnew[:, :], Xf[:, 0:P], c2, ALU.subtract), reads=[rXf, rbk],
                  writes=[r_vnew])
            pg.op('act', lambda e: e.activation(tq[:, :], c3, AF.Copy, scale=egc), reads=[rbk, r_fac],
                  writes=[r_tq])
            if d == 1:
                pg.op('pool', lambda e: e.tensor_tensor(tq[:, :], tq[:, :], ofb[:, s, :], ALU.add),
                      reads=[r_tq, r_ofb], writes=[r_tq])
            yield
            pg.op('pe', lambda e: e.matmul(c0, attnT[:, :], vnew[:, :], start=True, stop=True),
                  reads=[r_attnT, r_vnew], writes=[rbk])
            pg.op('pe', lambda e: e.matmul(c1, khat[:, :], vnew[:, :], start=True, stop=True),
                  reads=[r_khat, r_vnew], writes=[rbk])
            yield
            pg.op('dve', lambda e: e.tensor_tensor(ob[:, s, :], tq[:, :], c0, ALU.add), reads=[r_tq, rbk],
                  writes=[rob])
            pg.op('dve', lambda e: e.scalar_tensor_tensor(S[:, h, :], S[:, h, :], egl, c1, ALU.mult, ALU.add),
                  reads=[rS, r_fac, rbk], writes=[rS])
            yield

        def block_jobs(d, blk):
            t0, nt, isctx = blk
            ns = nt // P
            st = {'fs': None}

            def job(h):
                def run(sl):
                    if st['fs'] is None:
                        st['fs'] = fsets.next()
                        blk_pre(d, blk, st['fs'])
                    fs = st['fs']
                    (qT, rq), (kT, rk), (vT, rv) = sl['q'], sl['k'], sl['v']
                    (ofb, r_ofb), (ob, rob) = sl['of'], sl['ob']
                    pg.dma('sp', qT[:, 0:nt], self.CQ[h * P:(h + 1) * P, t0:t0 + nt], reads=[self.rCQ], writes=[rq])
                    pg.dma('sp', kT[:, 0:nt], self.CQ[(8 + h) * P:(9 + h) * P, t0:t0 + nt], reads=[self.rCQ],
                           writes=[rk])
                    pg.dma('sp', vT[:, 0:nt], self.CQ[(16 + h) * P:(17 + h) * P, t0:t0 + nt], reads=[self.rCQ],
                           writes=[rv])
                    if d == 1:
                        pg.dma('sp', ofb[:, 0:ns, :],
                               self.Od[t0:t0 + nt, h * P:(h + 1) * P].rearrange("(s p) d -> p s d", p=P),
                               reads=[self.rOd], writes=[r_ofb])
                    yield
                    for s in (range(ns) if d == 0 else range(ns - 1, -1, -1)):
                        yield from chunk(sl, fs, d, h, s)
                    pg.dma('sp', self.Od[t0:t0 + nt, h * P:(h + 1) * P].rearrange("(s p) d -> p s d", p=P),
                           ob[:, 0:ns, :], reads=[rob], writes=[self.rOd])
                return run
            return [job(h) for h in range(H)]

        def pipeline(jobs):
            active = {}
            free = list(range(NQ))
            ji = 0
            while ji < len(jobs) or active:
                if ji < len(jobs) and free:
                    q = free.pop(0)
                    active[q] = jobs[ji](slots[q])
                    ji += 1
                for q in list(active.keys()):
                    try:
                        next(active[q])
                    except StopIteration:
                        del active[q]
                        free.append(q)

        sblocks = [(0, NCTX, 1)] + [(NCTX + TB * k, TB, 0) for k in range(L // TB)]
        for d in range(2):
            pg.op('pool', lambda e: e.memset(S[:, :, :], 0.0), writes=[rS])
            order = list(sblocks) if d == 0 else [sblocks[0]] + list(sblocks[1:][::-1])
            jobs = []
            for blk in order:
                if 'scan' in skip:
                    break
                jobs += block_jobs(d, blk)
            pipeline(jobs)
        pg.barrier()
        self.aoff = save
        self.out_proj(li, 8, 128, self.gdn_norm_g, False, self.gdn_w_out[j], per_head_g=False)

    def hyena(self, li, j):
        pg = self.pg
        L = self.L
        W = self.hy_w_in[j]
        PI = math.pi

        def m1(blk):
            t0, nt, isctx = blk
            self.load_norm(blk, 0)
            for c in range(24):
                pt, rp = self.proj_feat(W, c * P, P, nt, 2 + c % 2)
                tt, rt = self.tmp512.next()
                pg.op('act', lambda e, tt=tt, pt=pt: e.copy(tt[:, 0:nt], pt[:, 0:nt]), reads=[rp], writes=[rt])
                pg.dma('sp', self.CQ[c * P:(c + 1) * P, t0:t0 + nt], tt[:, 0:nt], reads=[rt], writes=[self.rCQ])
        for blk in self.blocks[1:]:
            m1(blk)
        pg.barrier()
        save = self.aoff
        self.aoff = 0
        cin, r_cin = self.carve('hcin', [L + 2])
        acc, r_acc = self.carve('hacc', [L])
        cw, r_cw = self.carve('hcw', [24, 3])
        for k in range(3):
            pg.dma('sp', cw[:, :, k], self.hy_conv_w[j, k:k + 1, :].rearrange("o (c p) -> p (o c)", p=P),
                   writes=[r_cw], allow_slow_non_contiguous=True)
        pg.op('pool', lambda e: e.memset(cin[:, 0:1], 0.0), writes=[r_cin])
        pg.op('pool', lambda e: e.memset(cin[:, L + 1:L + 2], 0.0), writes=[r_cin])

        def conv_row(rc):
            pg.dma('sp', cin[:, 1:1 + L], self.CQ[rc * P:(rc + 1) * P, NCTX:NCTX + L], reads=[self.rCQ],
                   writes=[r_cin])

            def piece(p0, pn):
                pg.op('dve', lambda e: e.tensor_scalar(acc[:, p0:p0 + pn], cin[:, p0:p0 + pn], cw[:, rc, 0:1], None,
                                                       ALU.mult), reads=[r_cin, r_cw], writes=[r_acc])
                for k in range(1, 3):
                    pg.op('dve', lambda e, k=k: e.scalar_tensor_tensor(
                        acc[:, p0:p0 + pn], cin[:, p0 + k:p0 + k + pn], cw[:, rc, k:k + 1], acc[:, p0:p0 + pn],
                        ALU.mult, ALU.add), reads=[r_cin, r_cw, r_acc], writes=[r_acc])
            for p0 in range(0, L, 2048):
                piece(p0, min(2048, L - p0))
            pg.dma('sp', self.CQ[rc * P:(rc + 1) * P, NCTX:NCTX + L], acc[:, 0:L], reads=[r_acc], writes=[self.rCQ])
        for rc in range(24):
            conv_row(rc)

        pg.barrier()
        self.aoff = 0
        hid1, r_hid1 = self.carve('hid1', [L])
        hid2, r_hid2 = self.carve('hid2', [L])
        frow, r_frow = self.carve('frow', [L])
        zt, r_zt = self.carve('zt', [512])
        drow, r_drow = self.carve('drow', [512])
        wnd, r_wnd = self.carve('wnd', [512])
        mw, r_mw = self.carve('mw', [2048 + 64 + 64 + 8 + 8 + 32])
        w3 = mw[:, 0:2048]
        w1 = mw[:, 2048:2112]
        w2 = mw[:, 2112:2176]
        pv = mw[:, 2176:2184]
        dl = mw[:, 2184:2192]
        asum = mw[:, 2192:2224]
        pg.dma('sp', w3[0:64, :], self.hy_ff_w3[j], writes=[r_mw])
        pg.dma('sp', w1[0:33, :], self.hy_ff_w1[j], writes=[r_mw])
        pg.dma('sp', w2[0:64, :], self.hy_ff_w2[j], writes=[r_mw])
        for n, src in enumerate((self.hy_ff_b1, self.hy_ff_b2, self.hy_sin_freq)):
            pg.dma('sp', pv[0:64, n:n + 1], src[j:j + 1, :].rearrange("o p -> p o"), writes=[r_mw],
                   allow_slow_non_contiguous=True)
        pg.dma('sp', dl[:, :], self.hydelta[:, :], writes=[r_mw])
        pg.op('dve', lambda e: e.tensor_scalar(dl[:, :], dl[:, :], -1.0, None, ALU.mult), reads=[r_mw], writes=[r_mw])

        def sin_layer(dst, r_dst, wmat, K, src_fn, bcol, p0):
            pt, rp = self.psb(0)
            rhs, r_rhs = src_fn(p0)
            pg.op('pe', lambda e: e.matmul(pt[0:64, 0:512], wmat[0:K, :], rhs, start=True, stop=True),
                  reads=[r_mw, r_rhs], writes=[rp])
            tt, rt = self.hq.next()
            pg.op('dve', lambda e: e.tensor_scalar(tt[0:64, :], pt[0:64, 0:512], pv[0:64, bcol:bcol + 1],
                                                   pv[0:64, 2:3], ALU.add, ALU.mult), reads=[rp, r_mw], writes=[rt])
            m1, rm1 = self.hq.next()
            m2, rm2 = self.hq.next()
            pg.op('dve', lambda e: e.tensor_scalar(m1[0:64, :], tt[0:64, :], PI, -2.0 * PI, ALU.is_gt, ALU.mult),
                  reads=[rt], writes=[rm1])
            pg.op('dve', lambda e: e.tensor_scalar(m2[0:64, :], tt[0:64, :], -PI, 2.0 * PI, ALU.is_lt, ALU.mult),
                  reads=[rt], writes=[rm2])
            pg.op('dve', lambda e: e.tensor_tensor(tt[0:64, :], tt[0:64, :], m1[0:64, :], ALU.add),
                  reads=[rt, rm1], writes=[rt])
            pg.op('dve', lambda e: e.tensor_tensor(tt[0:64, :], tt[0:64, :], m2[0:64, :], ALU.add),
                  reads=[rt, rm2], writes=[rt])
            pg.op('act', lambda e: e.activation(dst[0:64, p0:p0 + 512], tt[0:64, :], AF.Sin),
                  reads=[rt], writes=[r_dst])

        def zsrc(p0):
            pg.dma('sp', zt[0:33, :], self.hyz[:, p0:p0 + 512], writes=[r_zt])
            return zt[0:33, :], r_zt

        def h1src(p0):
            return hid1[0:64, p0:p0 + 512], r_hid1
        for p0 in range(0, L, 512):
            sin_layer(hid1, r_hid1, w1, 33, zsrc, 0, p0)
        for p0 in range(0, L, 512):
            sin_layer(hid2, r_hid2, w2, 64, h1src, 1, p0)

        def filt_chunk(o, cc):
            def piece(p0, n):
                pt, rp = self.psb(1 + n % 2)
                pg.op('pe', lambda e: e.matmul(pt[:, 0:512], w3[0:64, o * 1024 + cc * P:o * 1024 + (cc + 1) * P],
                                               hid2[0:64, p0:p0 + 512], start=True, stop=True),
                      reads=[r_mw, r_hid2], writes=[rp])
                pg.dma('sp', drow[:, :], self.hydist[0:1, p0:p0 + 512].partition_broadcast(P), writes=[r_drow])
                pg.op('act', lambda e: e.activation(wnd[:, :], drow[:, :], AF.Exp, scale=dl[:, cc:cc + 1]),
                      reads=[r_drow, r_mw], writes=[r_wnd])
                pg.op('dve', lambda e: e.tensor_tensor(frow[:, p0:p0 + 512], pt[:, 0:512], wnd[:, :], ALU.mult),
                      reads=[rp, r_wnd], writes=[r_frow])
                pg.op('act', lambda e: e.activation(wnd[:, :], frow[:, p0:p0 + 512], AF.Abs,
                                                    accum_out=asum[:, n:n + 1]),
                      reads=[r_frow], writes=[r_wnd, r_mw])
            for n, p0 in enumerate(range(0, L, 512)):
                piece(p0, n)
            np_ = L // 512
            pg.op('dve', lambda e: e.reduce_sum(asum[:, 24:25], asum[:, 0:np_], axis=AX.X), reads=[r_mw],
                  writes=[r_mw])
            pg.op('dve', lambda e: e.reciprocal(asum[:, 25:26], asum[:, 24:25]), reads=[r_mw], writes=[r_mw])
            for p0 in range(0, L, 2048):
                pg.op('act', lambda e, p0=p0: e.activation(frow[:, p0:p0 + 2048], frow[:, p0:p0 + 2048], AF.Copy,
                                                           scale=asum[:, 25:26]), reads=[r_frow, r_mw],
                      writes=[r_frow])
            r0 = o * 1024 + cc * P
            pg.dma('sp', self.FT[r0:r0 + P, :], frow[:, 0:L], reads=[r_frow], writes=[self.rFT])
        for o in range(2):
            for cc in range(8):
                filt_chunk(o, cc)

        pg.barrier()
        self.aoff = 0
        tb, r_tb = self.carve('tb', [10, P])
        pg.dma('sp', tb[:, :, :], self.hyt.rearrange("p (a b) -> p a b", a=10), writes=[r_tb])
        skb, r_skb = self.carve('skb', [2048])
        pg.dma('sp', skb[:, :], self.hy_skip[j:j + 1, :].partition_broadcast(P), writes=[r_skb])
        CB = 8
        xin_r = RingL([self.carve('hx%d' % i, [CB, P]) for i in range(3)])
        gin_r = RingL([self.carve('hg%d' % i, [CB, P]) for i in range(3)])
        fin_r = RingL([self.carve('hf%d' % i, [CB, P]) for i in range(3)])
        zo_r = RingL([self.carve('hz%d' % i, [CB, P]) for i in range(3)])
        NQ = 8
        slots = []
        for q in range(NQ):
            slots.append(dict(
                T1=self.carve('T1_%d' % q, [256]), T2=self.carve('T2_%d' % q, [256]),
                B1=self.carve('B1_%d' % q, [256]), B2=self.carve('B2_%d' % q, [256]),
                GR=self.carve('GR_%d' % q, [256]), GI=self.carve('GI_%d' % q, [256]),
                Zb=self.carve('Zb_%d' % q, [256]), Eb=self.carve('Eb_%d' % q, [256]),
                bank=self.psb(q)))
        CmS = tb[0:64, 0:2, :].rearrange("p a b -> p (a b)")
        Cm = tb[:, 0, :]
        Sm = tb[:, 3, :]
        CS = tb[:, 2:4, :].rearrange("p a b -> p (a b)")
        mSC = tb[:, 1:3, :].rearrange("p a b -> p (a b)")
        CN = tb[:, 4, 32:96]
        mSN = tb[:, 5, 32:96]
        TCC = tb[:, 6:8, :].rearrange("p a b -> p (a b)")
        TSS = tb[:, 8:10, :].rearrange("p a b -> p (a b)")

        def fwd(sl, src, r_src):
            (T1, r_T1), (T2, r_T2), (B1, rB1), (B2, rB2) = sl['T1'], sl['T2'], sl['B1'], sl['B2']
            pbk, rpa = sl['bank']
            rpx = rpa
            pa, px = pbk[:, 0:256], pbk[:, 256:512]
            pg.op('pe', lambda e: e.matmul(pa[:, 0:256], src, CmS, start=True, stop=True),
                  reads=[r_src, r_tb], writes=[rpa])
            yield
            pg.op('dve', lambda e: e.tensor_tensor(T1[:, :], pa[:, 0:256], TCC, ALU.mult), reads=[rpa, r_tb],
                  writes=[r_T1])
            pg.op('dve', lambda e: e.tensor_tensor(T2[:, :], pa[:, 0:256], TSS, ALU.mult), reads=[rpa, r_tb],
                  writes=[r_T2])
            yield
            pg.op('pool', lambda e: e.tensor_tensor(B1[:, 0:P], T1[:, 0:P], T2[:, P:2 * P], ALU.add),
                  reads=[r_T1, r_T2], writes=[rB1])
            pg.op('pool', lambda e: e.tensor_tensor(B1[:, P:2 * P], T1[:, P:2 * P], T2[:, 0:P], ALU.subtract),
                  reads=[r_T1, r_T2], writes=[rB1])
            yield
            pg.op('act', lambda e: e.copy(B2[:, 0:P], B1[:, P:2 * P]), reads=[rB1], writes=[rB2])
            pg.op('act', lambda e: e.activation(B2[:, P:2 * P], B1[:, 0:P], AF.Copy, scale=-1.0), reads=[rB1],
                  writes=[rB2])
            yield
            pg.op('pe', lambda e: e.matmul(px[:, 0:256], Cm, B1[:, :], start=True, stop=False),
                  reads=[rB1, r_tb], writes=[rpx])
            pg.op('pe', lambda e: e.matmul(px[:, 0:256], Sm, B2[:, :], start=False, stop=True),
                  reads=[rB2, r_tb], writes=[rpx])
            yield

        def one(sl, o, c, k, xt, rx, gt, rg, ft, rf, zt_, rz):
            (T1, r_T1), (T2, r_T2) = sl['T1'], sl['T2']
            (GR, r_GR), (GI, r_GI), (Zb, r_Zb), (Eb, r_Eb) = sl['GR'], sl['GI'], sl['Zb'], sl['Eb']
            pbk, rpd = sl['bank']
            rpx = rpd
            pd, px = pbk[:, 0:256], pbk[:, 256:512]
            py, rpy = px, rpx
            pgx, rpgx = px, rpx
            yield from fwd(sl, ft[0:64, k, :], rf)
            pg.op('act', lambda e: e.copy(GR[:, :].rearrange("p (a b) -> p a b", a=2),
                                          pgx[:, 0:P].unsqueeze(1).to_broadcast([P, 2, P])), reads=[rpgx],
                  writes=[r_GR])
            pg.op('act', lambda e: e.copy(GI[:, :].rearrange("p (a b) -> p a b", a=2),
                                          pgx[:, P:2 * P].unsqueeze(1).to_broadcast([P, 2, P])), reads=[rpgx],
                  writes=[r_GI])
            yield
            yield from fwd(sl, xt[0:64, k, :], rx)
            pg.op('dve', lambda e: e.tensor_tensor(T1[:, :], px[:, 0:256], GR[:, :], ALU.mult), reads=[rpx, r_GR],
                  writes=[r_T1])
            pg.op('dve', lambda e: e.tensor_tensor(T2[:, :], px[:, 0:256], GI[:, :], ALU.mult), reads=[rpx, r_GI],
                  writes=[r_T2])
            yield
            pg.op('pool', lambda e: e.tensor_tensor(Zb[:, 0:P], T1[:, 0:P], T2[:, P:2 * P], ALU.subtract),
                  reads=[r_T1, r_T2], writes=[r_Zb])
            pg.op('pool', lambda e: e.tensor_tensor(Zb[:, P:2 * P], T2[:, 0:P], T1[:, P:2 * P], ALU.add),
                  reads=[r_T1, r_T2], writes=[r_Zb])
            yield
            pg.op('pe', lambda e: e.matmul(pd[:, 0:256], Zb[:, 0:P], CS, start=True, stop=False),
                  reads=[r_Zb, r_tb], writes=[rpd])
            pg.op('pe', lambda e: e.matmul(pd[:, 0:256], Zb[:, P:2 * P], mSC, start=False, stop=True),
                  reads=[r_Zb, r_tb], writes=[rpd])
            yield
            pg.op('dve', lambda e: e.tensor_tensor(T1[:, :], pd[:, 0:256], TCC, ALU.mult), reads=[rpd, r_tb],
                  writes=[r_T1])
            pg.op('dve', lambda e: e.tensor_tensor(T2[:, :], pd[:, 0:256], TSS, ALU.mult), reads=[rpd, r_tb],
                  writes=[r_T2])
            yield
            pg.op('pool', lambda e: e.tensor_tensor(Eb[:, 0:P], T1[:, 0:P], T2[:, P:2 * P], ALU.subtract),
                  reads=[r_T1, r_T2], writes=[r_Eb])
            pg.op('pool', lambda e: e.tensor_tensor(Eb[:, P:2 * P], T2[:, 0:P], T1[:, P:2 * P], ALU.add),
                  reads=[r_T1, r_T2], writes=[r_Eb])
            yield
            pg.op('pe', lambda e: e.matmul(py[0:64, 0:P], CN, Eb[:, 0:P], start=True, stop=False),
                  reads=[r_Eb, r_tb], writes=[rpy])
            pg.op('pe', lambda e: e.matmul(py[0:64, 0:P], mSN, Eb[:, P:2 * P], start=False, stop=True),
                  reads=[r_Eb, r_tb], writes=[rpy])
            yield
            sk = skb[0:64, o * 1024 + c:o * 1024 + c + 1]
            pg.op('dve', lambda e: e.scalar_tensor_tensor(zt_[0:64, k, :], xt[0:64, k, :], sk, py[0:64, 0:P],
                                                          ALU.mult, ALU.add), reads=[rx, r_skb, rpy], writes=[rz])
            yield
            pg.op('pool', lambda e: e.tensor_tensor(zt_[0:64, k, :], zt_[0:64, k, :], gt[0:64, k, :], ALU.mult),
                  reads=[rz, rg], writes=[rz])
            yield

        def group_jobs(o, c0):
            st = {'loaded': False, 'done': 0}
            bufs = {}

            def load():
                xt, rx = xin_r.next()
                gt, rg = gin_r.next()
                ft, rf = fin_r.next()
                zt_, rz = zo_r.next()
                if o == 0:
                    src, rsrc = self.CQ[c0:c0 + CB, NCTX:NCTX + L], self.rCQ
                    dst, rdst = self.Z1[c0:c0 + CB, :], self.rZ1
                else:
                    src, rsrc = self.Z1[c0:c0 + CB, :], self.rZ1
                    dst, rdst = self.Z2[c0:c0 + CB, :], self.rZ2
                g0 = 1024 * (o + 1) + c0
                pg.dma('sp', xt[0:64, :, :], src.rearrange("c (a b) -> a c b", b=P), reads=[rsrc], writes=[rx])
                pg.dma('sp', gt[0:64, :, :], self.CQ[g0:g0 + CB, NCTX:NCTX + L].rearrange("c (a b) -> a c b", b=P),
                       reads=[self.rCQ], writes=[rg])
                pg.dma('sp', ft[0:64, :, :],
                       self.FT[o * 1024 + c0:o * 1024 + c0 + CB, :].rearrange("c (a b) -> a c b", b=P),
                       reads=[self.rFT], writes=[rf])
                bufs.update(xt=xt, rx=rx, gt=gt, rg=rg, ft=ft, rf=rf, zt_=zt_, rz=rz, dst=dst, rdst=rdst)

            def job(k):
                def run(sl):
                    if not st['loaded']:
                        st['loaded'] = True
                        load()
                    yield from one(sl, o, c0 + k, k, bufs['xt'], bufs['rx'], bufs['gt'], bufs['rg'], bufs['ft'],
                                   bufs['rf'], bufs['zt_'], bufs['rz'])
                    st['done'] += 1
                    if st['done'] == CB:
                        pg.dma('sp', bufs['dst'].rearrange("c (a b) -> a c b", b=P), bufs['zt_'][0:64, :, :],
                               reads=[bufs['rz']], writes=[bufs['rdst']])
                return run
            return [job(k) for k in range(CB)]

        def pipeline(jobs):
            active = {}
            free = list(range(NQ))
            ji = 0
            first = True
            while ji < len(jobs) or active:
                if ji < len(jobs) and free:
                    q = free.pop(0)
                    active[q] = jobs[ji](slots[q])
                    ji += 1
                for q in list(active.keys()):
                    try:
                        next(active[q])
                    except StopIteration:
                        del active[q]
                        free.append(q)
        for o in range(2):
            jobs = []
            for c0 in range(0, 1024, CB):
                jobs += group_jobs(o, c0)
            pipeline(jobs)
        pg.barrier()
        self.aoff = save

        Wout = self.hy_w_out[j]
        onT, r_onT = self.gT, self.r_gT

        def ob_(blk):
            t0, nt, isctx = blk
            ns = nt // P
            pg.dma('sp', self.xin[:, 0:ns, :], self.X[t0:t0 + nt, :].rearrange("(s p) d -> p s d", p=P),
                   reads=[self.rX], writes=[self.r_xin_sb])
            pg.dma('sp', onT[:, 0:8, 0:nt],
                   self.Z2[:, t0 - NCTX:t0 - NCTX + nt].rearrange("(k p) t -> p k t", p=P), reads=[self.rZ2],
                   writes=[r_onT])
            for dg in range(2):
                accs = [self.psb(4 + s) for s in range(ns)]
                wt, rw = self.wring.next()
                pg.dma('sp', wt[:, 0:8, :], Wout[:, dg * 512:(dg + 1) * 512].rearrange("(k p) n -> p k n", p=P),
                       reads=[self.rW], writes=[rw])
                for vc in range(8):
                    for s in range(ns):
                        pt, rp = accs[s]
                        pg.op('pe', lambda e, pt=pt, vc=vc, s=s, wt=wt: e.matmul(
                            pt[:, :], onT[:, vc, s * P:(s + 1) * P], wt[:, vc, :],
                            start=(vc == 0), stop=(vc == 7)), reads=[rw, r_onT], writes=[rp])
                for s in range(ns):
                    pt, rp = accs[s]
                    self.update_x_store(blk, s, dg, pt, rp, 0, last=(dg == 1 and s == ns - 1))
        for blk in self.blocks[1:]:
            ob_(blk)

    def scan_linattn(self, kind, j, H, ndc, DV):
        pg = self.pg
        DK = ndc * P
        L = self.L
        pg.barrier()
        save = self.aoff
        self.aoff = 0
        cv = self.carve
        cst = self.cst
        NQ = 4
        TB = 256
        S, rS = cv('S', [H * ndc, DV])
        lsets = RingL([dict(ldt=cv('ldt%d' % i, [2, 512])) for i in range(2)])
        ldc = [cv('ldc%d' % h, [DK]) for h in range(H)] if kind == 1 else None
        lowaug, r_low = cv('lowaug', [TB])
        w2aug, r_w2 = cv('w2aug', [2, 512])
        tmpe, r_tmpe = cv('tmpe', [512])
        slots = []
        for q in range(NQ):
            slots.append(dict(
                q=cv('lq%d' % q, [ndc, TB]), k=cv('lk%d' % q, [ndc, TB]), v=cv('lv%d' % q, [2, DV]),
                ob=cv('lob%d' % q, [2, DV]),
                cum=cv('cum%d' % q, [ndc, P]), Eq=cv('Eq%d' % q, [ndc, P]), Ek=cv('Ek%d' % q, [ndc, P]),
                Eh=cv('Eh%d' % q, [ndc, P]), ekh=cv('ekh%d' % q, [DK]), refs=cv('refs%d' % q, [2, ndc]),
                qt=cv('qt%d' % q, [ndc, P]), kt=cv('kt%d' % q, [ndc, P]), qh=cv('qh%d' % q, [ndc, P]),
                kh=cv('kh%d' % q, [DK]), scs=cv('scs%d' % q, [P]),
                A=self.psb(2 * q), B=self.psb(2 * q + 1)))
        if kind == 2:
            pg.op('pool', lambda e: e.memset(lowaug[0:32, :], 1.0), writes=[r_low])
            for d_ in range(2):
                pg.dma('sp', w2aug[0:16, d_, :], self.gla_gate_w2[j, d_, :, :], writes=[r_w2])
                pg.dma('sp', w2aug[16:17, d_, :], self.gla_gate_b[j, d_:d_ + 1, :], writes=[r_w2])
        lgam = [math.log1p(-2.0 ** (-5.0 - h)) for h in range(H)]

        def gla_ld(d, blk, ls):
            t0, nt, isctx = blk
            ldt, r_ldt = ls['ldt']
            pg.dma('sp', lowaug[0:16, 0:nt], self.LOWT[d * 16:(d + 1) * 16, t0:t0 + nt],
                   reads=[self.rLOWT], writes=[r_low])
            for s in range(nt // P):
                pt, rp = self.psb(0)
                pg.op('pe', lambda e, s=s, pt=pt: e.matmul(pt[:, :], lowaug[0:17, s * P:(s + 1) * P],
                                                           w2aug[0:17, d, :], start=True, stop=True),
                      reads=[r_low, r_w2], writes=[rp])
                pg.op('act', lambda e, pt=pt: e.activation(tmpe[:, :], pt[:, :], AF.Exp, scale=-1.0),
                      reads=[rp], writes=[r_tmpe])
                pg.op('act', lambda e: e.activation(tmpe[:, :], tmpe[:, :], AF.Ln, bias=1.0),
                      reads=[r_tmpe], writes=[r_tmpe])
                pg.op('dve', lambda e, s=s: e.tensor_scalar(ldt[:, s, :], tmpe[:, :], -1.0 / 16.0, None, ALU.mult),
                      reads=[r_tmpe], writes=[r_ldt])

        def chunk(sl, d, h, s, ld, r_ld):
            TRI = cst[:, 2 + d, :]
            STR = cst[:, 4 + d, :]
            MSK = cst[:, 2 + d, :]
            last = P - 1 if d == 0 else 0
            tk = slice(s * P, (s + 1) * P)
            (qT, rq), (kT, rk), (v, rv), (ob, rob) = sl['q'], sl['k'], sl['v'], sl['ob']
            (cum, r_cum), (Eq, r_Eq), (Ek, r_Ek), (Eh, r_Eh) = sl['cum'], sl['Eq'], sl['Ek'], sl['Eh']
            (ekh, r_ekh), (refs, r_refs) = sl['ekh'], sl['refs']
            (qt_, r_qt), (kt_, r_kt), (qh_, r_qh), (kh_, r_kh), (scs, r_scs) = (sl['qt'], sl['kt'], sl['qh'],
                                                                               sl['kh'], sl['scs'])
            (A, rA), (B, rB) = sl['A'], sl['B']
            for dc in range(ndc):
                pg.op('pe', lambda e, dc=dc: e.matmul(A[:, dc * P:(dc + 1) * P], ld[:, dc * P:(dc + 1) * P], TRI,
                                                      start=True, stop=True), reads=[r_ld, self.r_cst], writes=[rA])
            pg.op('pe', lambda e: e.matmul(B[:, 0:DK], STR, ld, start=True, stop=True),
                  reads=[r_ld, self.r_cst], writes=[rB])
            yield
            pg.op('act', lambda e: e.copy(cum[:, :, :].rearrange("p a b -> p (a b)"), A[:, 0:DK]), reads=[rA],
                  writes=[r_cum])
            pg.op('act', lambda e: e.activation(ekh[:, :], B[:, 0:DK], AF.Exp), reads=[rB], writes=[r_ekh])
            yield
            pg.op('dve', lambda e: e.tensor_copy(refs[:, 0, :], cum[:, :, 64]), reads=[r_cum], writes=[r_refs])
            pg.op('dve', lambda e: e.tensor_scalar(refs[:, 1, :], cum[:, :, 64], -1.0, None, ALU.mult),
                  reads=[r_cum], writes=[r_refs])
            pg.op('pe', lambda e: e.transpose(A[:, 0:P], kT[:, 0, tk], self.ident),
                  reads=[rk, self.r_cst], writes=[rA])
            for dc in range(1, ndc):
                pg.op('pe', lambda e, dc=dc: e.transpose(A[:, dc * P:(dc + 1) * P], kT[:, dc, tk], self.ident),
                      reads=[rk, self.r_cst], writes=[rA])
            yield
            for dc in range(ndc):
                pg.op('act', lambda e, dc=dc: e.activation(Eq[:, dc, :], cum[:, dc, :], AF.Exp,
                                                           bias=refs[:, 1, dc:dc + 1]),
                      reads=[r_cum, r_refs], writes=[r_Eq])
                pg.op('act', lambda e, dc=dc: e.activation(Ek[:, dc, :], cum[:, dc, :], AF.Exp,
                                                           bias=refs[:, 0, dc:dc + 1], scale=-1.0),
                      reads=[r_cum, r_refs], writes=[r_Ek])
            pg.op('act', lambda e: e.activation(Eh[:, :, :], cum[:, :, :], AF.Exp), reads=[r_cum], writes=[r_Eh])
            pg.op('dve', lambda e: e.tensor_tensor(kh_[:, :], A[:, 0:DK], ekh[:, :], ALU.mult),
                  reads=[rA, r_ekh], writes=[r_kh])
            yield
            pg.op('dve', lambda e: e.tensor_tensor(qt_[:, :, :], qT[:, :, tk], Eq[:, :, :], ALU.mult),
                  reads=[rq, r_Eq], writes=[r_qt])
            pg.op('pool', lambda e: e.tensor_tensor(kt_[:, :, :], kT[:, :, tk], Ek[:, :, :], ALU.mult),
                  reads=[rk, r_Ek], writes=[r_kt])
            pg.op('pool', lambda e: e.tensor_tensor(qh_[:, :, :], qT[:, :, tk], Eh[:, :, :], ALU.mult),
                  reads=[rq, r_Eh], writes=[r_qh])
            yield
            for dc in range(ndc):
                pg.op('pe', lambda e, dc=dc: e.matmul(B[:, 0:P], kt_[:, dc, :], qt_[:, dc, :],
                                                      start=(dc == 0), stop=(dc == ndc - 1)),
                      reads=[r_kt, r_qt], writes=[rB])
            yield
            pg.op('dve', lambda e: e.tensor_tensor(scs[:, :], B[:, 0:P], MSK, ALU.mult),
                  reads=[rB, self.r_cst], writes=[r_scs])
            yield
            for dc in range(ndc):
                pg.op('pe', lambda e, dc=dc: e.matmul(A[:, 0:DV], qh_[:, dc, :], S[:, h * ndc + dc, :],
                                                      start=(dc == 0), stop=False),
                      reads=[r_qh, rS], writes=[rA])
            pg.op('pe', lambda e: e.matmul(A[:, 0:DV], scs[:, :], v[:, s, :], start=False, stop=True),
                  reads=[r_scs, rv], writes=[rA])
            pg.op('pe', lambda e: e.matmul(B[:, 0:DV], kh_[:, 0:P], v[:, s, :], start=True, stop=True),
                  reads=[r_kh, rv], writes=[rB])
            yield
            if d == 0:
                pg.op('act', lambda e: e.copy(ob[:, s, :], A[:, 0:DV]), reads=[rA], writes=[rob])
            else:
                pg.op('dve', lambda e: e.tensor_tensor(ob[:, s, :], A[:, 0:DV], ob[:, s, :], ALU.add),
                      reads=[rA, rob], writes=[rob])
            pg.op('dve', lambda e: e.scalar_tensor_tensor(
                S[:, h * ndc, :], S[:, h * ndc, :], Eh[:, 0, last:last + 1], B[:, 0:DV],
                ALU.mult, ALU.add), reads=[rS, r_Eh, rB], writes=[rS])
            yield
            for dc in range(1, ndc):
                pg.op('pe', lambda e, dc=dc: e.matmul(A[:, 0:DV], kh_[:, dc * P:(dc + 1) * P], v[:, s, :],
                                                      start=True, stop=True), reads=[r_kh, rv], writes=[rA])
                yield
                pg.op('dve', lambda e, dc=dc: e.scalar_tensor_tensor(
                    S[:, h * ndc + dc, :], S[:, h * ndc + dc, :], Eh[:, dc, last:last + 1], A[:, 0:DV],
                    ALU.mult, ALU.add), reads=[rS, r_Eh, rA], writes=[rS])
                yield

        def block_jobs(d, blk):
            t0, nt, isctx = blk
            ns = nt // P
            st = {'ls': None}

            def job(h):
                def run(sl):
                    if kind == 2 and st['ls'] is None:
                        st['ls'] = lsets.next()
                        gla_ld(d, blk, st['ls'])
                    (qT, rq), (kT, rk), (v, rv), (ob, rob) = sl['q'], sl['k'], sl['v'], sl['ob']
                    for dc in range(ndc):
                        r0 = h * DK + dc * P
                        pg.dma('sp', qT[:, dc, 0:nt], self.QT[r0:r0 + P, t0:t0 + nt], reads=[self.rQT], writes=[rq])
                        pg.dma('sp', kT[:, dc, 0:nt], self.KT[r0:r0 + P, t0:t0 + nt], reads=[self.rKT], writes=[rk])
                    pg.dma('sp', v[:, 0:ns, :],
                           self.Vd[t0:t0 + nt, h * DV:(h + 1) * DV].rearrange("(s p) d -> p s d", p=P),
                           reads=[self.rVd], writes=[rv])
                    if d == 1:
                        pg.dma('sp', ob[:, 0:ns, :],
                               self.Od[t0:t0 + nt, h * DV:(h + 1) * DV].rearrange("(s p) d -> p s d", p=P),
                               reads=[self.rOd], writes=[rob])
                    yield
                    for s in (range(ns) if d == 0 else range(ns - 1, -1, -1)):
                        if kind == 1:
                            ld, r_ld = ldc[h]
                            ld = ld[:, :]
                        else:
                            ldt, r_ld = st['ls']['ldt']
                            ld = ldt[:, s, h * P:(h + 1) * P]
                        yield from chunk(sl, d, h, s, ld, r_ld)
                    pg.dma('sp', self.Od[t0:t0 + nt, h * DV:(h + 1) * DV].rearrange("(s p) d -> p s d", p=P),
                           ob[:, 0:ns, :], reads=[rob], writes=[self.rOd])
                return run
            return [job(h) for h in range(H)]

        def pipeline(jobs):
            active = {}
            free = list(range(NQ))
            ji = 0
            while ji < len(jobs) or active:
                if ji < len(jobs) and free:
                    q = free.pop(0)
                    active[q] = jobs[ji](slots[q])
                    ji += 1
                for q in list(active.keys()):
                    try:
                        next(active[q])
                    except StopIteration:
                        del active[q]
                        free.append(q)

        sblocks = [(0, NCTX, 1)] + [(NCTX + TB * k, TB, 0) for k in range(L // TB)]
        for d in range(2):
            pg.op('pool', lambda e: e.memset(S[:, :, :], 0.0), writes=[rS])
            if kind == 1:
                for h in range(H):
                    hd = h if d == 0 else H - 1 - h
                    lt, r_lt = ldc[h]
                    pg.op('pool', lambda e, lt=lt, hd=hd: e.memset(lt[:, :], lgam[hd]), writes=[r_lt])
            order = list(sblocks) if d == 0 else [sblocks[0]] + list(sblocks[1:][::-1])
            jobs = []
            for blk in order:
                jobs += block_jobs(d, blk)
            pipeline(jobs)
        pg.barrier()
        self.aoff = save

    def out_proj(self, li, H, DV, ng, center, Wout, per_head_g):
        pg = self.pg
        V = H * DV
        nvc = V // P
        gsrc, r_gsrc = self.tmp512.next()
        ndv = DV // P
        for h in range(H):
            src = ng[0:1, h * DV:(h + 1) * DV] if per_head_g else ng[0:1, 0:DV]
            pg.dma('sp', gsrc[:, h * ndv:(h + 1) * ndv], src.rearrange("o (k p) -> p (o k)", p=P),
                   writes=[r_gsrc], allow_slow_non_contiguous=True)
        gbc, r_gbc = self.gbc, self.r_gbc
        self.bcast_row(gbc, r_gbc, gsrc, r_gsrc, nvc)
        onT, r_onT = self.gT, self.r_gT
        og, r_og = self.hT, self.r_hT
        ogf = og[:, :, :].rearrange("p a b -> p (a b)")
        stat = self.stat
        def ob_(blk):
            t0, nt, isctx = blk
            ns = nt // P
            pg.dma('sp', self.xin[:, 0:ns, :], self.X[t0:t0 + nt, :].rearrange("(s p) d -> p s d", p=P),
                   reads=[self.rX], writes=[self.r_xin_sb])
            for s in range(ns):
                o = ogf[:, 0:V]
                g = ogf[:, 2048:2048 + V]
                pg.dma('sp', o, self.Od[t0 + s * P:t0 + (s + 1) * P, 0:V], reads=[self.rOd], writes=[r_og])
                pg.dma('sp', g, self.Gd[t0 + s * P:t0 + (s + 1) * P, 0:V], reads=[self.rGd], writes=[r_og])
                pg.op('act', lambda e, g=g: e.activation(g, g, AF.Silu), reads=[r_og], writes=[r_og])
                for h in range(H):
                    oh = ogf[:, h * DV:(h + 1) * DV]
                    if center:
                        pg.op('dve', lambda e, oh=oh, h=h: e.reduce_sum(stat[:, 24:25], oh, axis=AX.X),
                              reads=[r_og], writes=[self.r_stat])
                        pg.op('dve', lambda e: e.tensor_scalar(stat[:, 24:25], stat[:, 24:25], -1.0 / DV, None,
                                                               ALU.mult), reads=[self.r_stat], writes=[self.r_stat])
                        pg.op('dve', lambda e, oh=oh: e.tensor_scalar(oh, oh, stat[:, 24:25], None, ALU.add),
                              reads=[r_og, self.r_stat], writes=[r_og])
                    pg.op('act', lambda e, oh=oh, h=h: e.activation(self.junk[:, 0:DV], oh, AF.Square,
                                                                     accum_out=stat[:, h:h + 1]),
                          reads=[r_og], writes=[self.r_junk, self.r_stat])
                self.rstd(stat[:, 16:16 + H], stat[:, 8:8 + H], stat[:, 0:H], 1.0 / DV, self.r_stat)
                for h in range(H):
                    oh = ogf[:, h * DV:(h + 1) * DV]
                    pg.op('dve', lambda e, oh=oh, h=h: e.scalar_tensor_tensor(
                        oh, oh, stat[:, 16 + h:17 + h], gbc[:, h * DV:(h + 1) * DV], ALU.mult, ALU.mult),
                        reads=[r_og, self.r_stat, r_gbc], writes=[r_og])
                pg.op('pool', lambda e, o=o, g=g: e.tensor_tensor(o, o, g, ALU.mult), reads=[r_og], writes=[r_og])
                for c0 in range(0, nvc, 4):
                    pt, rp = self.psb((c0 // 4) % 2)
                    for c in range(c0, c0 + 4):
                        pg.op('pe', lambda e, c=c, c0=c0, pt=pt: e.transpose(
                            pt[:, (c - c0) * P:(c - c0 + 1) * P], ogf[:, c * P:(c + 1) * P], self.ident),
                            reads=[r_og, self.r_cst], writes=[rp])
                    pg.op('act', lambda e, c0=c0, pt=pt, s=s: e.copy(
                        onT[:, c0:c0 + 4, s * P:(s + 1) * P], pt[:, :].rearrange("p (a b) -> p a b", a=4)),
                        reads=[rp], writes=[r_onT])
            for dg in range(2):
                accs = [self.psb(4 + s) for s in range(ns)]
                for v0 in range(0, nvc, 8):
                    nf = min(8, nvc - v0)
                    wt, rw = self.wring.next()
                    pg.dma('sp', wt[:, 0:nf, :],
                           Wout[v0 * P:(v0 + nf) * P, dg * 512:(dg + 1) * 512].rearrange("(k p) n -> p k n", p=P),
                           reads=[self.rW], writes=[rw])
                    for ff in range(nf):
                        vc = v0 + ff
                        for s in range(ns):
                            pt, rp = accs[s]
                            pg.op('pe', lambda e, pt=pt, vc=vc, ff=ff, s=s, wt=wt: e.matmul(
                                pt[:, :], onT[:, vc, s * P:(s + 1) * P], wt[:, ff, :],
                                start=(vc == 0), stop=(vc == nvc - 1)), reads=[rw, r_onT], writes=[rp])
                for s in range(ns):
                    pt, rp = accs[s]
                    self.update_x_store(blk, s, dg, pt, rp, 0, last=(dg == 1 and s == ns - 1))
        for blk in self.blocks:
            ob_(blk)


def rot_tables(L):
    T = L + NCTX
    n = np.arange(L)
    row = (n // 64).astype(np.float64)
    col = (n % 64).astype(np.float64)
    nf = 64
    inv = 10000.0 ** (-np.arange(nf, dtype=np.float64) / nf)
    ang = np.concatenate([row[:, None] * inv, col[:, None] * inv], axis=-1)
    ang = np.concatenate([row[:, None].astype(np.float32) * inv.astype(np.float32),
                          col[:, None].astype(np.float32) * inv.astype(np.float32)], axis=-1).astype(np.float64)
    t = np.zeros((2, P, T), np.float32)
    t[0, :, :NCTX] = 1.0
    t[0, :, NCTX:] = np.cos(ang).T
    t[1, :, NCTX:] = np.sin(ang).T
    return t


def hyena_tables(L):
    N = 2 * L
    n = np.arange(P, dtype=np.float64)
    ang = 2.0 * np.pi * np.outer(n, n) / P
    C, S = np.cos(ang), np.sin(ang)
    tw = 2.0 * np.pi * np.outer(n, n) / N
    TC, TS = np.cos(tw), np.sin(tw)
    t = np.zeros((P, 10, P), np.float64)
    t[:, 0] = C
    t[:, 1] = -S
    t[:, 2] = C
    t[:, 3] = S
    t[:, 4] = C / N
    t[:, 5] = -S / N
    t[:, 6] = TC
    t[:, 7] = TC
    t[:, 8] = TS
    t[:, 9] = TS
    f32 = np.float32
    pos = np.arange(L, dtype=f32)
    tt = pos / f32(L - 1)
    bands = 16
    freqs = np.linspace(1e-4, bands - 1, bands, dtype=f32)
    a = (f32(2.0 * math.pi / L) * pos[:, None] * freqs[None, :]).astype(np.float64)
    z = np.concatenate([tt[:, None].astype(np.float64), np.cos(a), -np.sin(a)], axis=-1)
    dist = (np.abs(pos - L // 2) / f32(L // 2)).astype(f32)
    deltas = np.abs(np.linspace(math.log(1e-2) / 1.5, math.log(1e-2) / 0.3, D, dtype=f32))
    return dict(hyt=t.reshape(P, 10 * P).astype(f32), hyz=np.ascontiguousarray(z.T).astype(f32),
                hydist=dist.reshape(1, L), hydelta=np.ascontiguousarray(deltas.reshape(8, P).T).astype(f32))


def host_inputs(inp, L, layer_ids):
    f = np.float32
    li = list(layer_ids)
    d = {}
    d['consts'] = host_consts()
    d['ada_w'] = np.ascontiguousarray(inp['ada_w'][li], f)
    d['ada_b'] = np.ascontiguousarray(inp['ada_b'][li], f)
    d['norm1_g'] = np.ascontiguousarray(inp['norm1_g'][li], f)
    d['norm2_g'] = np.ascontiguousarray(inp['norm2_g'][li], f)
    d['ffn_w1'] = np.ascontiguousarray(inp['ffn_w1'][li], f)
    d['ffn_w3'] = np.ascontiguousarray(inp['ffn_w3'][li], f)
    d['ffn_w2'] = np.ascontiguousarray(inp['ffn_w2'][li], f)
    d['final_norm_g'] = np.ascontiguousarray(inp['final_norm_g'], f).reshape(1, D)
    d['c_ctx'] = np.ascontiguousarray(inp['c_ctx'], f).reshape(1, D)
    d['rot'] = rot_tables(L)
    for k in ('ret_w_in', 'ret_w_out', 'gla_w_in', 'gla_gate_w2', 'gla_gate_b', 'gla_norm_g', 'gla_w_out'):
        d[k] = np.ascontiguousarray(inp[k], f)
    d['ret_norm_g'] = np.ascontiguousarray(inp['ret_norm_g'], f).reshape(1, 2048)
    for k in ('gdn_w_in', 'gdn_conv_w', 'gdn_norm_g', 'gdn_w_out'):
        d[k] = np.ascontiguousarray(inp[k], f)
    for k in ('hy_w_in', 'hy_conv_w', 'hy_ff_w1', 'hy_ff_b1', 'hy_ff_w2', 'hy_ff_b2', 'hy_ff_w3', 'hy_sin_freq', 'hy_w_out'):
        d[k] = np.ascontiguousarray(inp[k], f)
    d['hy_skip'] = np.ascontiguousarray(inp['hy_skip'], f).reshape(1, 2048)
    d.update(hyena_tables(L))
    d['gdn_a_log'] = np.ascontiguousarray(inp['gdn_a_log'], f).reshape(1, 16)
    d['gdn_dt_bias'] = np.ascontiguousarray(inp['gdn_dt_bias'], f).reshape(1, 16)
    return d


def core_inputs(full, inp, b, L):
    d = dict(full)
    d['x'] = np.ascontiguousarray(inp['x'][b, :L], np.float32)
    d['ctx'] = np.ascontiguousarray(inp['ctx'][b], np.float32)
    d['c'] = np.ascontiguousarray(inp['c'][b:b + 1], np.float32)
    return d


def kernel(**inputs):
    inp = {k: np.asarray(v) for k, v in inputs.items()}
    L = inp['x'].shape[1]
    B = inp['x'].shape[0]
    layers = [(i % 4, i // 4) for i in range(inp['ada_w'].shape[0])]
    mk = MK(L, layers)
    nc = mk.build()
    full = host_inputs(inp, L, range(len(layers)))
    maps = []
    for b in range(B):
        d = core_inputs(full, inp, b, L)
        maps.append({k: v for k, v in d.items() if k in mk.inputs})
    res = run_bass_kernel_spmd(nc, maps, core_ids=list(range(B)))
    return np.stack([np.asarray(r['out'], np.float32) for r in res.results], axis=0)
```

```python
import math
import numpy as np
from contextlib import ExitStack
import concourse.bass as bass
import concourse.mybir as mybir
from concourse.bass_utils import run_bass_kernel_spmd

F32 = mybir.dt.float32
ALU = mybir.AluOpType
AF = mybir.ActivationFunctionType
AX = mybir.AxisListType
P = 128
D = 1024
DFF = 2816
EPS = 1e-6
NCTX = 256
NDS = 24
SAME_WAIT = {'pe': False, 'dve': True, 'act': True, 'pool': True, 'sp': False}


class Res:
    __slots__ = ('name', 'w', 'r', 'excl')

    def __init__(self, name, excl=False):
        self.name = name
        self.w = {}
        self.r = {}
        self.excl = excl


class Prog:
    ENG = ['pe', 'dve', 'act', 'pool', 'sp']

    def __init__(self, nc, st):
        self.nc = nc
        self.st = st
        self.ops = {e: [] for e in self.ENG}
        self.sem = {e: st.enter_context(nc.semaphore('s_' + e)) for e in self.ENG}
        self.cnt = {e: 0 for e in self.ENG}
        self.known = {e: {} for e in self.ENG}
        self.dsem = [st.enter_context(nc.semaphore('d%d' % i)) for i in range(NDS)]
        self.dcnt = [0] * NDS
        self.dnext = 0
        self.nalloc = 0

    def sb(self, name, shape):
        t = self.st.enter_context(self.nc.sbuf_tensor(name, list(shape), F32))
        n = 1
        for s in shape[1:]:
            n *= s
        self.nalloc += n * 4
        return t, Res(name)

    def dram(self, name, shape):
        return self.nc.dram_tensor(name, list(shape), F32).ap(), Res(name)

    def _need(self, eng, reads, writes):
        waits = {}
        for r in reads:
            for dct in ((r.w, r.r) if r.excl else (r.w,)):
                for k, v in dct.items():
                    if waits.get(k, 0) < v:
                        waits[k] = v
        for r in writes:
            for dct in (r.w, r.r):
                for k, v in dct.items():
                    if waits.get(k, 0) < v:
                        waits[k] = v
        need = []
        kn = self.known[eng]
        for k, v in waits.items():
            if k == eng and not SAME_WAIT[eng]:
                continue
            if kn.get(k, 0) >= v:
                continue
            kn[k] = v
            need.append((k, v))
        return need

    def op(self, eng, fn, reads=(), writes=()):
        need = self._need(eng, reads, writes)
        self.cnt[eng] += 1
        v = self.cnt[eng]
        for r in reads:
            r.r[eng] = v
        for r in writes:
            r.w[eng] = v
        self.ops[eng].append((need, fn, None))

    def dma(self, eng, out, in_, reads=(), writes=(), **kw):
        i = self.dnext
        self.dnext = (i + 1) % NDS
        prev = self.dcnt[i]
        self.dcnt[i] += 16
        val = self.dcnt[i]
        key = ('d', i)
        need = self._need(eng, reads, writes)
        if prev > 0 and self.known[eng].get(key, 0) < prev:
            self.known[eng][key] = prev
            need.append((key, prev))
        for r in reads:
            r.r[key] = val
        for r in writes:
            r.w[key] = val
        self.ops[eng].append((need, lambda e: e.dma_start(out=out, in_=in_, **kw), i))

    def barrier(self):
        cur = {e: self.cnt[e] for e in self.ENG if self.cnt[e] > 0}
        for i in range(NDS):
            if self.dcnt[i] > 0:
                cur[('d', i)] = self.dcnt[i]
        for e in self.ENG:
            need = []
            for k, v in cur.items():
                if k == e or self.known[e].get(k, 0) >= v:
                    continue
                self.known[e][k] = v
                need.append((k, v))
            self.ops[e].append((need, None, None))

    def _semobj(self, k):
        return self.dsem[k[1]] if isinstance(k, tuple) else self.sem[k]

    def emit(self):
        nc = self.nc
        fin = [(('d', i), self.dcnt[i]) for i in range(NDS) if self.dcnt[i] > 0]
        fin += [(e, self.cnt[e]) for e in self.ENG if self.cnt[e] > 0 and e != 'sp']
        with nc.Block() as block:
            def run(eng, e, final=False):
                for need, fn, di in self.ops[eng]:
                    for k, v in need:
                        e.wait_ge(self._semobj(k), v)
                    if fn is None:
                        continue
                    ins = fn(e)
                    if di is None:
                        ins.then_inc(self.sem[eng], 1)
                    else:
                        ins.then_inc(self.dsem[di], 16)
                if final:
                    for k, v in fin:
                        e.wait_ge(self._semobj(k), v)

            @block.tensor
            def _(e):
                run('pe', e)

            @block.vector
            def _(e):
                run('dve', e)

            @block.scalar
            def _(e):
                run('act', e)

            @block.gpsimd
            def _(e):
                run('pool', e)

            @block.sync
            def _(e):
                run('sp', e, final=True)


def run_rr(gens):
    gens = list(gens)
    while gens:
        nxt = []
        for g in gens:
            try:
                next(g)
                nxt.append(g)
            except StopIteration:
                pass
        gens = nxt


class RingL:
    def __init__(self, items):
        self.items = items
        self.i = 0

    def next(self):
        it = self.items[self.i]
        self.i = (self.i + 1) % len(self.items)
        return it


class Ring:
    def __init__(self, pg, name, shape, n):
        self.items = [pg.sb('%s%d' % (name, i), shape) for i in range(n)]
        self.i = 0

    def next(self):
        it = self.items[self.i]
        self.i = (self.i + 1) % len(self.items)
        return it


def host_consts():
    i = np.arange(P)
    c = np.zeros((P, 8, P), np.float32)
    c[:, 0] = np.eye(P)
    c[:, 1] = 1.0
    c[:, 2] = (i[:, None] <= i[None, :])
    c[:, 3] = (i[:, None] >= i[None, :])
    c[:, 4] = (i[:, None] > i[None, :])
    c[:, 5] = (i[:, None] < i[None, :])
    return c.reshape(P, 8 * P)


class MK:
    def __init__(self, L, layers, dbg=()):
        self.L = L
        self.T = L + NCTX
        self.layers = layers
        self.dbg = dbg
        self.nc = bass.Bass("TRN2", target_bir_lowering=False)
        self.inputs = {}

    def inp(self, name, shape):
        t = self.nc.dram_tensor(name, list(shape), F32, kind="ExternalInput").ap()
        self.inputs[name] = t
        return t, Res(name)

    def build(self):
        nc = self.nc
        with ExitStack() as st:
            self.pg = pg = Prog(nc, st)
            T, L = self.T, self.L
            nl = len(self.layers)
            self.x_in, self.r_xin = self.inp('x', [L, D])
            self.ctx_in, self.r_ctxin = self.inp('ctx', [NCTX, D])
            self.c_in, _ = self.inp('c', [1, D])
            self.cc_in, _ = self.inp('c_ctx', [1, D])
            self.consts_in, _ = self.inp('consts', [P, 8 * P])
            self.ada_w, _ = self.inp('ada_w', [nl, D, 6 * D])
            self.ada_b, _ = self.inp('ada_b', [nl, 6 * D])
            self.n1g, _ = self.inp('norm1_g', [nl, D])
            self.n2g, _ = self.inp('norm2_g', [nl, D])
            self.w1, _ = self.inp('ffn_w1', [nl, D, DFF])
            self.w3, _ = self.inp('ffn_w3', [nl, D, DFF])
            self.w2, _ = self.inp('ffn_w2', [nl, DFF, D])
            self.fng, _ = self.inp('final_norm_g', [1, D])
            self.rW = Res('weights')
            self.out = nc.dram_tensor('out', [L, D], F32, kind="ExternalOutput").ap()
            self.r_out = Res('out')
            self.X, self.rX = pg.dram('X', [T, D])
            self.mixer_inputs()

            self.cst, self.r_cst = pg.sb('cst', [P, 8, P])
            pg.dma('sp', self.cst[:, :, :], self.consts_in.rearrange("p (k n) -> p k n", k=8),
                   writes=[self.r_cst])
            self.ident = self.cst[:, 0, :]
            self.ones = self.cst[:, 1, :]
            self.ps = []
            for i in range(8):
                t = st.enter_context(nc.psum_tensor('ps%d' % i, [P, 512], F32))
                self.ps.append((t, Res('ps%d' % i, excl=True)))
            self.NA = 34304
            self.arena = st.enter_context(nc.sbuf_tensor('arena', [P, self.NA], F32))
            pg.nalloc += self.NA * 4
            self.aoff = 0
            self.xin, self.r_xin_sb = self.carve('xin', [4, D])
            self.hT, self.r_hT = self.carve('hT', [8, 512])
            self.junk, self.r_junk = self.carve('junk', [D])
            self.wring = RingL([self.carve('wr%d' % i, [8, 512]) for i in range(3)])
            self.gT, self.r_gT = self.carve('gT', [22, 512])
            self.tmp512 = RingL([self.carve('tmp%d' % i, [512]) for i in range(3)])
            self.stat, self.r_stat = pg.sb('stat', [P, 40])
            self.hq = Ring(pg, 'hq', [P, 512], 4)
            self.gbc, self.r_gbc = pg.sb('gbc', [P, 2048])
            self.modT, self.r_modT = pg.sb('modT', [P, 48, 2])
            self.sc, self.r_sc = pg.sb('sc', [P, 8, 2])
            self.AB, self.r_AB = pg.sb('AB', [P, 4, 8, 2])
            self.gvec, self.r_gvec = pg.sb('gvec', [P, 3, 8])
            self.gtb, self.r_gtb = pg.sb('gtb', [P, 4, D])
            self.diag, self.r_diag = pg.sb('diag', [P, P])
            self.blocks = [(0, NCTX, 1)] + [(NCTX + 512 * k, 512, 0) for k in range(L // 512)]

            pg.dma('sp', self.X[0:NCTX, :], self.ctx_in[:, :], writes=[self.rX])
            pg.dma('sp', self.X[NCTX:T, :], self.x_in[:, :], writes=[self.rX])
            self.silu_c()
            for li, (kind, j) in enumerate(self.layers):
                self.layer_mod(li)
                self.mixer(li, kind, j)
                self.ffn(li)
            self.final_norm()
            pg.emit()
        return nc

    def carve(self, name, shape):
        n = 1
        for v in shape:
            n *= v
        assert self.aoff + n <= self.NA, (name, self.aoff, n)
        ap = self.arena[:, self.aoff:self.aoff + n]
        self.aoff += n
        if len(shape) == 2:
            ap = ap.rearrange("p (a b) -> p a b", a=shape[0])
        elif len(shape) == 3:
            ap = ap.rearrange("p (a b c) -> p a b c", a=shape[0], b=shape[1])
        return ap, Res(name)

    def psb(self, i):
        return self.ps[i]

    def rstd(self, out, tmp, ss, scale, res):
        pg = self.pg
        pg.op('dve', lambda e: e.tensor_scalar(tmp, ss, scale, EPS, ALU.mult, ALU.add), reads=[res], writes=[res])
        pg.op('act', lambda e: e.activation(tmp, tmp, AF.Sqrt), reads=[res], writes=[res])
        pg.op('dve', lambda e: e.reciprocal(out, tmp), reads=[res], writes=[res])

    def silu_c(self):
        pg = self.pg
        pg.dma('sp', self.sc[:, :, 0], self.c_in.rearrange("o (k p) -> p (o k)", p=P),
               writes=[self.r_sc], allow_slow_non_contiguous=True)
        pg.dma('sp', self.sc[:, :, 1], self.cc_in.rearrange("o (k p) -> p (o k)", p=P),
               writes=[self.r_sc], allow_slow_non_contiguous=True)
        pg.op('act', lambda e: e.activation(self.sc[:, :, :], self.sc[:, :, :], AF.Silu),
              reads=[self.r_sc], writes=[self.r_sc])

    def bcast_row(self, dst, r_dst, srcT, r_src, nchunk):
        pg = self.pg
        for c0 in range(0, nchunk, 4):
            pt, rp = self.psb(7)
            for c in range(c0, min(nchunk, c0 + 4)):
                pg.op('dve', lambda e, c=c: e.tensor_scalar(self.diag[:, :], self.ident, srcT[:, c:c + 1], None,
                                                            ALU.mult),
                      reads=[r_src, self.r_cst], writes=[self.r_diag])
                pg.op('pe', lambda e, c=c, pt=pt, c0=c0: e.matmul(pt[:, (c - c0) * P:(c - c0 + 1) * P], self.ones,
                                                                   self.diag[:, :], start=True, stop=True),
                      reads=[self.r_diag, self.r_cst], writes=[rp])
            n = min(nchunk, c0 + 4) - c0
            pg.op('act', lambda e, pt=pt, c0=c0, n=n: e.copy(dst[:, c0 * P:(c0 + n) * P], pt[:, 0:n * P]),
                  reads=[rp], writes=[r_dst])

    def layer_mod(self, li):
        pg = self.pg
        modT, sc = self.modT, self.sc
        badd, r_badd = self.tmp512.next()
        pg.dma('sp', badd[:, 0:48], self.ada_b[li:li + 1, :].rearrange("o (k p) -> p (o k)", p=P),
               writes=[r_badd], allow_slow_non_contiguous=True)
        pg.dma('sp', self.gvec[:, 0, :], self.n1g[li:li + 1, :].rearrange("o (k p) -> p (o k)", p=P),
               writes=[self.r_gvec], allow_slow_non_contiguous=True)
        pg.dma('sp', self.gvec[:, 1, :], self.n2g[li:li + 1, :].rearrange("o (k p) -> p (o k)", p=P),
               writes=[self.r_gvec], allow_slow_non_contiguous=True)
        for og in range(12):
            wt, rw = self.wring.next()
            pg.dma('sp', wt[:, :, :], self.ada_w[li, :, og * 512:(og + 1) * 512].rearrange("(k p) n -> p k n", p=P),
                   reads=[self.rW], writes=[rw])
            pt, rp = self.psb(6)
            for oc in range(4):
                for kc in range(8):
                    pg.op('pe', lambda e, wt=wt, pt=pt, oc=oc, kc=kc: e.matmul(
                        pt[:, oc * 2:oc * 2 + 2], wt[:, kc, oc * P:(oc + 1) * P], sc[:, kc, :],
                        start=(kc == 0), stop=(kc == 7)), reads=[rw, self.r_sc], writes=[rp])
            for oc in range(4):
                o = og * 4 + oc
                pg.op('dve', lambda e, pt=pt, oc=oc, o=o: e.tensor_scalar(
                    modT[:, o, :], pt[:, oc * 2:oc * 2 + 2], badd[:, o:o + 1], None, ALU.add),
                    reads=[rp, r_badd], writes=[self.r_modT])
        AB = self.AB
        for n, (gsel, so, sho) in enumerate([(0, 8, 0), (1, 32, 24)]):
            pg.op('dve', lambda e, n=n, so=so, gsel=gsel: e.scalar_tensor_tensor(
                AB[:, 2 * n, :, :], modT[:, so:so + 8, :], 1.0,
                self.gvec[:, gsel, :].unsqueeze(2).to_broadcast([P, 8, 2]), ALU.add, ALU.mult),
                reads=[self.r_modT, self.r_gvec], writes=[self.r_AB])
            pg.op('dve', lambda e, n=n, sho=sho: e.tensor_copy(AB[:, 2 * n + 1, :, :], modT[:, sho:sho + 8, :]),
                  reads=[self.r_modT], writes=[self.r_AB])
        gsrc, r_gsrc = self.tmp512.next()
        for n, go in enumerate([16, 40]):
            for j in range(2):
                idx = n * 2 + j
                pg.op('dve', lambda e, idx=idx, go=go, j=j: e.tensor_copy(gsrc[:, idx * 8:(idx + 1) * 8],
                                                                         modT[:, go:go + 8, j]),
                      reads=[self.r_modT], writes=[r_gsrc])
        for idx in range(4):
            self.bcast_row(self.gtb[:, idx, :], self.r_gtb, gsrc[:, idx * 8:(idx + 1) * 8], r_gsrc, 8)

    def load_norm(self, blk, which, g_final=None):
        pg = self.pg
        t0, nt, isctx = blk
        ns = nt // P
        xin, hT, stat = self.xin, self.hT, self.stat
        pg.dma('sp', xin[:, 0:ns, :], self.X[t0:t0 + nt, :].rearrange("(s p) d -> p s d", p=P),
               reads=[self.rX], writes=[self.r_xin_sb])
        for s in range(ns):
            pg.op('act', lambda e, s=s: e.activation(self.junk[:, :], xin[:, s, :], AF.Square,
                                                     accum_out=stat[:, s:s + 1]),
                  reads=[self.r_xin_sb], writes=[self.r_junk, self.r_stat])
        self.rstd(stat[:, 8:8 + ns], stat[:, 4:4 + ns], stat[:, 0:ns], 1.0 / D, self.r_stat)
        xn, r_xn = self.gT, self.r_gT
        for s in range(ns):
            pg.op('act', lambda e, s=s: e.activation(xn[:, s * 2:(s + 1) * 2, :].rearrange("p a b -> p (a b)"),
                                                     xin[:, s, :], AF.Copy, scale=stat[:, 8 + s:9 + s]),
                  reads=[self.r_xin_sb, self.r_stat], writes=[r_xn])
        for kc in range(8):
            pt, rp = self.psb(kc % 2)
            for s in range(ns):
                pg.op('pe', lambda e, s=s, kc=kc, pt=pt: e.transpose(
                    pt[:, s * P:(s + 1) * P],
                    xn[:, s * 2 + kc // 4, (kc % 4) * P:(kc % 4 + 1) * P], self.ident),
                    reads=[r_xn, self.r_cst], writes=[rp])
            if g_final is None:
                A = self.AB[:, 2 * which, kc, isctx:isctx + 1]
                B = self.AB[:, 2 * which + 1, kc, isctx:isctx + 1]
                pg.op('dve', lambda e, kc=kc, pt=pt, A=A, B=B: e.tensor_scalar(
                    hT[:, kc, 0:nt], pt[:, 0:nt], A, B, ALU.mult, ALU.add),
                    reads=[rp, self.r_AB], writes=[self.r_hT])
            else:
                pg.op('dve', lambda e, kc=kc, pt=pt: e.tensor_scalar(
                    hT[:, kc, 0:nt], pt[:, 0:nt], g_final[:, kc:kc + 1], None, ALU.mult),
                    reads=[rp, self.r_gvec], writes=[self.r_hT])

    def update_x_store(self, blk, s, dg, pt, rp, gsel, last):
        pg = self.pg
        t0, nt, isctx = blk
        tt, rt = self.tmp512.next()
        grow = self.gtb[:, gsel * 2 + isctx, dg * 512:(dg + 1) * 512]
        pg.op('dve', lambda e: e.tensor_tensor(tt[:, :], pt[:, :], grow, ALU.mult),
              reads=[rp, self.r_gtb], writes=[rt])
        pg.op('pool', lambda e: e.tensor_tensor(self.xin[:, s, dg * 512:(dg + 1) * 512],
                                                self.xin[:, s, dg * 512:(dg + 1) * 512], tt[:, :], ALU.add),
              reads=[rt, self.r_xin_sb], writes=[self.r_xin_sb])
        if last:
            ns = nt // P
            pg.dma('sp', self.X[t0:t0 + nt, :].rearrange("(s p) d -> p s d", p=P), self.xin[:, 0:ns, :],
                   reads=[self.r_xin_sb], writes=[self.rX])

    def ffn(self, li):
        for blk in self.blocks:
            self.ffn_blk(li, blk)

    def ffn_blk(self, li, blk):
        pg = self.pg
        t0, nt, isctx = blk
        ns = nt // P
        self.load_norm(blk, 1)
        hT, gT = self.hT, self.gT
        for fc in range(22):
            w1t, rw1 = self.wring.next()
            pg.dma('sp', w1t[:, :, 0:P], self.w1[li, :, fc * P:(fc + 1) * P].rearrange("(k p) n -> p k n", p=P),
                   reads=[self.rW], writes=[rw1])
            pg.dma('sp', w1t[:, :, P:2 * P], self.w3[li, :, fc * P:(fc + 1) * P].rearrange("(k p) n -> p k n", p=P),
                   reads=[self.rW], writes=[rw1])
            p1, rp1 = self.psb(2)
            p3, rp3 = self.psb(3)
            for kc in range(8):
                pg.op('pe', lambda e, kc=kc, w1t=w1t, p1=p1: e.matmul(
                    p1[:, 0:nt], w1t[:, kc, 0:P], hT[:, kc, 0:nt], start=(kc == 0), stop=(kc == 7)),
                    reads=[rw1, self.r_hT], writes=[rp1])
            for kc in range(8):
                pg.op('pe', lambda e, kc=kc, w1t=w1t, p3=p3: e.matmul(
                    p3[:, 0:nt], w1t[:, kc, P:2 * P], hT[:, kc, 0:nt], start=(kc == 0), stop=(kc == 7)),
                    reads=[rw1, self.r_hT], writes=[rp3])
            tt, rt = self.tmp512.next()
            pg.op('act', lambda e, tt=tt, p1=p1: e.activation(tt[:, 0:nt], p1[:, 0:nt], AF.Silu),
                  reads=[rp1], writes=[rt])
            pg.op('dve', lambda e, tt=tt, p3=p3, fc=fc: e.tensor_tensor(gT[:, fc, 0:nt], tt[:, 0:nt],
                                                                       p3[:, 0:nt], ALU.mult),
                  reads=[rt, rp3], writes=[self.r_gT])
        for dg in range(2):
            accs = [self.psb(4 + s) for s in range(ns)]
            for f0 in range(0, 22, 8):
                nf = min(8, 22 - f0)
                w2t, rw2 = self.wring.next()
                pg.dma('sp', w2t[:, 0:nf, :],
                       self.w2[li, f0 * P:(f0 + nf) * P, dg * 512:(dg + 1) * 512].rearrange("(k p) n -> p k n", p=P),
                       reads=[self.rW], writes=[rw2])
                for ff in range(nf):
                    fc = f0 + ff
                    for s in range(ns):
                        pt, rp = accs[s]
                        pg.op('pe', lambda e, pt=pt, fc=fc, ff=ff, s=s, w2t=w2t: e.matmul(
                            pt[:, :], gT[:, fc, s * P:(s + 1) * P], w2t[:, ff, :],
                            start=(fc == 0), stop=(fc == 21)), reads=[rw2, self.r_gT], writes=[rp])
            for s in range(ns):
                pt, rp = accs[s]
                self.update_x_store(blk, s, dg, pt, rp, 1, last=(dg == 1 and s == ns - 1))

    def final_norm(self):
        pg = self.pg
        pg.dma('sp', self.gvec[:, 2, :], self.fng.rearrange("o (k p) -> p (o k)", p=P),
               writes=[self.r_gvec], allow_slow_non_contiguous=True)
        grow, r_grow = self.gtb[:, 0, :], self.r_gtb
        self.bcast_row(grow, r_grow, self.gvec[:, 2, :], self.r_gvec, 8)
        for blk in self.blocks[1:]:
            t0, nt, isctx = blk
            ns = nt // P
            xin, stat = self.xin, self.stat
            pg.dma('sp', xin[:, 0:ns, :], self.X[t0:t0 + nt, :].rearrange("(s p) d -> p s d", p=P),
                   reads=[self.rX], writes=[self.r_xin_sb])
            for s in range(ns):
                pg.op('act', lambda e, s=s: e.activation(self.junk[:, :], xin[:, s, :], AF.Square,
                                                         accum_out=stat[:, s:s + 1]),
                      reads=[self.r_xin_sb], writes=[self.r_junk, self.r_stat])
            self.rstd(stat[:, 8:8 + ns], stat[:, 4:4 + ns], stat[:, 0:ns], 1.0 / D, self.r_stat)
            for s in range(ns):
                pg.op('dve', lambda e, s=s: e.scalar_tensor_tensor(
                    xin[:, s, :], xin[:, s, :], stat[:, 8 + s:9 + s], grow, ALU.mult, ALU.mult),
                    reads=[self.r_xin_sb, self.r_stat, r_grow], writes=[self.r_xin_sb])
            pg.dma('sp', self.out[t0 - NCTX:t0 - NCTX + nt, :].rearrange("(s p) d -> p s d", p=P),
                   xin[:, 0:ns, :], reads=[self.r_xin_sb], writes=[self.r_out])

    def mixer(self, li, kind, j):
        if kind < 0:
            return
        if kind in (1, 2):
            self.linattn(li, kind, j)
            return
        if kind == 0:
            self.gdn(li, j)
            return
        if kind == 3:
            self.hyena(li, j)
            return
        raise NotImplementedError

    def mixer_inputs(self):
        nc, pg, T = self.nc, self.pg, self.T
        kinds = [k for k, _ in self.layers]
        if 1 in kinds or 2 in kinds or 0 in kinds:
            self.QT, self.rQT = pg.dram('QT', [1024, T])
            self.KT, self.rKT = pg.dram('KT', [1024, T])
            self.Vd, self.rVd = pg.dram('Vd', [T, 2048])
            self.Gd, self.rGd = pg.dram('Gd', [T, 2048])
            self.Od, self.rOd = pg.dram('Od', [T, 2048])
            self.LOWT, self.rLOWT = pg.dram('LOWT', [32, T])
        if 0 in kinds:
            self.CQ, self.rCQ = pg.dram('CQ', [3072, T])
            self.BG, self.rBG = pg.dram('BG', [T, 32])
            self.gdn_w_in, _ = self.inp('gdn_w_in', [1, D, 4128])
            self.gdn_conv_w, _ = self.inp('gdn_conv_w', [1, 5, 3072])
            self.gdn_a_log, _ = self.inp('gdn_a_log', [1, 16])
            self.gdn_dt_bias, _ = self.inp('gdn_dt_bias', [1, 16])
            self.gdn_norm_g, _ = self.inp('gdn_norm_g', [1, 128])
            self.gdn_w_out, _ = self.inp('gdn_w_out', [1, 1024, D])
        if 3 in kinds:
            if 0 not in kinds:
                self.CQ, self.rCQ = pg.dram('CQ', [3072, T])
            self.FT, self.rFT = pg.dram('FT', [2048, self.L])
            self.Z1, self.rZ1 = pg.dram('Z1', [1024, self.L])
            self.Z2, self.rZ2 = pg.dram('Z2', [1024, self.L])
            self.hy_w_in, _ = self.inp('hy_w_in', [1, D, 3072])
            self.hy_conv_w, _ = self.inp('hy_conv_w', [1, 3, 3072])
            self.hy_ff_w1, _ = self.inp('hy_ff_w1', [1, 33, 64])
            self.hy_ff_b1, _ = self.inp('hy_ff_b1', [1, 64])
            self.hy_ff_w2, _ = self.inp('hy_ff_w2', [1, 64, 64])
            self.hy_ff_b2, _ = self.inp('hy_ff_b2', [1, 64])
            self.hy_ff_w3, _ = self.inp('hy_ff_w3', [1, 64, 2048])
            self.hy_sin_freq, _ = self.inp('hy_sin_freq', [1, 64])
            self.hy_skip, _ = self.inp('hy_skip', [1, 2048])
            self.hy_w_out, _ = self.inp('hy_w_out', [1, D, D])
            self.hyt, _ = self.inp('hyt', [P, 10 * P])
            self.hyz, _ = self.inp('hyz', [33, self.L])
            self.hydist, _ = self.inp('hydist', [1, self.L])
            self.hydelta, _ = self.inp('hydelta', [P, 8])
        if 1 in kinds:
            self.ret_w_in, _ = self.inp('ret_w_in', [1, D, 6144])
            self.ret_norm_g, _ = self.inp('ret_norm_g', [1, 2048])
            self.ret_w_out, _ = self.inp('ret_w_out', [1, 2048, D])
            self.rot, _ = self.inp('rot', [2, P, T])
        if 2 in kinds:
            self.gla_w_in, _ = self.inp('gla_w_in', [1, D, 3104])
            self.gla_gate_w2, _ = self.inp('gla_gate_w2', [1, 2, 16, 512])
            self.gla_gate_b, _ = self.inp('gla_gate_b', [1, 2, 512])
            self.gla_norm_g, _ = self.inp('gla_norm_g', [1, 256])
            self.gla_w_out, _ = self.inp('gla_w_out', [1, 1024, D])

    def proj_feat(self, W, col0, M, nt, bank):
        pg = self.pg
        wt, rw = self.wring.next()
        pg.dma('sp', wt[:, :, 0:M], W[:, col0:col0 + M].rearrange("(k p) n -> p k n", p=P),
               reads=[self.rW], writes=[rw])
        pt, rp = self.psb(bank)
        for kc in range(8):
            pg.op('pe', lambda e, kc=kc: e.matmul(pt[0:M, 0:nt], wt[:, kc, 0:M], self.hT[:, kc, 0:nt],
                                                  start=(kc == 0), stop=(kc == 7)),
                  reads=[rw, self.r_hT], writes=[rp])
        return pt, rp

    def proj_tok_store(self, W, col0, N, blk, dst, rdst, dcol0):
        pg = self.pg
        t0, nt, isctx = blk
        wt, rw = self.wring.next()
        pg.dma('sp', wt[:, :, 0:N], W[:, col0:col0 + N].rearrange("(k p) n -> p k n", p=P),
               reads=[self.rW], writes=[rw])
        for s in range(nt // P):
            pt, rp = self.psb(2 + s % 2)
            for kc in range(8):
                pg.op('pe', lambda e, kc=kc, s=s, pt=pt: e.matmul(
                    pt[:, 0:N], self.hT[:, kc, s * P:(s + 1) * P], wt[:, kc, 0:N],
                    start=(kc == 0), stop=(kc == 7)), reads=[rw, self.r_hT], writes=[rp])
            tt, rt = self.tmp512.next()
            pg.op('act', lambda e, pt=pt, tt=tt: e.copy(tt[:, 0:N], pt[:, 0:N]), reads=[rp], writes=[rt])
            pg.dma('sp', dst[t0 + s * P:t0 + (s + 1) * P, dcol0:dcol0 + N], tt[:, 0:N], reads=[rt], writes=[rdst])

    def linattn(self, li, kind, j):
        pg = self.pg
        if kind == 2:
            H, ndc, DV = 4, 1, 256
            W = self.gla_w_in[j]
            qoff, koff, voff, goff = 0, 512, 1024, 2048
            Wout, ng = self.gla_w_out[j], self.gla_norm_g
        else:
            H, ndc, DV = 4, 2, 512
            W = self.ret_w_in[j]
            qoff, koff, voff, goff = 0, 1024, 2048, 4096
            Wout, ng = self.ret_w_out[j], self.ret_norm_g
        DK = ndc * P
        V = H * DV
        QK = H * DK
        qscale = float(DK) ** -0.5
        def m1(blk):
            t0, nt, isctx = blk
            self.load_norm(blk, 0)
            rotate = (kind == 1 and not isctx)
            if rotate:
                cs, r_cs = self.junk, self.r_junk
                pg.dma('sp', cs[:, 0:nt], self.rot[0, :, t0:t0 + nt], writes=[r_cs])
                pg.dma('sp', cs[:, 512:512 + nt], self.rot[1, :, t0:t0 + nt], writes=[r_cs])
            for (off, dst, rdst, scale) in ((qoff, self.QT, self.rQT, qscale), (koff, self.KT, self.rKT, 1.0)):
                if not rotate:
                    for c in range(QK // P):
                        pt, rp = self.proj_feat(W, off + c * P, P, nt, 2 + c % 2)
                        tt, rt = self.tmp512.next()
                        pg.op('act', lambda e, pt=pt, tt=tt, scale=scale: e.activation(
                            tt[:, 0:nt], pt[:, 0:nt], AF.Copy, scale=scale), reads=[rp], writes=[rt])
                        pg.dma('sp', dst[c * P:(c + 1) * P, t0:t0 + nt], tt[:, 0:nt], reads=[rt], writes=[rdst])
                else:
                    for h in range(H):
                        pa, rpa = self.proj_feat(W, off + (2 * h) * P, P, nt, 2)
                        pb, rpb = self.proj_feat(W, off + (2 * h + 1) * P, P, nt, 3)
                        a, ra = self.tmp512.next()
                        b, rb = self.tmp512.next()
                        o1, ro1 = self.tmp512.next()
                        pg.op('act', lambda e, pa=pa, a=a, scale=scale: e.activation(
                            a[:, 0:nt], pa[:, 0:nt], AF.Copy, scale=scale), reads=[rpa], writes=[ra])
                        pg.op('act', lambda e, pb=pb, b=b, scale=scale: e.activation(
                            b[:, 0:nt], pb[:, 0:nt], AF.Copy, scale=scale), reads=[rpb], writes=[rb])
                        cos, sin = cs[:, 0:nt], cs[:, 512:512 + nt]
                        t1, rt1 = self.hq.next()
                        t2, rt2 = self.hq.next()
                        pg.op('dve', lambda e, t1=t1, a=a: e.tensor_tensor(t1[:, 0:nt], a[:, 0:nt], cos, ALU.mult),
                              reads=[ra, r_cs], writes=[rt1])
                        pg.op('pool', lambda e, t2=t2, b=b: e.tensor_tensor(t2[:, 0:nt], b[:, 0:nt], sin, ALU.mult),
                              reads=[rb, r_cs], writes=[rt2])
                        pg.op('dve', lambda e, o1=o1, t1=t1, t2=t2: e.tensor_tensor(
                            o1[:, 0:nt], t1[:, 0:nt], t2[:, 0:nt], ALU.subtract), reads=[rt1, rt2], writes=[ro1])
                        pg.dma('sp', dst[(2 * h) * P:(2 * h + 1) * P, t0:t0 + nt], o1[:, 0:nt], reads=[ro1],
                               writes=[rdst])
                        t3, rt3 = self.hq.next()
                        t4, rt4 = self.hq.next()
                        pg.op('dve', lambda e, t3=t3, a=a: e.tensor_tensor(t3[:, 0:nt], a[:, 0:nt], sin, ALU.mult),
                              reads=[ra, r_cs], writes=[rt3])
                        pg.op('pool', lambda e, t4=t4, b=b: e.tensor_tensor(t4[:, 0:nt], b[:, 0:nt], cos, ALU.mult),
                              reads=[rb, r_cs], writes=[rt4])
                        pg.op('dve', lambda e, t3=t3, t4=t4: e.tensor_tensor(
                            t3[:, 0:nt], t3[:, 0:nt], t4[:, 0:nt], ALU.add), reads=[rt3, rt4], writes=[rt3])
                        pg.dma('sp', dst[(2 * h + 1) * P:(2 * h + 2) * P, t0:t0 + nt], t3[:, 0:nt], reads=[rt3],
                               writes=[rdst])
            for cg in range(V // 512):
                self.proj_tok_store(W, voff + cg * 512, 512, blk, self.Vd, self.rVd, cg * 512)
                self.proj_tok_store(W, goff + cg * 512, 512, blk, self.Gd, self.rGd, cg * 512)
            if kind == 2:
                pt, rp = self.proj_feat(W, 3072, 32, nt, 2)
                tt, rt = self.tmp512.next()
                pg.op('act', lambda e, pt=pt, tt=tt: e.copy(tt[0:32, 0:nt], pt[0:32, 0:nt]), reads=[rp], writes=[rt])
                pg.dma('sp', self.LOWT[0:32, t0:t0 + nt], tt[0:32, 0:nt], reads=[rt], writes=[self.rLOWT])
        for blk in self.blocks:
            m1(blk)
        self.scan_linattn(kind, j, H, ndc, DV)
        self.out_proj(li, H, DV, ng, kind == 1, Wout, per_head_g=(kind == 1))

    def gdn(self, li, j):
        pg = self.pg
        W = self.gdn_w_in[j]
        H = 8
        cst = self.cst
        cb, r_cb = self.hq.next()
        pg.dma('sp', cb[:, 0:16], self.gdn_dt_bias[j:j + 1, :].partition_broadcast(P), writes=[r_cb])
        pg.dma('sp', cb[:, 16:32], self.gdn_a_log[j:j + 1, :].partition_broadcast(P), writes=[r_cb])
        pg.op('act', lambda e: e.activation(cb[:, 16:32], cb[:, 16:32], AF.Exp), reads=[r_cb], writes=[r_cb])
        pg.op('dve', lambda e: e.tensor_scalar(cb[:, 16:32], cb[:, 16:32], -1.0, None, ALU.mult), reads=[r_cb],
              writes=[r_cb])

        def m1(blk):
            t0, nt, isctx = blk
            self.load_norm(blk, 0)
            for c in range(24):
                pt, rp = self.proj_feat(W, c * P, P, nt, 2 + c % 2)
                tt, rt = self.tmp512.next()
                pg.op('act', lambda e, tt=tt, pt=pt: e.copy(tt[:, 0:nt], pt[:, 0:nt]), reads=[rp], writes=[rt])
                pg.dma('sp', self.CQ[c * P:(c + 1) * P, t0:t0 + nt], tt[:, 0:nt], reads=[rt], writes=[self.rCQ])
            for cg in range(2):
                self.proj_tok_store(W, 3072 + cg * 512, 512, blk, self.Gd, self.rGd, cg * 512)
            wt, rw = self.wring.next()
            pg.dma('sp', wt[:, :, 0:32], W[:, 4096:4128].rearrange("(k p) n -> p k n", p=P), reads=[self.rW],
                   writes=[rw])
            for s in range(nt // P):
                pt, rp = self.psb(2 + s % 2)
                for kc in range(8):
                    pg.op('pe', lambda e, kc=kc, s=s, pt=pt: e.matmul(
                        pt[:, 0:32], self.hT[:, kc, s * P:(s + 1) * P], wt[:, kc, 0:32],
                        start=(kc == 0), stop=(kc == 7)), reads=[rw, self.r_hT], writes=[rp])
                tt, rt = self.tmp512.next()
                pg.op('act', lambda e, pt=pt, tt=tt: e.activation(tt[:, 0:16], pt[:, 0:16], AF.Sigmoid),
                      reads=[rp], writes=[rt])
                pg.op('dve', lambda e, pt=pt, tt=tt: e.tensor_tensor(tt[:, 16:32], pt[:, 16:32], cb[:, 0:16], ALU.add),
                      reads=[rp, r_cb], writes=[rt])
                pg.op('act', lambda e, tt=tt: e.activation(tt[:, 16:32], tt[:, 16:32], AF.Exp), reads=[rt], writes=[rt])
                pg.op('act', lambda e, tt=tt: e.activation(tt[:, 16:32], tt[:, 16:32], AF.Ln, bias=1.0), reads=[rt],
                      writes=[rt])
                pg.op('dve', lambda e, tt=tt: e.tensor_tensor(tt[:, 16:32], tt[:, 16:32], cb[:, 16:32], ALU.mult),
                      reads=[rt, r_cb], writes=[rt])
                pg.dma('sp', self.BG[t0 + s * P:t0 + (s + 1) * P, :], tt[:, 0:32], reads=[rt], writes=[self.rBG])
        for blk in self.blocks:
            m1(blk)

        pg.barrier()
        save = self.aoff
        self.aoff = 0
        L = self.L
        sq, r_sq = self.carve('csq', [512])
        cin, r_cin = self.carve('cin', [L + 4])
        acc, r_acc = self.carve('cacc', [L])
        cw, r_cw = self.carve('cw', [24, 5])
        for k in range(5):
            pg.dma('sp', cw[:, :, k], self.gdn_conv_w[j, k:k + 1, :].rearrange("o (c p) -> p (o c)", p=P),
                   writes=[r_cw], allow_slow_non_contiguous=True)

        skip = ''

        def conv_seg(rc, a0, n):
            pg.op('pool', lambda e: e.memset(cin[:, 0:2], 0.0), writes=[r_cin])
            pg.op('pool', lambda e: e.memset(cin[:, n + 2:n + 4], 0.0), writes=[r_cin])
            pg.dma('sp', cin[:, 2:2 + n], self.CQ[rc * P:(rc + 1) * P, a0:a0 + n], reads=[self.rCQ], writes=[r_cin])
            def piece(p0, pn):
                pg.op('dve', lambda e: e.tensor_scalar(acc[:, p0:p0 + pn], cin[:, p0:p0 + pn], cw[:, rc, 0:1], None,
                                                       ALU.mult), reads=[r_cin, r_cw], writes=[r_acc])
                for k in range(1, 5):
                    pg.op('dve', lambda e, k=k: e.scalar_tensor_tensor(
                        acc[:, p0:p0 + pn], cin[:, p0 + k:p0 + k + pn], cw[:, rc, k:k + 1], acc[:, p0:p0 + pn],
                        ALU.mult, ALU.add), reads=[r_cin, r_cw, r_acc], writes=[r_acc])
                pg.op('act', lambda e: e.activation(acc[:, p0:p0 + pn], acc[:, p0:p0 + pn], AF.Silu), reads=[r_acc],
                      writes=[r_acc])
            for p0 in range(0, n, 2048):
                piece(p0, min(2048, n - p0))

            def l2piece(p0, pn, scale):
                pg.op('pool', lambda e: e.tensor_tensor(sq[:, 0:pn], acc[:, p0:p0 + pn], acc[:, p0:p0 + pn],
                                                        ALU.mult), reads=[r_acc], writes=[r_sq])
                pt, rp = self.psb((p0 // 512) % 2)
                pg.op('pe', lambda e: e.matmul(pt[:, 0:pn], self.ones, sq[:, 0:pn], start=True, stop=True),
                      reads=[r_sq, self.r_cst], writes=[rp])
                pg.op('dve', lambda e: e.tensor_scalar(sq[:, 0:pn], pt[:, 0:pn], 1.0, EPS, ALU.mult, ALU.add),
                      reads=[rp], writes=[r_sq])
                pg.op('act', lambda e: e.activation(sq[:, 0:pn], sq[:, 0:pn], AF.Sqrt), reads=[r_sq], writes=[r_sq])
                pg.op('dve', lambda e: e.reciprocal(sq[:, 0:pn], sq[:, 0:pn]), reads=[r_sq], writes=[r_sq])
                pg.op('dve', lambda e: e.scalar_tensor_tensor(acc[:, p0:p0 + pn], acc[:, p0:p0 + pn], scale,
                                                              sq[:, 0:pn], ALU.mult, ALU.mult),
                      reads=[r_acc, r_sq], writes=[r_acc])
            if rc < 16 and 'l2' not in skip:
                for p0 in range(0, n, 512):
                    l2piece(p0, min(512, n - p0), (128.0 ** -0.5) if rc < 8 else 1.0)
            pg.dma('sp', self.CQ[rc * P:(rc + 1) * P, a0:a0 + n], acc[:, 0:n], reads=[r_acc], writes=[self.rCQ])
        for rc in range(24):
            if 'conv' in skip:
                break
            conv_seg(rc, 0, NCTX)
            conv_seg(rc, NCTX, L)

        pg.barrier()
        self.aoff = 0
        cv = self.carve
        NQ = 8
        TB = 256
        S, rS = cv('gS', [H, P])
        fsets = RingL([dict(bgt=cv('bgt%d' % i, [2, 32]), fac=cv('fac%d' % i, [5, 2, 8]), gcs=cv('gcs%d' % i, [8]))
                       for i in range(2)])
        slots = []
        for q in range(NQ):
            slots.append(dict(
                q=cv('gq%d' % q, [TB]), k=cv('gk%d' % q, [TB]), v=cv('gv%d' % q, [TB]),
                of=cv('gof%d' % q, [2, P]), ob=cv('gob%d' % q, [2, P]),
                gle=cv('gle%d' % q, [P]), ggt=cv('ggt%d' % q, [P]), decs=cv('decs%d' % q, [P]),
                decT=cv('decT%d' % q, [P]),
                N=[cv('N%d_%d' % (i, q), [P]) for i in range(2)], NT=[cv('NT%d_%d' % (i, q), [P]) for i in range(2)],
                X=[cv('X%d_%d' % (i, q), [256]) for i in range(2)],
                khat=cv('khat%d' % q, [P]), wT=cv('wT%d' % q, [P]), attnT=cv('attnT%d' % q, [P]),
                vnew=cv('vnew%d' % q, [P]), tq=cv('tq%d' % q, [P]), bank=self.psb(q)))

        def blk_pre(d, blk, fs):
            t0, nt, isctx = blk
            ns = nt // P
            (bgt, r_bgt), (fac, r_fac), (gcs, r_gcs) = fs['bgt'], fs['fac'], fs['gcs']
            pg.dma('sp', bgt[:, 0:ns, :], self.BG[t0:t0 + nt, :].rearrange("(s p) c -> p s c", p=P),
                   reads=[self.rBG], writes=[r_bgt])
            for s in range(ns):
                gcol = bgt[:, s, 16 + 8 * d:24 + 8 * d]
                pgc, rpgc = self.psb(0)
                pg.op('pe', lambda e, gcol=gcol: e.matmul(pgc[:, 0:8], cst[:, 2 + d, :], gcol, start=True, stop=True),
                      reads=[r_bgt, self.r_cst], writes=[rpgc])
                pg.op('pe', lambda e, gcol=gcol: e.matmul(pgc[:, 8:16], self.ones, gcol, start=True, stop=True),
                      reads=[r_bgt, self.r_cst], writes=[rpgc])
                pg.op('act', lambda e, s=s: e.activation(fac[:, 0, s, :], pgc[:, 0:8], AF.Exp), reads=[rpgc],
                      writes=[r_fac])
                pg.op('act', lambda e, s=s: e.activation(fac[:, 2, s, :], pgc[:, 8:16], AF.Exp), reads=[rpgc],
                      writes=[r_fac])
                pg.op('act', lambda e: e.copy(gcs[:, :], pgc[:, 0:8]), reads=[rpgc], writes=[r_gcs])
                pg.op('dve', lambda e, s=s: e.tensor_tensor(fac[:, 1, s, :], pgc[:, 8:16], gcs[:, :], ALU.subtract),
                      reads=[rpgc, r_gcs], writes=[r_fac])
                pg.op('act', lambda e, s=s: e.activation(fac[:, 1, s, :], fac[:, 1, s, :], AF.Exp), reads=[r_fac],
                      writes=[r_fac])
                pg.op('dve', lambda e, s=s: e.tensor_tensor(fac[:, 3, s, :], fac[:, 0, s, :],
                                                            bgt[:, s, 8 * d:8 * d + 8], ALU.mult),
                      reads=[r_fac, r_bgt], writes=[r_fac])
                pg.op('dve', lambda e, s=s: e.tensor_scalar(fac[:, 4, s, :], bgt[:, s, 8 * d:8 * d + 8], -1.0, None,
                                                            ALU.mult), reads=[r_bgt], writes=[r_fac])

        def chunk(sl, fs, d, h, s):
            (bgt, r_bgt), (fac, r_fac) = fs['bgt'], fs['fac']
            (qT, rq), (kT, rk), (vT, rv) = sl['q'], sl['k'], sl['v']
            (ofb, r_ofb), (ob, rob) = sl['of'], sl['ob']
            (gle, r_gle), (ggt, r_ggt), (decs, r_decs), (decT, r_decT) = sl['gle'], sl['ggt'], sl['decs'], sl['decT']
            Nb, NTb, Xb = sl['N'], sl['NT'], sl['X']
            (khat, r_khat), (wT, r_wT), (attnT, r_attnT) = sl['khat'], sl['wT'], sl['attnT']
            (vnew, r_vnew), (tq, r_tq) = sl['vnew'], sl['tq']
            bk, rbk = sl['bank']
            c0, c1, c2, c3 = bk[:, 0:P], bk[:, P:2 * P], bk[:, 2 * P:3 * P], bk[:, 3 * P:4 * P]
            tk = slice(s * P, (s + 1) * P)
            gcol = bgt[:, s, 16 + 8 * d + h:17 + 8 * d + h]
            bcol = bgt[:, s, 8 * d + h:8 * d + h + 1]
            egc = fac[:, 0, s, h:h + 1]
            ekd = fac[:, 1, s, h:h + 1]
            egl = fac[:, 2, s, h:h + 1]
            bek = fac[:, 3, s, h:h + 1]
            nbeta = fac[:, 4, s, h:h + 1]
            pg.op('dve', lambda e: e.tensor_scalar(gle[:, :], cst[:, 2 + d, :], gcol, None, ALU.mult),
                  reads=[r_bgt, self.r_cst], writes=[r_gle])
            pg.op('pool', lambda e: e.tensor_scalar(ggt[:, :], cst[:, 4 + d, :], gcol, None, ALU.mult),
                  reads=[r_bgt, self.r_cst], writes=[r_ggt])
            yield
            pg.op('pe', lambda e: e.matmul(c0, gle[:, :], cst[:, 4 + d, :], start=True, stop=True),
                  reads=[r_gle, self.r_cst], writes=[rbk])
            pg.op('pe', lambda e: e.matmul(c1, ggt[:, :], cst[:, 2 + d, :], start=True, stop=True),
                  reads=[r_ggt, self.r_cst], writes=[rbk])
            pg.op('pe', lambda e: e.matmul(c2, kT[:, tk], kT[:, tk], start=True, stop=True),
                  reads=[rk], writes=[rbk])
            yield
            pg.op('act', lambda e: e.activation(decs[:, :], c0, AF.Exp), reads=[rbk], writes=[r_decs])
            pg.op('act', lambda e: e.activation(decT[:, :], c1, AF.Exp), reads=[rbk], writes=[r_decT])
            yield
            pg.op('pool', lambda e: e.tensor_tensor(decs[:, :], decs[:, :], cst[:, 4 + d, :], ALU.mult),
                  reads=[r_decs, self.r_cst], writes=[r_decs])
            pg.op('pool', lambda e: e.tensor_tensor(decT[:, :], decT[:, :], cst[:, 2 + d, :], ALU.mult),
                  reads=[r_decT, self.r_cst], writes=[r_decT])
            yield
            (N0, rN0), (NT0, rNT0) = Nb[0], NTb[0]
            pg.op('dve', lambda e: e.scalar_tensor_tensor(N0[:, :], c2, nbeta, decs[:, :], ALU.mult, ALU.mult),
                  reads=[rbk, r_fac, r_decs], writes=[rN0])
            yield
            pg.op('pe', lambda e: e.transpose(c3, N0[:, :], self.ident), reads=[rN0, self.r_cst], writes=[rbk])
            pg.op('pe', lambda e: e.transpose(c0, kT[:, tk], self.ident), reads=[rk, self.r_cst], writes=[rbk])
            pg.op('pe', lambda e: e.transpose(c1, vT[:, tk], self.ident), reads=[rv, self.r_cst], writes=[rbk])
            yield
            (X, rX) = Xb[0]
            pg.op('act', lambda e: e.copy(NT0[:, :], c3), reads=[rbk], writes=[rNT0])
            pg.op('act', lambda e: e.activation(X[:, 0:P], c1, AF.Copy, scale=bcol), reads=[rbk, r_bgt], writes=[rX])
            pg.op('act', lambda e: e.activation(khat[:, :], c0, AF.Copy, scale=ekd), reads=[rbk, r_fac],
                  writes=[r_khat])
            pg.op('act', lambda e: e.activation(X[:, P:2 * P], c0, AF.Copy, scale=bek), reads=[rbk, r_fac],
                  writes=[rX])
            yield
            cur = 0
            for lvl in range(7):
                (Nc, rNc), (NTc, rNTc) = Nb[cur], NTb[cur]
                (Xc, rXc), (Xn, rXn) = Xb[lvl % 2], Xb[(lvl + 1) % 2]
                pg.op('pe', lambda e, NTc=NTc, Xc=Xc: e.matmul(bk[:, 0:256], NTc[:, :], Xc[:, :], start=True, stop=True),
                      reads=[rNTc, rXc], writes=[rbk])
                if lvl < 6:
                    pg.op('pe', lambda e, NTc=NTc, Nc=Nc: e.matmul(c2, NTc[:, :], Nc[:, :], start=True, stop=True),
                          reads=[rNTc, rNc], writes=[rbk])
                    pg.op('pe', lambda e, NTc=NTc, Nc=Nc: e.matmul(c3, Nc[:, :], NTc[:, :], start=True, stop=True),
                          reads=[rNTc, rNc], writes=[rbk])
                yield
                pg.op('dve', lambda e, Xc=Xc, Xn=Xn: e.tensor_tensor(Xn[:, :], Xc[:, :], bk[:, 0:256], ALU.add),
                      reads=[rXc, rbk], writes=[rXn])
                if lvl < 6:
                    (Nn, rNn), (NTn, rNTn) = Nb[1 - cur], NTb[1 - cur]
                    pg.op('act', lambda e, Nn=Nn: e.copy(Nn[:, :], c2), reads=[rbk], writes=[rNn])
                    pg.op('act', lambda e, NTn=NTn: e.copy(NTn[:, :], c3), reads=[rbk], writes=[rNTn])
                    cur = 1 - cur
                yield
            (Xf, rXf) = Xb[7 % 2]
            pg.op('pe', lambda e: e.transpose(c0, Xf[:, P:2 * P], self.ident), reads=[rXf, self.r_cst], writes=[rbk])
            pg.op('pe', lambda e: e.matmul(c1, kT[:, tk], qT[:, tk], start=True, stop=True), reads=[rk, rq],
                  writes=[rbk])
            yield
            pg.op('act', lambda e: e.copy(wT[:, :], c0), reads=[rbk], writes=[r_wT])
            pg.op('dve', lambda e: e.tensor_tensor(attnT[:, :], c1, decT[:, :], ALU.mult), reads=[rbk, r_decT],
                  writes=[r_attnT])
            yield
            pg.op('pe', lambda e: e.matmul(c2, wT[:, :], S[:, h, :], start=True, stop=True), reads=[r_wT, rS],
                  writes=[rbk])
            pg.op('pe', lambda e: e.matmul(c3, qT[:, tk], S[:, h, :], start=True, stop=True), reads=[rq, rS],
                  writes=[rbk])
            yield
            pg.op('dve', lambda e: e.tensor_tensor(vnew[:, :], Xf[:, 0:P], c2, ALU.subtract), reads=[rXf, rbk],
                  writes=[r_vnew])
            pg.op('act', lambda e: e.activation(tq[:, :], c3, AF.Copy, scale=egc), reads=[rbk, r_fac],
                  writes=[r_tq])
            if d == 1:
                pg.op('pool', lambda e: e.tensor_tensor(tq[:, :], tq[:, :], ofb[:, s, :], ALU.add),
                      reads=[r_tq, r_ofb], writes=[r_tq])
            yield
            pg.op('pe', lambda e: e.matmul(c0, attnT[:, :], vnew[:, :], start=True, stop=True),
                  reads=[r_attnT, r_vnew], writes=[rbk])
            pg.op('pe', lambda e: e.matmul(c1, khat[:, :], vnew[:, :], start=True, stop=True),
                  reads=[r_khat, r_vnew], writes=[rbk])
            yield
            pg.op('dve', lambda e: e.tensor_tensor(ob[:, s, :], tq[:, :], c0, ALU.add), reads=[r_tq, rbk],
                  writes=[rob])
            pg.op('dve', lambda e: e.scalar_tensor_tensor(S[:, h, :], S[:, h, :], egl, c1, ALU.mult, ALU.add),
                  reads=[rS, r_fac, rbk], writes=[rS])
            yield

        def block_jobs(d, blk):
            t0, nt, isctx = blk
            ns = nt // P
            st = {'fs': None}

            def job(h):
                def run(sl):
                    if st['fs'] is None:
                        st['fs'] = fsets.next()
                        blk_pre(d, blk, st['fs'])
                    fs = st['fs']
                    (qT, rq), (kT, rk), (vT, rv) = sl['q'], sl['k'], sl['v']
                    (ofb, r_ofb), (ob, rob) = sl['of'], sl['ob']
                    pg.dma('sp', qT[:, 0:nt], self.CQ[h * P:(h + 1) * P, t0:t0 + nt], reads=[self.rCQ], writes=[rq])
                    pg.dma('sp', kT[:, 0:nt], self.CQ[(8 + h) * P:(9 + h) * P, t0:t0 + nt], reads=[self.rCQ],
                           writes=[rk])
                    pg.dma('sp', vT[:, 0:nt], self.CQ[(16 + h) * P:(17 + h) * P, t0:t0 + nt], reads=[self.rCQ],
                           writes=[rv])
                    if d == 1:
                        pg.dma('sp', ofb[:, 0:ns, :],
                               self.Od[t0:t0 + nt, h * P:(h + 1) * P].rearrange("(s p) d -> p s d", p=P),
                               reads=[self.rOd], writes=[r_ofb])
                    yield
                    for s in (range(ns) if d == 0 else range(ns - 1, -1, -1)):
                        yield from chunk(sl, fs, d, h, s)
                    pg.dma('sp', self.Od[t0:t0 + nt, h * P:(h + 1) * P].rearrange("(s p) d -> p s d", p=P),
                           ob[:, 0:ns, :], reads=[rob], writes=[self.rOd])
                return run
            return [job(h) for h in range(H)]

        def pipeline(jobs):
            active = {}
            free = list(range(NQ))
            ji = 0
            while ji < len(jobs) or active:
                if ji < len(jobs) and free:
                    q = free.pop(0)
                    active[q] = jobs[ji](slots[q])
                    ji += 1
                for q in list(active.keys()):
                    try:
                        next(active[q])
                    except StopIteration:
                        del active[q]
                        free.append(q)

        sblocks = [(0, NCTX, 1)] + [(NCTX + TB * k, TB, 0) for k in range(L // TB)]
        for d in range(2):
            pg.op('pool', lambda e: e.memset(S[:, :, :], 0.0), writes=[rS])
            order = list(sblocks) if d == 0 else [sblocks[0]] + list(sblocks[1:][::-1])
            jobs = []
            for blk in order:
                if 'scan' in skip:
                    break
                jobs += block_jobs(d, blk)
            pipeline(jobs)
        pg.barrier()
        self.aoff = save
        self.out_proj(li, 8, 128, self.gdn_norm_g, False, self.gdn_w_out[j], per_head_g=False)

    def hyena(self, li, j):
        pg = self.pg
        L = self.L
        W = self.hy_w_in[j]
        PI = math.pi

        def m1(blk):
            t0, nt, isctx = blk
            self.load_norm(blk, 0)
            for c in range(24):
                pt, rp = self.proj_feat(W, c * P, P, nt, 2 + c % 2)
                tt, rt = self.tmp512.next()
                pg.op('act', lambda e, tt=tt, pt=pt: e.copy(tt[:, 0:nt], pt[:, 0:nt]), reads=[rp], writes=[rt])
                pg.dma('sp', self.CQ[c * P:(c + 1) * P, t0:t0 + nt], tt[:, 0:nt], reads=[rt], writes=[self.rCQ])
        for blk in self.blocks[1:]:
            m1(blk)
        pg.barrier()
        save = self.aoff
        self.aoff = 0
        cin, r_cin = self.carve('hcin', [L + 2])
        acc, r_acc = self.carve('hacc', [L])
        cw, r_cw = self.carve('hcw', [24, 3])
        for k in range(3):
            pg.dma('sp', cw[:, :, k], self.hy_conv_w[j, k:k + 1, :].rearrange("o (c p) -> p (o c)", p=P),
                   writes=[r_cw], allow_slow_non_contiguous=True)
        pg.op('pool', lambda e: e.memset(cin[:, 0:1], 0.0), writes=[r_cin])
        pg.op('pool', lambda e: e.memset(cin[:, L + 1:L + 2], 0.0), writes=[r_cin])

        def conv_row(rc):
            pg.dma('sp', cin[:, 1:1 + L], self.CQ[rc * P:(rc + 1) * P, NCTX:NCTX + L], reads=[self.rCQ],
                   writes=[r_cin])

            def piece(p0, pn):
                pg.op('dve', lambda e: e.tensor_scalar(acc[:, p0:p0 + pn], cin[:, p0:p0 + pn], cw[:, rc, 0:1], None,
                                                       ALU.mult), reads=[r_cin, r_cw], writes=[r_acc])
                for k in range(1, 3):
                    pg.op('dve', lambda e, k=k: e.scalar_tensor_tensor(
                        acc[:, p0:p0 + pn], cin[:, p0 + k:p0 + k + pn], cw[:, rc, k:k + 1], acc[:, p0:p0 + pn],
                        ALU.mult, ALU.add), reads=[r_cin, r_cw, r_acc], writes=[r_acc])
            for p0 in range(0, L, 2048):
                piece(p0, min(2048, L - p0))
            pg.dma('sp', self.CQ[rc * P:(rc + 1) * P, NCTX:NCTX + L], acc[:, 0:L], reads=[r_acc], writes=[self.rCQ])
        for rc in range(24):
            conv_row(rc)

        pg.barrier()
        self.aoff = 0
        hid1, r_hid1 = self.carve('hid1', [L])
        hid2, r_hid2 = self.carve('hid2', [L])
        frow, r_frow = self.carve('frow', [L])
        zt, r_zt = self.carve('zt', [512])
        drow, r_drow = self.carve('drow', [512])
        wnd, r_wnd = self.carve('wnd', [512])
        mw, r_mw = self.carve('mw', [2048 + 64 + 64 + 8 + 8 + 32])
        w3 = mw[:, 0:2048]
        w1 = mw[:, 2048:2112]
        w2 = mw[:, 2112:2176]
        pv = mw[:, 2176:2184]
        dl = mw[:, 2184:2192]
        asum = mw[:, 2192:2224]
        pg.dma('sp', w3[0:64, :], self.hy_ff_w3[j], writes=[r_mw])
        pg.dma('sp', w1[0:33, :], self.hy_ff_w1[j], writes=[r_mw])
        pg.dma('sp', w2[0:64, :], self.hy_ff_w2[j], writes=[r_mw])
        for n, src in enumerate((self.hy_ff_b1, self.hy_ff_b2, self.hy_sin_freq)):
            pg.dma('sp', pv[0:64, n:n + 1], src[j:j + 1, :].rearrange("o p -> p o"), writes=[r_mw],
                   allow_slow_non_contiguous=True)
        pg.dma('sp', dl[:, :], self.hydelta[:, :], writes=[r_mw])
        pg.op('dve', lambda e: e.tensor_scalar(dl[:, :], dl[:, :], -1.0, None, ALU.mult), reads=[r_mw], writes=[r_mw])

        def sin_layer(dst, r_dst, wmat, K, src_fn, bcol, p0):
            pt, rp = self.psb(0)
            rhs, r_rhs = src_fn(p0)
            pg.op('pe', lambda e: e.matmul(pt[0:64, 0:512], wmat[0:K, :], rhs, start=True, stop=True),
                  reads=[r_mw, r_rhs], writes=[rp])
            tt, rt = self.hq.next()
            pg.op('dve', lambda e: e.tensor_scalar(tt[0:64, :], pt[0:64, 0:512], pv[0:64, bcol:bcol + 1],
                                                   pv[0:64, 2:3], ALU.add, ALU.mult), reads=[rp, r_mw], writes=[rt])
            m1, rm1 = self.hq.next()
            m2, rm2 = self.hq.next()
            pg.op('dve', lambda e: e.tensor_scalar(m1[0:64, :], tt[0:64, :], PI, -2.0 * PI, ALU.is_gt, ALU.mult),
                  reads=[rt], writes=[rm1])
            pg.op('dve', lambda e: e.tensor_scalar(m2[0:64, :], tt[0:64, :], -PI, 2.0 * PI, ALU.is_lt, ALU.mult),
                  reads=[rt], writes=[rm2])
            pg.op('dve', lambda e: e.tensor_tensor(tt[0:64, :], tt[0:64, :], m1[0:64, :], ALU.add),
                  reads=[rt, rm1], writes=[rt])
            pg.op('dve', lambda e: e.tensor_tensor(tt[0:64, :], tt[0:64, :], m2[0:64, :], ALU.add),
                  reads=[rt, rm2], writes=[rt])
            pg.op('act', lambda e: e.activation(dst[0:64, p0:p0 + 512], tt[0:64, :], AF.Sin),
                  reads=[rt], writes=[r_dst])

        def zsrc(p0):
            pg.dma('sp', zt[0:33, :], self.hyz[:, p0:p0 + 512], writes=[r_zt])
            return zt[0:33, :], r_zt

        def h1src(p0):
            return hid1[0:64, p0:p0 + 512], r_hid1
        for p0 in range(0, L, 512):
            sin_layer(hid1, r_hid1, w1, 33, zsrc, 0, p0)
        for p0 in range(0, L, 512):
            sin_layer(hid2, r_hid2, w2, 64, h1src, 1, p0)

        def filt_chunk(o, cc):
            def piece(p0, n):
                pt, rp = self.psb(1 + n % 2)
                pg.op('pe', lambda e: e.matmul(pt[:, 0:512], w3[0:64, o * 1024 + cc * P:o * 1024 + (cc + 1) * P],
                                               hid2[0:64, p0:p0 + 512], start=True, stop=True),
                      reads=[r_mw, r_hid2], writes=[rp])
                pg.dma('sp', drow[:, :], self.hydist[0:1, p0:p0 + 512].partition_broadcast(P), writes=[r_drow])
                pg.op('act', lambda e: e.activation(wnd[:, :], drow[:, :], AF.Exp, scale=dl[:, cc:cc + 1]),
                      reads=[r_drow, r_mw], writes=[r_wnd])
                pg.op('dve', lambda e: e.tensor_tensor(frow[:, p0:p0 + 512], pt[:, 0:512], wnd[:, :], ALU.mult),
                      reads=[rp, r_wnd], writes=[r_frow])
                pg.op('act', lambda e: e.activation(wnd[:, :], frow[:, p0:p0 + 512], AF.Abs,
                                                    accum_out=asum[:, n:n + 1]),
                      reads=[r_frow], writes=[r_wnd, r_mw])
            for n, p0 in enumerate(range(0, L, 512)):
                piece(p0, n)
            np_ = L // 512
            pg.op('dve', lambda e: e.reduce_sum(asum[:, 24:25], asum[:, 0:np_], axis=AX.X), reads=[r_mw],
                  writes=[r_mw])
            pg.op('dve', lambda e: e.reciprocal(asum[:, 25:26], asum[:, 24:25]), reads=[r_mw], writes=[r_mw])
            for p0 in range(0, L, 2048):
                pg.op('act', lambda e, p0=p0: e.activation(frow[:, p0:p0 + 2048], frow[:, p0:p0 + 2048], AF.Copy,
                                                           scale=asum[:, 25:26]), reads=[r_frow, r_mw],
                      writes=[r_frow])
            r0 = o * 1024 + cc * P
            pg.dma('sp', self.FT[r0:r0 + P, :], frow[:, 0:L], reads=[r_frow], writes=[self.rFT])
        for o in range(2):
            for cc in range(8):
                filt_chunk(o, cc)

        pg.barrier()
        self.aoff = 0
        tb, r_tb = self.carve('tb', [10, P])
        pg.dma('sp', tb[:, :, :], self.hyt.rearrange("p (a b) -> p a b", a=10), writes=[r_tb])
        skb, r_skb = self.carve('skb', [2048])
        pg.dma('sp', skb[:, :], self.hy_skip[j:j + 1, :].partition_broadcast(P), writes=[r_skb])
        CB = 8
        xin_r = RingL([self.carve('hx%d' % i, [CB, P]) for i in range(3)])
        gin_r = RingL([self.carve('hg%d' % i, [CB, P]) for i in range(3)])
        fin_r = RingL([self.carve('hf%d' % i, [CB, P]) for i in range(3)])
        zo_r = RingL([self.carve('hz%d' % i, [CB, P]) for i in range(3)])
        NQ = 8
        slots = []
        for q in range(NQ):
            slots.append(dict(
                T1=self.carve('T1_%d' % q, [256]), T2=self.carve('T2_%d' % q, [256]),
                B1=self.carve('B1_%d' % q, [256]), B2=self.carve('B2_%d' % q, [256]),
                GR=self.carve('GR_%d' % q, [256]), GI=self.carve('GI_%d' % q, [256]),
                Zb=self.carve('Zb_%d' % q, [256]), Eb=self.carve('Eb_%d' % q, [256]),
                bank=self.psb(q)))
        CmS = tb[0:64, 0:2, :].rearrange("p a b -> p (a b)")
        Cm = tb[:, 0, :]
        Sm = tb[:, 3, :]
        CS = tb[:, 2:4, :].rearrange("p a b -> p (a b)")
        mSC = tb[:, 1:3, :].rearrange("p a b -> p (a b)")
        CN = tb[:, 4, 32:96]
        mSN = tb[:, 5, 32:96]
        TCC = tb[:, 6:8, :].rearrange("p a b -> p (a b)")
        TSS = tb[:, 8:10, :].rearrange("p a b -> p (a b)")

        def fwd(sl, src, r_src):
            (T1, r_T1), (T2, r_T2), (B1, rB1), (B2, rB2) = sl['T1'], sl['T2'], sl['B1'], sl['B2']
            pbk, rpa = sl['bank']
            rpx = rpa
            pa, px = pbk[:, 0:256], pbk[:, 256:512]
            pg.op('pe', lambda e: e.matmul(pa[:, 0:256], src, CmS, start=True, stop=True),
                  reads=[r_src, r_tb], writes=[rpa])
            yield
            pg.op('dve', lambda e: e.tensor_tensor(T1[:, :], pa[:, 0:256], TCC, ALU.mult), reads=[rpa, r_tb],
                  writes=[r_T1])
            pg.op('dve', lambda e: e.tensor_tensor(T2[:, :], pa[:, 0:256], TSS, ALU.mult), reads=[rpa, r_tb],
                  writes=[r_T2])
            yield
            pg.op('pool', lambda e: e.tensor_tensor(B1[:, 0:P], T1[:, 0:P], T2[:, P:2 * P], ALU.add),
                  reads=[r_T1, r_T2], writes=[rB1])
            pg.op('pool', lambda e: e.tensor_tensor(B1[:, P:2 * P], T1[:, P:2 * P], T2[:, 0:P], ALU.subtract),
                  reads=[r_T1, r_T2], writes=[rB1])
            yield
            pg.op('act', lambda e: e.copy(B2[:, 0:P], B1[:, P:2 * P]), reads=[rB1], writes=[rB2])
            pg.op('act', lambda e: e.activation(B2[:, P:2 * P], B1[:, 0:P], AF.Copy, scale=-1.0), reads=[rB1],
                  writes=[rB2])
            yield
            pg.op('pe', lambda e: e.matmul(px[:, 0:256], Cm, B1[:, :], start=True, stop=False),
                  reads=[rB1, r_tb], writes=[rpx])
            pg.op('pe', lambda e: e.matmul(px[:, 0:256], Sm, B2[:, :], start=False, stop=True),
                  reads=[rB2, r_tb], writes=[rpx])
            yield

        def one(sl, o, c, k, xt, rx, gt, rg, ft, rf, zt_, rz):
            (T1, r_T1), (T2, r_T2) = sl['T1'], sl['T2']
            (GR, r_GR), (GI, r_GI), (Zb, r_Zb), (Eb, r_Eb) = sl['GR'], sl['GI'], sl['Zb'], sl['Eb']
            pbk, rpd = sl['bank']
            rpx = rpd
            pd, px = pbk[:, 0:256], pbk[:, 256:512]
            py, rpy = px, rpx
            pgx, rpgx = px, rpx
            yield from fwd(sl, ft[0:64, k, :], rf)
            pg.op('act', lambda e: e.copy(GR[:, :].rearrange("p (a b) -> p a b", a=2),
                                          pgx[:, 0:P].unsqueeze(1).to_broadcast([P, 2, P])), reads=[rpgx],
                  writes=[r_GR])
            pg.op('act', lambda e: e.copy(GI[:, :].rearrange("p (a b) -> p a b", a=2),
                                          pgx[:, P:2 * P].unsqueeze(1).to_broadcast([P, 2, P])), reads=[rpgx],
                  writes=[r_GI])
            yield
            yield from fwd(sl, xt[0:64, k, :], rx)
            pg.op('dve', lambda e: e.tensor_tensor(T1[:, :], px[:, 0:256], GR[:, :], ALU.mult), reads=[rpx, r_GR],
                  writes=[r_T1])
            pg.op('dve', lambda e: e.tensor_tensor(T2[:, :], px[:, 0:256], GI[:, :], ALU.mult), reads=[rpx, r_GI],
                  writes=[r_T2])
            yield
            pg.op('pool', lambda e: e.tensor_tensor(Zb[:, 0:P], T1[:, 0:P], T2[:, P:2 * P], ALU.subtract),
                  reads=[r_T1, r_T2], writes=[r_Zb])
            pg.op('pool', lambda e: e.tensor_tensor(Zb[:, P:2 * P], T2[:, 0:P], T1[:, P:2 * P], ALU.add),
                  reads=[r_T1, r_T2], writes=[r_Zb])
            yield
            pg.op('pe', lambda e: e.matmul(pd[:, 0:256], Zb[:, 0:P], CS, start=True, stop=False),
                  reads=[r_Zb, r_tb], writes=[rpd])
            pg.op('pe', lambda e: e.matmul(pd[:, 0:256], Zb[:, P:2 * P], mSC, start=False, stop=True),
                  reads=[r_Zb, r_tb], writes=[rpd])
            yield
            pg.op('dve', lambda e: e.tensor_tensor(T1[:, :], pd[:, 0:256], TCC, ALU.mult), reads=[rpd, r_tb],
                  writes=[r_T1])
            pg.op('dve', lambda e: e.tensor_tensor(T2[:, :], pd[:, 0:256], TSS, ALU.mult), reads=[rpd, r_tb],
                  writes=[r_T2])
            yield
            pg.op('pool', lambda e: e.tensor_tensor(Eb[:, 0:P], T1[:, 0:P], T2[:, P:2 * P], ALU.subtract),
                  reads=[r_T1, r_T2], writes=[r_Eb])
            pg.op('pool', lambda e: e.tensor_tensor(Eb[:, P:2 * P], T2[:, 0:P], T1[:, P:2 * P], ALU.add),
                  reads=[r_T1, r_T2], writes=[r_Eb])
            yield
            pg.op('pe', lambda e: e.matmul(py[0:64, 0:P], CN, Eb[:, 0:P], start=True, stop=False),
                  reads=[r_Eb, r_tb], writes=[rpy])
            pg.op('pe', lambda e: e.matmul(py[0:64, 0:P], mSN, Eb[:, P:2 * P], start=False, stop=True),
                  reads=[r_Eb, r_tb], writes=[rpy])
            yield
            sk = skb[0:64, o * 1024 + c:o * 1024 + c + 1]
            pg.op('dve', lambda e: e.scalar_tensor_tensor(zt_[0:64, k, :], xt[0:64, k, :], sk, py[0:64, 0:P],
                                                          ALU.mult, ALU.add), reads=[rx, r_skb, rpy], writes=[rz])
            yield
            pg.op('pool', lambda e: e.tensor_tensor(zt_[0:64, k, :], zt_[0:64, k, :], gt[0:64, k, :], ALU.mult),
                  reads=[rz, rg], writes=[rz])
            yield

        def group_jobs(o, c0):
            st = {'loaded': False, 'done': 0}
            bufs = {}

            def load():
                xt, rx = xin_r.next()
                gt, rg = gin_r.next()
                ft, rf = fin_r.next()
                zt_, rz = zo_r.next()
                if o == 0:
                    src, rsrc = self.CQ[c0:c0 + CB, NCTX:NCTX + L], self.rCQ
                    dst, rdst = self.Z1[c0:c0 + CB, :], self.rZ1
                else:
                    src, rsrc = self.Z1[c0:c0 + CB, :], self.rZ1
                    dst, rdst = self.Z2[c0:c0 + CB, :], self.rZ2
                g0 = 1024 * (o + 1) + c0
                pg.dma('sp', xt[0:64, :, :], src.rearrange("c (a b) -> a c b", b=P), reads=[rsrc], writes=[rx])
                pg.dma('sp', gt[0:64, :, :], self.CQ[g0:g0 + CB, NCTX:NCTX + L].rearrange("c (a b) -> a c b", b=P),
                       reads=[self.rCQ], writes=[rg])
                pg.dma('sp', ft[0:64, :, :],
                       self.FT[o * 1024 + c0:o * 1024 + c0 + CB, :].rearrange("c (a b) -> a c b", b=P),
                       reads=[self.rFT], writes=[rf])
                bufs.update(xt=xt, rx=rx, gt=gt, rg=rg, ft=ft, rf=rf, zt_=zt_, rz=rz, dst=dst, rdst=rdst)

            def job(k):
                def run(sl):
                    if not st['loaded']:
                        st['loaded'] = True
                        load()
                    yield from one(sl, o, c0 + k, k, bufs['xt'], bufs['rx'], bufs['gt'], bufs['rg'], bufs['ft'],
                                   bufs['rf'], bufs['zt_'], bufs['rz'])
                    st['done'] += 1
                    if st['done'] == CB:
                        pg.dma('sp', bufs['dst'].rearrange("c (a b) -> a c b", b=P), bufs['zt_'][0:64, :, :],
                               reads=[bufs['rz']], writes=[bufs['rdst']])
                return run
            return [job(k) for k in range(CB)]

        def pipeline(jobs):
            active = {}
            free = list(range(NQ))
            ji = 0
            first = True
            while ji < len(jobs) or active:
                if ji < len(jobs) and free:
                    q = free.pop(0)
                    active[q] = jobs[ji](slots[q])
                    ji += 1
                for q in list(active.keys()):
                    try:
                        next(active[q])
                    except StopIteration:
                        del active[q]
                        free.append(q)
        for o in range(2):
            jobs = []
            for c0 in range(0, 1024, CB):
                jobs += group_jobs(o, c0)
            pipeline(jobs)
        pg.barrier()
        self.aoff = save

        Wout = self.hy_w_out[j]
        onT, r_onT = self.gT, self.r_gT

        def ob_(blk):
            t0, nt, isctx = blk
            ns = nt // P
            pg.dma('sp', self.xin[:, 0:ns, :], self.X[t0:t0 + nt, :].rearrange("(s p) d -> p s d", p=P),
                   reads=[self.rX], writes=[self.r_xin_sb])
            pg.dma('sp', onT[:, 0:8, 0:nt],
                   self.Z2[:, t0 - NCTX:t0 - NCTX + nt].rearrange("(k p) t -> p k t", p=P), reads=[self.rZ2],
                   writes=[r_onT])
            for dg in range(2):
                accs = [self.psb(4 + s) for s in range(ns)]
                wt, rw = self.wring.next()
                pg.dma('sp', wt[:, 0:8, :], Wout[:, dg * 512:(dg + 1) * 512].rearrange("(k p) n -> p k n", p=P),
                       reads=[self.rW], writes=[rw])
                for vc in range(8):
                    for s in range(ns):
                        pt, rp = accs[s]
                        pg.op('pe', lambda e, pt=pt, vc=vc, s=s, wt=wt: e.matmul(
                            pt[:, :], onT[:, vc, s * P:(s + 1) * P], wt[:, vc, :],
                            start=(vc == 0), stop=(vc == 7)), reads=[rw, r_onT], writes=[rp])
                for s in range(ns):
                    pt, rp = accs[s]
                    self.update_x_store(blk, s, dg, pt, rp, 0, last=(dg == 1 and s == ns - 1))
        for blk in self.blocks[1:]:
            ob_(blk)

    def scan_linattn(self, kind, j, H, ndc, DV):
        pg = self.pg
        DK = ndc * P
        L = self.L
        pg.barrier()
        save = self.aoff
        self.aoff = 0
        cv = self.carve
        cst = self.cst
        NQ = 4
        TB = 256
        S, rS = cv('S', [H * ndc, DV])
        lsets = RingL([dict(ldt=cv('ldt%d' % i, [2, 512])) for i in range(2)])
        ldc = [cv('ldc%d' % h, [DK]) for h in range(H)] if kind == 1 else None
        lowaug, r_low = cv('lowaug', [TB])
        w2aug, r_w2 = cv('w2aug', [2, 512])
        tmpe, r_tmpe = cv('tmpe', [512])
        slots = []
        for q in range(NQ):
            slots.append(dict(
                q=cv('lq%d' % q, [ndc, TB]), k=cv('lk%d' % q, [ndc, TB]), v=cv('lv%d' % q, [2, DV]),
                ob=cv('lob%d' % q, [2, DV]),
                cum=cv('cum%d' % q, [ndc, P]), Eq=cv('Eq%d' % q, [ndc, P]), Ek=cv('Ek%d' % q, [ndc, P]),
                Eh=cv('Eh%d' % q, [ndc, P]), ekh=cv('ekh%d' % q, [DK]), refs=cv('refs%d' % q, [2, ndc]),
                qt=cv('qt%d' % q, [ndc, P]), kt=cv('kt%d' % q, [ndc, P]), qh=cv('qh%d' % q, [ndc, P]),
                kh=cv('kh%d' % q, [DK]), scs=cv('scs%d' % q, [P]),
                A=self.psb(2 * q), B=self.psb(2 * q + 1)))
        if kind == 2:
            pg.op('pool', lambda e: e.memset(lowaug[0:32, :], 1.0), writes=[r_low])
            for d_ in range(2):
                pg.dma('sp', w2aug[0:16, d_, :], self.gla_gate_w2[j, d_, :, :], writes=[r_w2])
                pg.dma('sp', w2aug[16:17, d_, :], self.gla_gate_b[j, d_:d_ + 1, :], writes=[r_w2])
        lgam = [math.log1p(-2.0 ** (-5.0 - h)) for h in range(H)]

        def gla_ld(d, blk, ls):
            t0, nt, isctx = blk
            ldt, r_ldt = ls['ldt']
            pg.dma('sp', lowaug[0:16, 0:nt], self.LOWT[d * 16:(d + 1) * 16, t0:t0 + nt],
                   reads=[self.rLOWT], writes=[r_low])
            for s in range(nt // P):
                pt, rp = self.psb(0)
                pg.op('pe', lambda e, s=s, pt=pt: e.matmul(pt[:, :], lowaug[0:17, s * P:(s + 1) * P],
                                                           w2aug[0:17, d, :], start=True, stop=True),
                      reads=[r_low, r_w2], writes=[rp])
                pg.op('act', lambda e, pt=pt: e.activation(tmpe[:, :], pt[:, :], AF.Exp, scale=-1.0),
                      reads=[rp], writes=[r_tmpe])
                pg.op('act', lambda e: e.activation(tmpe[:, :], tmpe[:, :], AF.Ln, bias=1.0),
                      reads=[r_tmpe], writes=[r_tmpe])
                pg.op('dve', lambda e, s=s: e.tensor_scalar(ldt[:, s, :], tmpe[:, :], -1.0 / 16.0, None, ALU.mult),
                      reads=[r_tmpe], writes=[r_ldt])

        def chunk(sl, d, h, s, ld, r_ld):
            TRI = cst[:, 2 + d, :]
            STR = cst[:, 4 + d, :]
            MSK = cst[:, 2 + d, :]
            last = P - 1 if d == 0 else 0
            tk = slice(s * P, (s + 1) * P)
            (qT, rq), (kT, rk), (v, rv), (ob, rob) = sl['q'], sl['k'], sl['v'], sl['ob']
            (cum, r_cum), (Eq, r_Eq), (Ek, r_Ek), (Eh, r_Eh) = sl['cum'], sl['Eq'], sl['Ek'], sl['Eh']
            (ekh, r_ekh), (refs, r_refs) = sl['ekh'], sl['refs']
            (qt_, r_qt), (kt_, r_kt), (qh_, r_qh), (kh_, r_kh), (scs, r_scs) = (sl['qt'], sl['kt'], sl['qh'],
                                                                               sl['kh'], sl['scs'])
            (A, rA), (B, rB) = sl['A'], sl['B']
            for dc in range(ndc):
                pg.op('pe', lambda e, dc=dc: e.matmul(A[:, dc * P:(dc + 1) * P], ld[:, dc * P:(dc + 1) * P], TRI,
                                                      start=True, stop=True), reads=[r_ld, self.r_cst], writes=[rA])
            pg.op('pe', lambda e: e.matmul(B[:, 0:DK], STR, ld, start=True, stop=True),
                  reads=[r_ld, self.r_cst], writes=[rB])
            yield
            pg.op('act', lambda e: e.copy(cum[:, :, :].rearrange("p a b -> p (a b)"), A[:, 0:DK]), reads=[rA],
                  writes=[r_cum])
            pg.op('act', lambda e: e.activation(ekh[:, :], B[:, 0:DK], AF.Exp), reads=[rB], writes=[r_ekh])
            yield
            pg.op('dve', lambda e: e.tensor_copy(refs[:, 0, :], cum[:, :, 64]), reads=[r_cum], writes=[r_refs])
            pg.op('dve', lambda e: e.tensor_scalar(refs[:, 1, :], cum[:, :, 64], -1.0, None, ALU.mult),
                  reads=[r_cum], writes=[r_refs])
            pg.op('pe', lambda e: e.transpose(A[:, 0:P], kT[:, 0, tk], self.ident),
                  reads=[rk, self.r_cst], writes=[rA])
            for dc in range(1, ndc):
                pg.op('pe', lambda e, dc=dc: e.transpose(A[:, dc * P:(dc + 1) * P], kT[:, dc, tk], self.ident),
                      reads=[rk, self.r_cst], writes=[rA])
            yield
            for dc in range(ndc):
                pg.op('act', lambda e, dc=dc: e.activation(Eq[:, dc, :], cum[:, dc, :], AF.Exp,
                                                           bias=refs[:, 1, dc:dc + 1]),
                      reads=[r_cum, r_refs], writes=[r_Eq])
                pg.op('act', lambda e, dc=dc: e.activation(Ek[:, dc, :], cum[:, dc, :], AF.Exp,
                                                           bias=refs[:, 0, dc:dc + 1], scale=-1.0),
                      reads=[r_cum, r_refs], writes=[r_Ek])
            pg.op('act', lambda e: e.activation(Eh[:, :, :], cum[:, :, :], AF.Exp), reads=[r_cum], writes=[r_Eh])
            pg.op('dve', lambda e: e.tensor_tensor(kh_[:, :], A[:, 0:DK], ekh[:, :], ALU.mult),
                  reads=[rA, r_ekh], writes=[r_kh])
            yield
            pg.op('dve', lambda e: e.tensor_tensor(qt_[:, :, :], qT[:, :, tk], Eq[:, :, :], ALU.mult),
                  reads=[rq, r_Eq], writes=[r_qt])
            pg.op('pool', lambda e: e.tensor_tensor(kt_[:, :, :], kT[:, :, tk], Ek[:, :, :], ALU.mult),
                  reads=[rk, r_Ek], writes=[r_kt])
            pg.op('pool', lambda e: e.tensor_tensor(qh_[:, :, :], qT[:, :, tk], Eh[:, :, :], ALU.mult),
                  reads=[rq, r_Eh], writes=[r_qh])
            yield
            for dc in range(ndc):
                pg.op('pe', lambda e, dc=dc: e.matmul(B[:, 0:P], kt_[:, dc, :], qt_[:, dc, :],
                                                      start=(dc == 0), stop=(dc == ndc - 1)),
                      reads=[r_kt, r_qt], writes=[rB])
            yield
            pg.op('dve', lambda e: e.tensor_tensor(scs[:, :], B[:, 0:P], MSK, ALU.mult),
                  reads=[rB, self.r_cst], writes=[r_scs])
            yield
            for dc in range(ndc):
                pg.op('pe', lambda e, dc=dc: e.matmul(A[:, 0:DV], qh_[:, dc, :], S[:, h * ndc + dc, :],
                                                      start=(dc == 0), stop=False),
                      reads=[r_qh, rS], writes=[rA])
            pg.op('pe', lambda e: e.matmul(A[:, 0:DV], scs[:, :], v[:, s, :], start=False, stop=True),
                  reads=[r_scs, rv], writes=[rA])
            pg.op('pe', lambda e: e.matmul(B[:, 0:DV], kh_[:, 0:P], v[:, s, :], start=True, stop=True),
                  reads=[r_kh, rv], writes=[rB])
            yield
            if d == 0:
                pg.op('act', lambda e: e.copy(ob[:, s, :], A[:, 0:DV]), reads=[rA], writes=[rob])
            else:
                pg.op('dve', lambda e: e.tensor_tensor(ob[:, s, :], A[:, 0:DV], ob[:, s, :], ALU.add),
                      reads=[rA, rob], writes=[rob])
            pg.op('dve', lambda e: e.scalar_tensor_tensor(
                S[:, h * ndc, :], S[:, h * ndc, :], Eh[:, 0, last:last + 1], B[:, 0:DV],
                ALU.mult, ALU.add), reads=[rS, r_Eh, rB], writes=[rS])
            yield
            for dc in range(1, ndc):
                pg.op('pe', lambda e, dc=dc: e.matmul(A[:, 0:DV], kh_[:, dc * P:(dc + 1) * P], v[:, s, :],
                                                      start=True, stop=True), reads=[r_kh, rv], writes=[rA])
                yield
                pg.op('dve', lambda e, dc=dc: e.scalar_tensor_tensor(
                    S[:, h * ndc + dc, :], S[:, h * ndc + dc, :], Eh[:, dc, last:last + 1], A[:, 0:DV],
                    ALU.mult, ALU.add), reads=[rS, r_Eh, rA], writes=[rS])
                yield

        def block_jobs(d, blk):
            t0, nt, isctx = blk
            ns = nt // P
            st = {'ls': None}

            def job(h):
                def run(sl):
                    if kind == 2 and st['ls'] is None:
                        st['ls'] = lsets.next()
                        gla_ld(d, blk, st['ls'])
                    (qT, rq), (kT, rk), (v, rv), (ob, rob) = sl['q'], sl['k'], sl['v'], sl['ob']
                    for dc in range(ndc):
                        r0 = h * DK + dc * P
                        pg.dma('sp', qT[:, dc, 0:nt], self.QT[r0:r0 + P, t0:t0 + nt], reads=[self.rQT], writes=[rq])
                        pg.dma('sp', kT[:, dc, 0:nt], self.KT[r0:r0 + P, t0:t0 + nt], reads=[self.rKT], writes=[rk])
                    pg.dma('sp', v[:, 0:ns, :],
                           self.Vd[t0:t0 + nt, h * DV:(h + 1) * DV].rearrange("(s p) d -> p s d", p=P),
                           reads=[self.rVd], writes=[rv])
                    if d == 1:
                        pg.dma('sp', ob[:, 0:ns, :],
                               self.Od[t0:t0 + nt, h * DV:(h + 1) * DV].rearrange("(s p) d -> p s d", p=P),
                               reads=[self.rOd], writes=[rob])
                    yield
                    for s in (range(ns) if d == 0 else range(ns - 1, -1, -1)):
                        if kind == 1:
                            ld, r_ld = ldc[h]
                            ld = ld[:, :]
                        else:
                            ldt, r_ld = st['ls']['ldt']
                            ld = ldt[:, s, h * P:(h + 1) * P]
                        yield from chunk(sl, d, h, s, ld, r_ld)
                    pg.dma('sp', self.Od[t0:t0 + nt, h * DV:(h + 1) * DV].rearrange("(s p) d -> p s d", p=P),
                           ob[:, 0:ns, :], reads=[rob], writes=[self.rOd])
                return run
            return [job(h) for h in range(H)]

        def pipeline(jobs):
            active = {}
            free = list(range(NQ))
            ji = 0
            while ji < len(jobs) or active:
                if ji < len(jobs) and free:
                    q = free.pop(0)
                    active[q] = jobs[ji](slots[q])
                    ji += 1
                for q in list(active.keys()):
                    try:
                        next(active[q])
                    except StopIteration:
                        del active[q]
                        free.append(q)

        sblocks = [(0, NCTX, 1)] + [(NCTX + TB * k, TB, 0) for k in range(L // TB)]
        for d in range(2):
            pg.op('pool', lambda e: e.memset(S[:, :, :], 0.0), writes=[rS])
            if kind == 1:
                for h in range(H):
                    hd = h if d == 0 else H - 1 - h
                    lt, r_lt = ldc[h]
                    pg.op('pool', lambda e, lt=lt, hd=hd: e.memset(lt[:, :], lgam[hd]), writes=[r_lt])
            order = list(sblocks) if d == 0 else [sblocks[0]] + list(sblocks[1:][::-1])
            jobs = []
            for blk in order:
                jobs += block_jobs(d, blk)
            pipeline(jobs)
        pg.barrier()
        self.aoff = save

    def out_proj(self, li, H, DV, ng, center, Wout, per_head_g):
        pg = self.pg
        V = H * DV
        nvc = V // P
        gsrc, r_gsrc = self.tmp512.next()
        ndv = DV // P
        for h in range(H):
            src = ng[0:1, h * DV:(h + 1) * DV] if per_head_g else ng[0:1, 0:DV]
            pg.dma('sp', gsrc[:, h * ndv:(h + 1) * ndv], src.rearrange("o (k p) -> p (o k)", p=P),
                   writes=[r_gsrc], allow_slow_non_contiguous=True)
        gbc, r_gbc = self.gbc, self.r_gbc
        self.bcast_row(gbc, r_gbc, gsrc, r_gsrc, nvc)
        onT, r_onT = self.gT, self.r_gT
        og, r_og = self.hT, self.r_hT
        ogf = og[:, :, :].rearrange("p a b -> p (a b)")
        stat = self.stat
        def ob_(blk):
            t0, nt, isctx = blk
            ns = nt // P
            pg.dma('sp', self.xin[:, 0:ns, :], self.X[t0:t0 + nt, :].rearrange("(s p) d -> p s d", p=P),
                   reads=[self.rX], writes=[self.r_xin_sb])
            for s in range(ns):
                o = ogf[:, 0:V]
                g = ogf[:, 2048:2048 + V]
                pg.dma('sp', o, self.Od[t0 + s * P:t0 + (s + 1) * P, 0:V], reads=[self.rOd], writes=[r_og])
                pg.dma('sp', g, self.Gd[t0 + s * P:t0 + (s + 1) * P, 0:V], reads=[self.rGd], writes=[r_og])
                pg.op('act', lambda e, g=g: e.activation(g, g, AF.Silu), reads=[r_og], writes=[r_og])
                for h in range(H):
                    oh = ogf[:, h * DV:(h + 1) * DV]
                    if center:
                        pg.op('dve', lambda e, oh=oh, h=h: e.reduce_sum(stat[:, 24:25], oh, axis=AX.X),
                              reads=[r_og], writes=[self.r_stat])
                        pg.op('dve', lambda e: e.tensor_scalar(stat[:, 24:25], stat[:, 24:25], -1.0 / DV, None,
                                                               ALU.mult), reads=[self.r_stat], writes=[self.r_stat])
                        pg.op('dve', lambda e, oh=oh: e.tensor_scalar(oh, oh, stat[:, 24:25], None, ALU.add),
                              reads=[r_og, self.r_stat], writes=[r_og])
                    pg.op('act', lambda e, oh=oh, h=h: e.activation(self.junk[:, 0:DV], oh, AF.Square,
                                                                     accum_out=stat[:, h:h + 1]),
                          reads=[r_og], writes=[self.r_junk, self.r_stat])
                self.rstd(stat[:, 16:16 + H], stat[:, 8:8 + H], stat[:, 0:H], 1.0 / DV, self.r_stat)
                for h in range(H):
                    oh = ogf[:, h * DV:(h + 1) * DV]
                    pg.op('dve', lambda e, oh=oh, h=h: e.scalar_tensor_tensor(
                        oh, oh, stat[:, 16 + h:17 + h], gbc[:, h * DV:(h + 1) * DV], ALU.mult, ALU.mult),
                        reads=[r_og, self.r_stat, r_gbc], writes=[r_og])
                pg.op('pool', lambda e, o=o, g=g: e.tensor_tensor(o, o, g, ALU.mult), reads=[r_og], writes=[r_og])
                for c0 in range(0, nvc, 4):
                    pt, rp = self.psb((c0 // 4) % 2)
                    for c in range(c0, c0 + 4):
                        pg.op('pe', lambda e, c=c, c0=c0, pt=pt: e.transpose(
                            pt[:, (c - c0) * P:(c - c0 + 1) * P], ogf[:, c * P:(c + 1) * P], self.ident),
                            reads=[r_og, self.r_cst], writes=[rp])
                    pg.op('act', lambda e, c0=c0, pt=pt, s=s: e.copy(
                        onT[:, c0:c0 + 4, s * P:(s + 1) * P], pt[:, :].rearrange("p (a b) -> p a b", a=4)),
                        reads=[rp], writes=[r_onT])
            for dg in range(2):
                accs = [self.psb(4 + s) for s in range(ns)]
                for v0 in range(0, nvc, 8):
                    nf = min(8, nvc - v0)
                    wt, rw = self.wring.next()
                    pg.dma('sp', wt[:, 0:nf, :],
                           Wout[v0 * P:(v0 + nf) * P, dg * 512:(dg + 1) * 512].rearrange("(k p) n -> p k n", p=P),
                           reads=[self.rW], writes=[rw])
                    for ff in range(nf):
                        vc = v0 + ff
                        for s in range(ns):
                            pt, rp = accs[s]
                            pg.op('pe', lambda e, pt=pt, vc=vc, ff=ff, s=s, wt=wt: e.matmul(
                                pt[:, :], onT[:, vc, s * P:(s + 1) * P], wt[:, ff, :],
                                start=(vc == 0), stop=(vc == nvc - 1)), reads=[rw, r_onT], writes=[rp])
                for s in range(ns):
                    pt, rp = accs[s]
                    self.update_x_store(blk, s, dg, pt, rp, 0, last=(dg == 1 and s == ns - 1))
        for blk in self.blocks:
            ob_(blk)


def rot_tables(L):
    T = L + NCTX
    n = np.arange(L)
    row = (n // 64).astype(np.float64)
    col = (n % 64).astype(np.float64)
    nf = 64
    inv = 10000.0 ** (-np.arange(nf, dtype=np.float64) / nf)
    ang = np.concatenate([row[:, None] * inv, col[:, None] * inv], axis=-1)
    ang = np.concatenate([row[:, None].astype(np.float32) * inv.astype(np.float32),
                          col[:, None].astype(np.float32) * inv.astype(np.float32)], axis=-1).astype(np.float64)
    t = np.zeros((2, P, T), np.float32)
    t[0, :, :NCTX] = 1.0
    t[0, :, NCTX:] = np.cos(ang).T
    t[1, :, NCTX:] = np.sin(ang).T
    return t


def hyena_tables(L):
    N = 2 * L
    n = np.arange(P, dtype=np.float64)
    ang = 2.0 * np.pi * np.outer(n, n) / P
    C, S = np.cos(ang), np.sin(ang)
    tw = 2.0 * np.pi * np.outer(n, n) / N
    TC, TS = np.cos(tw), np.sin(tw)
    t = np.zeros((P, 10, P), np.float64)
    t[:, 0] = C
    t[:, 1] = -S
    t[:, 2] = C
    t[:, 3] = S
    t[:, 4] = C / N
    t[:, 5] = -S / N
    t[:, 6] = TC
    t[:, 7] = TC
    t[:, 8] = TS
    t[:, 9] = TS
    f32 = np.float32
    pos = np.arange(L, dtype=f32)
    tt = pos / f32(L - 1)
    bands = 16
    freqs = np.linspace(1e-4, bands - 1, bands, dtype=f32)
    a = (f32(2.0 * math.pi / L) * pos[:, None] * freqs[None, :]).astype(np.float64)
    z = np.concatenate([tt[:, None].astype(np.float64), np.cos(a), -np.sin(a)], axis=-1)
    dist = (np.abs(pos - L // 2) / f32(L // 2)).astype(f32)
    deltas = np.abs(np.linspace(math.log(1e-2) / 1.5, math.log(1e-2) / 0.3, D, dtype=f32))
    return dict(hyt=t.reshape(P, 10 * P).astype(f32), hyz=np.ascontiguousarray(z.T).astype(f32),
                hydist=dist.reshape(1, L), hydelta=np.ascontiguousarray(deltas.reshape(8, P).T).astype(f32))


def host_inputs(inp, L, layer_ids):
    f = np.float32
    li = list(layer_ids)
    d = {}
    d['consts'] = host_consts()
    d['ada_w'] = np.ascontiguousarray(inp['ada_w'][li], f)
    d['ada_b'] = np.ascontiguousarray(inp['ada_b'][li], f)
    d['norm1_g'] = np.ascontiguousarray(inp['norm1_g'][li], f)
    d['norm2_g'] = np.ascontiguousarray(inp['norm2_g'][li], f)
    d['ffn_w1'] = np.ascontiguousarray(inp['ffn_w1'][li], f)
    d['ffn_w3'] = np.ascontiguousarray(inp['ffn_w3'][li], f)
    d['ffn_w2'] = np.ascontiguousarray(inp['ffn_w2'][li], f)
    d['final_norm_g'] = np.ascontiguousarray(inp['final_norm_g'], f).reshape(1, D)
    d['c_ctx'] = np.ascontiguousarray(inp['c_ctx'], f).reshape(1, D)
    d['rot'] = rot_tables(L)
    for k in ('ret_w_in', 'ret_w_out', 'gla_w_in', 'gla_gate_w2', 'gla_gate_b', 'gla_norm_g', 'gla_w_out'):
        d[k] = np.ascontiguousarray(inp[k], f)
    d['ret_norm_g'] = np.ascontiguousarray(inp['ret_norm_g'], f).reshape(1, 2048)
    for k in ('gdn_w_in', 'gdn_conv_w', 'gdn_norm_g', 'gdn_w_out'):
        d[k] = np.ascontiguousarray(inp[k], f)
    for k in ('hy_w_in', 'hy_conv_w', 'hy_ff_w1', 'hy_ff_b1', 'hy_ff_w2', 'hy_ff_b2', 'hy_ff_w3', 'hy_sin_freq', 'hy_w_out'):
        d[k] = np.ascontiguousarray(inp[k], f)
    d['hy_skip'] = np.ascontiguousarray(inp['hy_skip'], f).reshape(1, 2048)
    d.update(hyena_tables(L))
    d['gdn_a_log'] = np.ascontiguousarray(inp['gdn_a_log'], f).reshape(1, 16)
    d['gdn_dt_bias'] = np.ascontiguousarray(inp['gdn_dt_bias'], f).reshape(1, 16)
    return d


def core_inputs(full, inp, b, L):
    d = dict(full)
    d['x'] = np.ascontiguousarray(inp['x'][b, :L], np.float32)
    d['ctx'] = np.ascontiguousarray(inp['ctx'][b], np.float32)
    d['c'] = np.ascontiguousarray(inp['c'][b:b + 1], np.float32)
    return d


def kernel(**inputs):
    inp = {k: np.asarray(v) for k, v in inputs.items()}
    L = inp['x'].shape[1]
    B = inp['x'].shape[0]
    layers = [(i % 4, i // 4) for i in range(inp['ada_w'].shape[0])]
    mk = MK(L, layers)
    nc = mk.build()
    full = host_inputs(inp, L, range(len(layers)))
    maps = []
    for b in range(B):
        d = core_inputs(full, inp, b, L)
        maps.append({k: v for k, v in d.items() if k in mk.inputs})
    res = run_bass_kernel_spmd(nc, maps, core_ids=list(range(B)))
    return np.stack([np.asarray(r['out'], np.float32) for r in res.results], axis=0)
```
